# Optimizing a Trainium2 kernel written in Bass

```python
import jax
import jax.numpy as jnp
from jax import lax
import numpy as np


D_MODEL = 1024
BATCH = 32
SEQ = 2048
DEPTH = 4
DEC_BATCH = 8
DEC_SEQ = 4096
PAST_LEN = 128

GRID_W = 64
HEAD_DIM = 64
D_FOURIER = D_MODEL // 4
N_FOURIER_GROUPS = D_FOURIER // HEAD_DIM
D_ATTN = D_MODEL // 2
N_ATTN_HEADS = D_ATTN // HEAD_DIM
D_CONV = D_MODEL // 4
CONV_WIDTH = 31
D_MIX = D_FOURIER + D_ATTN + D_CONV
N_OUT_GROUPS = D_MIX // HEAD_DIM
D_IN_PROJ = D_FOURIER + 3 * D_ATTN + 2 * D_CONV
D_FF = 4 * D_MODEL
WIN_ROWS_MAX = 8
WIN_COLS = 16
QCOL_BLOCK = 16
KCOL_SPAN = QCOL_BLOCK + WIN_COLS
RMS_EPS = 1e-6
LN_EPS = 1e-5

kernel_name = 'hybrid_fnet_natten_conformer_encoder'


def rms_norm(x, g):
    xf = x.astype(jnp.float32)
    y = xf * lax.rsqrt(jnp.mean(xf * xf, axis=-1, keepdims=True) + RMS_EPS)
    return (y * g.astype(jnp.float32)).astype(x.dtype)


def layer_norm(x, g, b):
    xf = x.astype(jnp.float32)
    mu = jnp.mean(xf, axis=-1, keepdims=True)
    xc = xf - mu
    var = jnp.mean(xc * xc, axis=-1, keepdims=True)
    y = xc * lax.rsqrt(var + LN_EPS) * g.astype(jnp.float32) + b.astype(jnp.float32)
    return y.astype(x.dtype)


def fourier_mix(a):
    B, S, _ = a.shape
    ag = a.reshape(B, S, N_FOURIER_GROUPS, HEAD_DIM).astype(jnp.float32)
    f = jnp.fft.fft2(ag, axes=(1, 3), norm='ortho').real
    return f.reshape(B, S, D_FOURIER).astype(a.dtype)


def neighbourhood_attention(q, k, v, rpb):
    B, S, H, hd = q.shape
    rows = S // GRID_W
    kh = min(WIN_ROWS_MAX, rows)
    n_cb = GRID_W // QCOL_BLOCK
    q = q.reshape(B, rows, GRID_W, H, hd) * (hd ** -0.5)
    k = k.reshape(B, rows, GRID_W, H, hd)
    v = v.reshape(B, rows, GRID_W, H, hd)
    row_start = jnp.clip(jnp.arange(rows) - kh // 2, 0, rows - kh)
    cols = jnp.arange(GRID_W)
    col_start = jnp.clip(cols - WIN_COLS // 2, 0, GRID_W - WIN_COLS)
    span_start = jnp.clip(col_start[::QCOL_BLOCK], 0, GRID_W - KCOL_SPAN)
    span_cols = span_start[:, None] + jnp.arange(KCOL_SPAN)
    q_cols = cols.reshape(n_cb, QCOL_BLOCK)
    q_cs = col_start.reshape(n_cb, QCOL_BLOCK)
    kc = span_cols[:, None, :]
    col_valid = (kc >= q_cs[..., None]) & (kc < q_cs[..., None] + WIN_COLS)
    dc = jnp.clip(kc - q_cols[..., None] + WIN_COLS - 1, 0, 2 * WIN_COLS - 2)
    rpb_f = rpb.astype(jnp.float32)

    def row_block(r):
        rs = row_start[r]
        q_r = lax.dynamic_index_in_dim(q, r, axis=1, keepdims=False).reshape(B, n_cb, QCOL_BLOCK, H, hd)
        k_r = lax.dynamic_slice_in_dim(k, rs, kh, axis=1)[:, :, span_cols]
        v_r = lax.dynamic_slice_in_dim(v, rs, kh, axis=1)[:, :, span_cols]
        s = jnp.einsum('bnqhd,binshd->bhnqis', q_r, k_r).astype(jnp.float32)
        dr = rs + jnp.arange(kh) - r + WIN_ROWS_MAX - 1
        bias = rpb_f[:, dr[None, None, :, None], dc[:, :, None, :]]
        s = jnp.where(col_valid[:, :, None, :], s + bias, -jnp.inf)
        p = jax.nn.softmax(s.reshape(B, H, n_cb, QCOL_BLOCK, kh * KCOL_SPAN), axis=-1)
        p = p.reshape(s.shape).astype(v.dtype)
        o = jnp.einsum('bhnqis,binshd->bnqhd', p, v_r)
        return o.reshape(B, GRID_W, H, hd)

    out = lax.map(row_block, jnp.arange(rows))
    return jnp.moveaxis(out, 0, 1).reshape(B, S, H * hd)


def conformer_conv(u, conv_w, conv_b, ln_g, ln_b):
    g = u[..., :D_CONV] * jax.nn.sigmoid(u[..., D_CONV:])
    y = lax.conv_general_dilated(
        g, conv_w[:, None, :].astype(g.dtype), window_strides=(1,),
        padding=[(CONV_WIDTH // 2, CONV_WIDTH // 2)],
        dimension_numbers=('NWC', 'WIO', 'NWC'), feature_group_count=D_CONV)
    y = layer_norm(y + conv_b, ln_g, ln_b)
    return jax.nn.silu(y)


def encoder_layer(x, norm1_g, w_in, rpb, conv_w, conv_b, conv_ln_g, conv_ln_b,
                  mix_norm_g, w_out, norm2_g, w_ff_in, w_ff_out):
    B, S, _ = x.shape
    h = rms_norm(x, norm1_g)
    p = h @ w_in
    a = p[..., :D_FOURIER]
    o0 = D_FOURIER
    q = p[..., o0:o0 + D_ATTN].reshape(B, S, N_ATTN_HEADS, HEAD_DIM)
    k = p[..., o0 + D_ATTN:o0 + 2 * D_ATTN].reshape(B, S, N_ATTN_HEADS, HEAD_DIM)
    v = p[..., o0 + 2 * D_ATTN:o0 + 3 * D_ATTN].reshape(B, S, N_ATTN_HEADS, HEAD_DIM)
    u = p[..., o0 + 3 * D_ATTN:]
    o_f = fourier_mix(a)
    o_a = neighbourhood_attention(q, k, v, rpb)
    o_c = conformer_conv(u, conv_w, conv_b, conv_ln_g, conv_ln_b)
    o = jnp.concatenate([o_f, o_a, o_c], axis=-1).reshape(B, S, N_OUT_GROUPS, HEAD_DIM)
    o = rms_norm(o, mix_norm_g.reshape(N_OUT_GROUPS, HEAD_DIM)).reshape(B, S, D_MIX)
    x = x + o @ w_out
    h = rms_norm(x, norm2_g)
    f = jnp.square(jax.nn.relu(h @ w_ff_in))
    return x + f @ w_ff_out


def trunk(x, norm1_g, w_in, rpb, conv_w, conv_b, conv_ln_g, conv_ln_b,
          mix_norm_g, w_out, norm2_g, w_ff_in, w_ff_out, final_norm_g):
    for l in range(DEPTH):
        x = encoder_layer(x, norm1_g[l], w_in[l], rpb[l], conv_w[l], conv_b[l],
                          conv_ln_g[l], conv_ln_b[l], mix_norm_g[l], w_out[l],
                          norm2_g[l], w_ff_in[l], w_ff_out[l])
    return rms_norm(x, final_norm_g)


def setup_inputs(seed: int = 0) -> dict:
    key = jax.random.key(seed)
    ks = jax.random.split(key, 16)
    f32 = jnp.float32
    def nrm(k, shape, scale):
        return jax.random.normal(k, shape, f32) * scale
    return {
        'x_prompt': nrm(ks[0], (BATCH, SEQ, D_MODEL), 1.0),
        'x_sample': nrm(ks[1], (DEC_BATCH, DEC_SEQ, D_MODEL), 1.0),
        'norm1_g': 1.0 + nrm(ks[2], (DEPTH, D_MODEL), 0.01),
        'w_in': nrm(ks[3], (DEPTH, D_MODEL, D_IN_PROJ), D_MODEL ** -0.5),
        'rpb': nrm(ks[4], (DEPTH, N_ATTN_HEADS, 2 * WIN_ROWS_MAX - 1, 2 * WIN_COLS - 1), 0.02),
        'conv_w': nrm(ks[5], (DEPTH, CONV_WIDTH, D_CONV), CONV_WIDTH ** -0.5),
        'conv_b': nrm(ks[6], (DEPTH, D_CONV), 0.01),
        'conv_ln_g': 1.0 + nrm(ks[7], (DEPTH, D_CONV), 0.01),
        'conv_ln_b': nrm(ks[8], (DEPTH, D_CONV), 0.01),
        'mix_norm_g': 1.0 + nrm(ks[9], (DEPTH, D_MIX), 0.01),
        'w_out': nrm(ks[10], (DEPTH, D_MIX, D_MODEL), D_MIX ** -0.5),
        'norm2_g': 1.0 + nrm(ks[11], (DEPTH, D_MODEL), 0.01),
        'w_ff_in': nrm(ks[12], (DEPTH, D_MODEL, D_FF), D_MODEL ** -0.5),
        'w_ff_out': nrm(ks[13], (DEPTH, D_FF, D_MODEL), D_FF ** -0.5),
        'final_norm_g': 1.0 + nrm(ks[14], (D_MODEL,), 0.01),
    }


def reference(x_prompt, x_sample, norm1_g, w_in, rpb, conv_w, conv_b, conv_ln_g, conv_ln_b,
              mix_norm_g, w_out, norm2_g, w_ff_in, w_ff_out, final_norm_g):
    y_prompt = trunk(x_prompt, norm1_g, w_in, rpb, conv_w, conv_b, conv_ln_g, conv_ln_b,
                     mix_norm_g, w_out, norm2_g, w_ff_in, w_ff_out, final_norm_g)
    y_sample = trunk(x_sample, norm1_g, w_in, rpb, conv_w, conv_b, conv_ln_g, conv_ln_b,
                     mix_norm_g, w_out, norm2_g, w_ff_in, w_ff_out, final_norm_g)
    return (y_prompt, y_sample)
```

```python
import numpy as np
import ml_dtypes
import concourse.bass as bass
import concourse.mybir as mybir
from concourse.bass_utils import run_bass_kernel_spmd

F32 = mybir.dt.float32
BF16 = mybir.dt.bfloat16
AF = mybir.ActivationFunctionType
ALU = mybir.AluOpType
AX = mybir.AxisListType

D = 1024
NKC = 8
DIN = 2304
DFF = 4096
NH = 8
HD = 64
CW = 31
RMS_EPS = 1e-6
LN_EPS = 1e-5
NEG = -30000.0
NMAT = 21
VW = 72
GP = 16


class Sem:
    def __init__(self, nc, name):
        self.h = nc.alloc_semaphore(name)
        self.v = 0


class Buf:
    __slots__ = ("name", "w", "r")

    def __init__(self, name=""):
        self.name = name
        self.w = {}
        self.r = {}


class Q:
    def __init__(self, name, sem):
        self.name = name
        self.sem = sem
        self.seen = {}
        self.prog = []
        self.pending = False


class Tracker:
    def __init__(self, nc):
        self.nc = nc
        self.q = {}
        for n in ("pe", "act", "dve", "pool", "sp"):
            self.q[n] = Q(n, Sem(nc, "q_" + n))
        self.bar = Sem(nc, "bar")
        self.dsems = []
        self.dsem_next = 0
        self.swsems = []
        self.swsem_next = 0
        self.ninst = 0

    def new_dsem(self):
        if self.dsem_next == len(self.dsems):
            self.dsems.append(Sem(self.nc, "d%d" % len(self.dsems)))
        s = self.dsems[self.dsem_next]
        self.dsem_next += 1
        return s

    def new_swsem(self):
        if self.swsem_next == len(self.swsems):
            self.swsems.append(Sem(self.nc, "w%d" % len(self.swsems)))
        s = self.swsems[self.swsem_next]
        self.swsem_next += 1
        return s

    def _waits(self, q, needs):
        for s, v in needs.items():
            if v > 0 and q.seen.get(s, 0) < v:
                q.prog.append(("wait", s, v))
                q.seen[s] = v

    def op(self, qn, fn, r=(), w=(), sig=True):
        q = self.q[qn]
        needs = {}
        is_pe = qn == "pe"
        for b in r:
            for s, v in b.w.items():
                if s is q.sem and is_pe:
                    continue
                if needs.get(s, 0) < v:
                    needs[s] = v
        for b in w:
            for s, v in b.w.items():
                if s is q.sem:
                    continue
                if needs.get(s, 0) < v:
                    needs[s] = v
            for s, v in b.r.items():
                if s is q.sem:
                    continue
                if needs.get(s, 0) < v:
                    needs[s] = v
        self._waits(q, needs)
        if sig:
            q.sem.v += 1
            tv = q.sem.v
            q.prog.append(("inst", fn, q.sem, 1))
            q.pending = False
        else:
            assert is_pe
            tv = q.sem.v + 1
            q.prog.append(("inst", fn, None, 0))
            q.pending = True
        s = q.sem
        for b in r:
            if b.r.get(s, 0) < tv:
                b.r[s] = tv
        for b in w:
            if b.w.get(s, 0) < tv:
                b.w[s] = tv
        self.ninst += 1

    def dma(self, qn, out, in_, r=(), w=(), dsem=None, **kw):
        q = self.q[qn]
        needs = {}
        for b in r:
            for s, v in b.w.items():
                if needs.get(s, 0) < v:
                    needs[s] = v
        for b in w:
            for s, v in b.w.items():
                if needs.get(s, 0) < v:
                    needs[s] = v
            for s, v in b.r.items():
                if needs.get(s, 0) < v:
                    needs[s] = v
        if needs.get(dsem, 0) < dsem.v:
            needs[dsem] = dsem.v
        self._waits(q, needs)
        dsem.v += 16
        tv = dsem.v

        def fn(e, out=out, in_=in_, kw=kw):
            return e.dma_start(out=out, in_=in_, **kw)
        q.prog.append(("inst", fn, dsem, 16))
        for b in r:
            if b.r.get(dsem, 0) < tv:
                b.r[dsem] = tv
        for b in w:
            if b.w.get(dsem, 0) < tv:
                b.w[dsem] = tv
        self.ninst += 1

    def barrier(self):
        sp = self.q["sp"]
        needs = {}
        for n, q in self.q.items():
            assert not q.pending, n
            if n != "sp":
                needs[q.sem] = q.sem.v
        for s in self.dsems + self.swsems:
            needs[s] = s.v
        self._waits(sp, needs)
        self.bar.v += 1
        sp.prog.append(("seminc", self.bar, 1))
        for n, q in self.q.items():
            if n != "sp":
                q.prog.append(("wait", self.bar, self.bar.v))
            for s, v in needs.items():
                q.seen[s] = max(q.seen.get(s, 0), v)
        self.dsem_next = 0
        self.swsem_next = 0

    def replay(self):
        nc = self.nc
        progs = self.q

        def run(e, prog):
            for it in prog:
                if it[0] == "wait":
                    e.wait_ge(it[1].h, it[2])
                elif it[0] == "inst":
                    ins = it[1](e)
                    if it[2] is not None:
                        ins.then_inc(it[2].h, it[3])
                else:
                    e.sem_inc(it[1].h, it[2])

        with nc.Block() as block:
            @block.sync
            def _(e):
                run(e, progs["sp"].prog)

            @block.tensor
            def _(e):
                run(e, progs["pe"].prog)

            @block.scalar
            def _(e):
                run(e, progs["act"].prog)

            @block.vector
            def _(e):
                run(e, progs["dve"].prog)

            @block.gpsimd
            def _(e):
                run(e, progs["pool"].prog)


class Alloc:
    def __init__(self, nc, base, limit):
        self.nc = nc
        self.base = base
        self.off = base
        self.limit = limit
        self.cnt = 0
        self.peak = base

    def reset(self):
        self.off = self.base

    def __call__(self, shape, dtype, name="t"):
        nbytes = int(np.prod(shape[1:])) * (4 if dtype == F32 else 2)
        nbytes = (nbytes + 63) // 64 * 64
        assert self.off + nbytes <= self.limit, ("SBUF overflow", name, self.off, nbytes, self.limit)
        self.cnt += 1
        h = self.nc.alloc_sbuf_tensor_at("%s_%d" % (name, self.cnt), list(shape), dtype, offset=self.off)
        self.off += nbytes
        self.peak = max(self.peak, self.off)
        return h.ap()


def _bias_index_tables():
    R = 16
    cls = [(0, [0, 1, 2, 3]), (1, [-1, 0, 1, 2]), (3, [-2, -1, 0, 1, 2]),
           (R // 2 - 2, [-2, -1, 0, 1]), (R // 2 - 1, [-3, -2, -1, 0])]
    DR = np.zeros((NMAT, 128, 128), np.int64)
    DC = np.zeros((NMAT, 128, 128), np.int64)
    VAL = np.zeros((NMAT, 128, 128), bool)
    kk = np.arange(128)
    krl, kc = kk // 64, kk % 64
    qrl, qc = kk // 64, kk % 64
    m = 0
    for (i, deltas) in cls:
        for dl in deltas:
            j = i + dl
            r = (2 * i + qrl)[None, :]
            kr = (2 * j + krl)[:, None]
            rs = np.clip(r - 4, 0, R - 8)
            vrow = (kr >= rs) & (kr < rs + 8)
            cs = np.clip(qc - 8, 0, 48)[None, :]
            vcol = (kc[:, None] >= cs) & (kc[:, None] < cs + 16)
            DR[m] = np.clip(kr - r + 7, 0, 14)
            DC[m] = np.clip(kc[:, None] - qc[None, :] + 15, 0, 30)
            VAL[m] = vrow & vcol
            m += 1
    assert m == NMAT
    return DR, DC, VAL


def _pair_class(i, npairs):
    if i == 0:
        return [0, 1, 2, 3], 0
    if i == 1:
        return [0, 1, 2, 3], 4
    if i == npairs - 2:
        return [i - 2, i - 1, i, i + 1], 13
    if i == npairs - 1:
        return [i - 3, i - 2, i - 1, i], 17
    return [i - 2, i - 1, i, i + 1, i + 2], 8


def _dft_tables(S):
    nS, nK = S // 128, S // 512
    idx = (np.arange(S, dtype=np.int64)[:, None] * np.arange(S, dtype=np.int64)[None, :]) % S
    ang = 2.0 * np.pi * idx.astype(np.float64) / S
    out = []
    for tab in (np.cos(ang), np.sin(ang)):
        t = tab.reshape(nS, 128, nK, 512).transpose(2, 1, 0, 3).reshape(nK, 128, nS * 512)[:nK // 2]
        out.append(np.ascontiguousarray(t).astype(ml_dtypes.bfloat16))
    a64 = 2.0 * np.pi * ((np.arange(64)[:, None] * np.arange(64)[None, :]) % 64) / 64.0
    sc = 1.0 / np.sqrt(S * 64.0)
    bdc = np.zeros((128, 128))
    bds = np.zeros((128, 128))
    for g in range(2):
        bdc[g * 64:(g + 1) * 64, g * 64:(g + 1) * 64] = np.cos(a64) * sc
        bds[g * 64:(g + 1) * 64, g * 64:(g + 1) * 64] = -np.sin(a64) * sc
    bdc2 = np.concatenate([bdc, bdc], axis=1)
    bds2 = np.concatenate([bds, -bds], axis=1)
    return out[0], out[1], bdc2.astype(ml_dtypes.bfloat16), bds2.astype(ml_dtypes.bfloat16)


def build_program(seq_lens, depth):
    nc = bass.Bass("TRN2", target_bir_lowering=False)
    L = depth
    T = sum(seq_lens)
    seq_off = [sum(seq_lens[:i]) for i in range(len(seq_lens))]
    gpad_off = [sum(s + 2 * GP for s in seq_lens[:i]) for i in range(len(seq_lens))]
    TP = sum(s + 2 * GP for s in seq_lens)
    Sset = sorted(set(seq_lens))
    SMAX = max(seq_lens)
    NSMAX = SMAX // 128

    def din(name, shape, dt=F32):
        return nc.dram_tensor(name, list(shape), dt, kind="ExternalInput").ap()

    def dtmp(name, shape, dt):
        return nc.dram_tensor(name, list(shape), dt).ap()

    xin = din("xin", [T, D])
    norm1_gT = din("norm1_gT", [L, 128, NKC])
    w_in = din("w_in", [L, D, DIN])
    biasmat = din("biasmat", [L, 128, NH * NMAT * 128], BF16)
    conv_wT = din("conv_wT", [L, 256, CW])
    conv_b = din("conv_b", [L, 256])
    conv_ln_g = din("conv_ln_g", [L, 256])
    conv_ln_b = din("conv_ln_b", [L, 256])
    mix_gT = din("mix_gT", [L, 128, NKC])
    w_out = din("w_out", [L, D, D])
    norm2_gT = din("norm2_gT", [L, 128, NKC])
    w_ff_in = din("w_ff_in", [L, D, DFF])
    w_ff_out = din("w_ff_out", [L, DFF, D])
    final_norm_g = din("final_norm_g", [1, D])
    ident_d = din("ident", [128, 128], BF16)
    cos_d, sin_d, bdc_d, bds_d = {}, {}, {}, {}
    for S in Sset:
        cos_d[S] = din("cos%d" % S, [S // 1024, 128, (S // 128) * 512], BF16)
        sin_d[S] = din("sin%d" % S, [S // 1024, 128, (S // 128) * 512], BF16)
        bdc_d[S] = din("bdc%d" % S, [128, 256], BF16)
        bds_d[S] = din("bds%d" % S, [128, 256], BF16)
    jrev_d = din("jrev", [128, 128], BF16)
    alt_d = din("alt", [128, 2], BF16)
    y = nc.dram_tensor("y", [T, D], F32, kind="ExternalOutput").ap()

    xa_d = dtmp("xa_d", [T, D], F32)
    xb_d = dtmp("xb_d", [T, D], F32)
    qz_d = dtmp("qz_d", [T // 128, 128, NH * 128], BF16)
    kt_d = dtmp("kt_d", [4, 128, T], BF16)
    v_d = dtmp("v_d", [T, NH * VW], BF16)
    a_d = dtmp("a_d", [T, 256], BF16)
    gt_d = dtmp("gt_d", [2, 128, TP], BF16)
    of_d = dtmp("of_d", [T, 256], BF16)

    TR = Tracker(nc)
    PS = nc.alloc_psum_tensor("ps", [128, 4096], F32).ap()
    PSB = PS.bitcast(BF16)
    bankB = [Buf("bank%d" % i) for i in range(8)]

    def bank(i):
        return PS[:, i * 512:(i + 1) * 512]

    def bankb(i):
        return PSB[:, i * 1024:(i + 1) * 1024]

    SB_BASE = 16512
    SB_LIMIT = 229344
    CA = Alloc(nc, SB_BASE, SB_BASE + 8192)
    ident = CA([128, 128], BF16, "ident")
    identB = Buf("ident")
    mhalf = CA([128, 16], F32, "mhalf")
    zpad = CA([128, 2, GP], BF16, "zpad")
    epsc = CA([128, 2], F32, "epsc")
    constB = Buf("const")
    bdc, bds = {}, {}
    for S in Sset:
        bdc[S] = CA([128, 256], BF16, "bdc")
        bds[S] = CA([128, 256], BF16, "bds")
    jrev = CA([128, 128], BF16, "jrev")
    alt = CA([128, 2], BF16, "alt")
    PA = Alloc(nc, CA.off, SB_LIMIT)

    def mm(out, lhsT, rhs, start, stop, r, w, sig):
        TR.op("pe", lambda e, o=out, a=lhsT, b=rhs, s0=start, s1=stop: e.matmul(o, lhsT=a, rhs=b, start=s0, stop=s1),
              r=r, w=w, sig=sig)

    def tr(out, in_, r, w, sig):
        TR.op("pe", lambda e, o=out, a=in_: e.transpose(o, a, ident), r=list(r) + [identB], w=w, sig=sig)

    def act(out, in_, func, r, w, scale=1.0, accum=None):
        if accum is None:
            TR.op("act", lambda e, o=out, i=in_, f=func, s=scale: e.activation(out=o, in_=i, func=f, scale=s), r=r, w=w)
        else:
            TR.op("act", lambda e, o=out, i=in_, f=func, s=scale, a=accum: e.activation(out=o, in_=i, func=f, scale=s, accum_out=a),
                  r=r, w=w)

    def vcopy(qn, out, in_, r, w):
        TR.op(qn, lambda e, o=out, i=in_: e.tensor_copy(out=o, in_=i), r=r, w=w)

    def tt(qn, out, in0, in1, op, r, w):
        TR.op(qn, lambda e, o=out, a=in0, b=in1, p=op: e.tensor_tensor(out=o, in0=a, in1=b, op=p), r=r, w=w)

    def ts(qn, out, in0, s1, s2, op0, op1, r, w):
        if s2 is None:
            TR.op(qn, lambda e, o=out, a=in0, x=s1, p=op0: e.tensor_scalar(out=o, in0=a, scalar1=x, scalar2=None, op0=p), r=r, w=w)
        else:
            TR.op(qn, lambda e, o=out, a=in0, x=s1, y=s2, p=op0, p1=op1: e.tensor_scalar(out=o, in0=a, scalar1=x, scalar2=y, op0=p, op1=p1),
                  r=r, w=w)

    def stt(out, in0, scalar, in1, op0, op1, r, w):
        TR.op("dve", lambda e, o=out, a=in0, s=scalar, b=in1, p0=op0, p1=op1:
              e.scalar_tensor_tensor(out=o, in0=a, scalar=s, in1=b, op0=p0, op1=p1), r=r, w=w)

    def rsqrt_pool(out, in_, mul, add, r, w):
        n = out.shape[1]
        ts("pool", out, in_, mul, add, ALU.mult, ALU.add, r=r, w=w)
        tt("pool", out, out, mhalf[:, 0:n], ALU.pow, r=list(w) + [constB], w=w)

    def bcast_load(dst, src_row, buf, dsem):
        TR.dma("sp", dst, src_row.partition_broadcast(128), w=[buf], dsem=dsem)

    ebal = [0]

    def evq():
        ebal[0] += 1
        return "act" if ebal[0] % 2 else "dve"

    def evac(qn, out, in_, r, w, scale=None):
        if qn == "act":
            act(out, in_, AF.Copy, r, w, scale=1.0 if scale is None else scale)
        elif scale is None:
            vcopy(qn, out, in_, r, w)
        else:
            ts(qn, out, in_, scale, None, ALU.mult, None, r, w)

    cds = TR.new_dsem()
    TR.dma("sp", ident, ident_d, w=[identB], dsem=cds)
    for S in Sset:
        TR.dma("sp", bdc[S], bdc_d[S], w=[constB], dsem=cds)
        TR.dma("sp", bds[S], bds_d[S], w=[constB], dsem=cds)
    TR.dma("sp", jrev, jrev_d, w=[constB], dsem=cds)
    TR.dma("sp", alt, alt_d, w=[constB], dsem=cds)
    TR.op("pool", lambda e: e.memset(mhalf, -0.5), w=[constB])
    TR.op("pool", lambda e: e.memset(zpad, 0.0), w=[constB])
    TR.op("pool", lambda e: e.memset(epsc, RMS_EPS), w=[constB])
    for qi, S in enumerate(seq_lens):
        for side in range(2):
            o = gpad_off[qi] + (0 if side == 0 else GP + S)
            TR.dma("sp", gt_d[:, :, o:o + GP].rearrange("c p t -> p c t"), zpad, r=[constB], dsem=cds)
    TR.barrier()

    tiles = []
    for qi, S in enumerate(seq_lens):
        for t in range(S // 512):
            tiles.append((qi, t, seq_off[qi] + t * 512))
    NT = len(tiles)

    def load_fold_weight(W, WBs, src_rows, gt, gB, dsem, nsplit, width):
        for kc in range(len(WBs)):
            if nsplit > 1:
                TR.dma("pool", W[:, kc, :].rearrange("p (a b) -> p a b", a=nsplit),
                       src_rows(kc).rearrange("p (a b) -> p a b", a=nsplit), w=[WBs[kc]], dsem=dsem)
            else:
                TR.dma("pool", W[:, kc, :], src_rows(kc), w=[WBs[kc]], dsem=dsem)
            if gt is not None:
                ts("dve", W[:, kc, :], W[:, kc, :], gt[:, kc:kc + 1], None, ALU.mult, None, r=[WBs[kc], gB], w=[WBs[kc]])

    def load_fold_cols(W, src2d, blocks, gt, gB, dsem=None):
        bufs = {}
        for (c0, c1) in blocks:
            b = Buf()
            bufs[(c0, c1)] = b
            TR.dma("pool", W[:, :, c0:c1], src2d[:, c0:c1].rearrange("(k p) c -> p k c", p=128), w=[b], dsem=TR.new_swsem())
            tt("dve", W[:, :, c0:c1], W[:, :, c0:c1], gt.unsqueeze(2).to_broadcast([128, NKC, c1 - c0]), ALU.mult,
               r=[b, gB], w=[b])
        return bufs

    def phase_A(l):
        PA.reset()
        TR.dsem_next = 0
        src = xin if l == 0 else xa_d
        Win = PA([128, NKC, DIN], BF16, "Win")
        g1t = PA([128, NKC], F32, "g1t")
        g1B = Buf()
        TR.dma("sp", g1t, norm1_gT[l], w=[g1B], dsem=TR.new_dsem())
        order = [1, 2, 3, 4, 8, 7, 5, 6, 0]
        wcb = load_fold_cols(Win, w_in[l], [(b * 256, (b + 1) * 256) for b in order], g1t, g1B)
        WC = lambda col: wcb[((col // 256) * 256, (col // 256) * 256 + 256)]
        xs = [PA([128, D], F32, "x") for _ in range(8)]
        xB = [Buf() for _ in range(8)]
        xds = [TR.new_dsem() for _ in range(8)]
        hs = [PA([128, D], BF16, "h") for _ in range(8)]
        hB = [Buf() for _ in range(8)]
        ss = PA([128, 8], F32, "ss")
        rs = PA([128, 8], F32, "rs")
        ssB = [Buf(), Buf()]
        rsB = [Buf(), Buf()]
        hT = [PA([128, NKC, 512], BF16, "hT") for _ in range(2)]
        hTB = [[Buf() for _ in range(4)] for _ in range(2)]
        Qz = [PA([128, NH, 512], BF16, "Qz") for _ in range(2)]
        Kst = [PA([128, 4, 512], BF16, "Kst") for _ in range(2)]
        Gst = [PA([128, 2, 512], BF16, "Gst") for _ in range(2)]
        Vst = [PA([128, 4, NH, VW], BF16, "Vst") for _ in range(2)]
        Ast = [PA([128, 4, 256], BF16, "Ast") for _ in range(2)]
        QzB, KstB, GstB, VstB, AstB = ([Buf(), Buf()] for _ in range(5))
        stds = [[TR.new_dsem() for _ in range(5)] for _ in range(2)]
        et = [PA([128, 512], F32, "et") for _ in range(2)]
        etB = [Buf(), Buf()]
        for sl in range(2):
            TR.op("pool", lambda e, a=Qz[sl]: e.memset(a, 0.0), w=[QzB[sl]])
            TR.op("pool", lambda e, a=Vst[sl]: e.memset(a, 1.0), w=[VstB[sl]])
        bctr = [0]

        def nbank():
            b = bctr[0] % 6
            bctr[0] += 1
            return b

        def FE_el(k):
            qi, t, tok0 = tiles[k]
            sl = k % 2
            for s in range(4):
                i = sl * 4 + s
                TR.dma("sp", xs[i], src[tok0 + s * 128: tok0 + (s + 1) * 128, :], w=[xB[i]], dsem=xds[i])
            for s in range(4):
                i = sl * 4 + s
                act(hs[i], xs[i], AF.Square, r=[xB[i]], w=[hB[i], ssB[sl]], accum=ss[:, i:i + 1])
            rsqrt_pool(rs[:, sl * 4:sl * 4 + 4], ss[:, sl * 4:sl * 4 + 4], 1.0 / D, RMS_EPS, r=[ssB[sl]], w=[rsB[sl]])
            for s in range(4):
                i = sl * 4 + s
                ts("dve", hs[i], xs[i], rs[:, i:i + 1], None, ALU.mult, None, r=[xB[i], rsB[sl]], w=[hB[i]])

        def FE_pe(k):
            sl = k % 2
            for pr in range(4):
                bk = 6 + (pr % 2)
                for kk in range(2):
                    kc = pr * 2 + kk
                    for s in range(4):
                        i = sl * 4 + s
                        tr(bankb(bk)[:, kk * 512 + s * 128: kk * 512 + (s + 1) * 128], hs[i][:, kc * 128:(kc + 1) * 128],
                           r=[hB[i]], w=[bankB[bk]], sig=(kk == 1 and s == 3))
                evac(evq(), hT[sl][:, pr * 2:pr * 2 + 2, :].rearrange("p a b -> p (a b)"), bankb(bk), r=[bankB[bk]], w=[hTB[sl][pr]])

        def fm_group(sl, col):
            bk = nbank()
            for kc in range(NKC):
                mm(bank(bk), Win[:, kc, col:col + 128], hT[sl][:, kc, :], kc == 0, kc == NKC - 1,
                   r=[WC(col), hTB[sl][kc // 2]], w=[bankB[bk]], sig=(kc == NKC - 1))
            return bk

        def BE(k, hook):
            qi, t, tok0 = tiles[k]
            sl = k % 2
            for c in range(4):
                bk = fm_group(sl, 256 + c * 128)
                act(Qz[sl][0:64, 2 * c, :], bank(bk)[0:64, :], AF.Copy, r=[bankB[bk]], w=[QzB[sl]], scale=0.125)
                act(Qz[sl][64:128, 2 * c + 1, :], bank(bk)[64:128, :], AF.Copy, r=[bankB[bk]], w=[QzB[sl]], scale=0.125)
            for c in range(4):
                bk = fm_group(sl, 768 + c * 128)
                vcopy("dve", Kst[sl][:, c, :], bank(bk), r=[bankB[bk]], w=[KstB[sl]])
            if hook is not None:
                hook()
            for c in range(2):
                bk = fm_group(sl, 2048 + c * 128)
                act(et[c], bank(bk), AF.Exp, r=[bankB[bk]], w=[etB[c]], scale=-1.0)
                ts("pool", et[c], et[c], 1.0, 1.0, ALU.add, ALU.mult, r=[etB[c]], w=[etB[c]])
                TR.op("dve", lambda e, a=et[c]: e.reciprocal(out=a, in_=a), r=[etB[c]], w=[etB[c]])
                bk = fm_group(sl, 1792 + c * 128)
                tt("dve", Gst[sl][:, c, :], bank(bk), et[c], ALU.mult, r=[bankB[bk], etB[c]], w=[GstB[sl]])
            for s in range(4):
                bk = nbank()
                for kc in range(NKC):
                    mm(bank(bk), hT[sl][:, kc, s * 128:(s + 1) * 128], Win[:, kc, 1280:1792], kc == 0, kc == NKC - 1,
                       r=[WC(1280), WC(1536), hTB[sl][kc // 2]], w=[bankB[bk]], sig=(kc == NKC - 1))
                act(Vst[sl][:, s, :, 0:64], bank(bk).rearrange("p (h d) -> p h d", h=NH), AF.Copy, r=[bankB[bk]], w=[VstB[sl]])
                bk = nbank()
                for kc in range(NKC):
                    mm(bank(bk)[:, 0:256], hT[sl][:, kc, s * 128:(s + 1) * 128], Win[:, kc, 0:256], kc == 0, kc == NKC - 1,
                       r=[WC(0), hTB[sl][kc // 2]], w=[bankB[bk]], sig=(kc == NKC - 1))
                vcopy("dve", Ast[sl][:, s, :], bank(bk)[:, 0:256], r=[bankB[bk]], w=[AstB[sl]])
            ds = stds[sl]
            TR.dma("sp", qz_d[tok0 // 128: tok0 // 128 + 4, :, :].rearrange("s p (h t) -> p h s t", h=NH),
                   Qz[sl].rearrange("p h (s t) -> p h s t", s=4), r=[QzB[sl]], dsem=ds[0])
            TR.dma("sp", kt_d[:, :, tok0:tok0 + 512].rearrange("c p t -> p c t"), Kst[sl], r=[KstB[sl]], dsem=ds[1])
            go = gpad_off[qi] + GP + t * 512
            TR.dma("sp", gt_d[:, :, go:go + 512].rearrange("c p t -> p c t"), Gst[sl], r=[GstB[sl]], dsem=ds[2])
            TR.dma("sp", v_d[tok0:tok0 + 512, :].rearrange("(s p) d -> p s d", p=128), Vst[sl].rearrange("p s h d -> p s (h d)"),
                   r=[VstB[sl]], dsem=ds[3])
            TR.dma("sp", a_d[tok0:tok0 + 512, :].rearrange("(s p) d -> p s d", p=128), Ast[sl], r=[AstB[sl]], dsem=ds[4])

        FE_el(0)
        FE_pe(0)
        if NT > 1:
            FE_el(1)
        for k in range(NT):
            if k + 1 < NT:
                FE_pe(k + 1)
            BE(k, (lambda kk=k: FE_el(kk + 2)) if k + 2 < NT else None)
        TR.barrier()

    def phase_F(l):
        PA.reset()
        TR.dsem_next = 0
        NSEQ = len(seq_lens)
        Atm = [PA([128, seq_lens[qi] // 128, 256], BF16, "Atm") for qi in range(NSEQ)]
        AtmB = [Buf() for _ in range(NSEQ)]
        for qi in range(NSEQ):
            S = seq_lens[qi]
            TR.dma("sp", Atm[qi], a_d[seq_off[qi]:seq_off[qi] + S, :].rearrange("(t p) c -> p t c", p=128),
                   w=[AtmB[qi]], dsem=TR.new_dsem())
        cosb = [PA([128, NSMAX * 512], BF16, "cos") for _ in range(2)]
        sinb = [PA([128, NSMAX * 512], BF16, "sin") for _ in range(2)]
        cosB = [Buf(), Buf()]
        sinB = [Buf(), Buf()]
        cds_ = [TR.new_dsem(), TR.new_dsem()]
        sds_ = [TR.new_dsem(), TR.new_dsem()]
        PQ = [[PA([128, 512], BF16, "pq") for _ in range(4)] for _ in range(2)]
        PQB = [[Buf() for _ in range(4)] for _ in range(2)]
        sqf = [PA([128, 512], F32, "sqf") for _ in range(4)]
        ssf = [PA([128, 8], F32, "ssf") for _ in range(4)]
        rsf = [PA([128, 8], F32, "rsf") for _ in range(4)]
        om = [PA([128, 256], BF16, "om") for _ in range(4)]
        sqfB, ssfB, rsfB, omB = ([Buf() for _ in range(4)] for _ in range(4))
        ofd = [PA([128, 4, 256], BF16, "ofd") for _ in range(2)]
        ofm = [PA([128, 4, 256], BF16, "ofm") for _ in range(2)]
        ofdB = [Buf(), Buf()]
        ofmB = [Buf(), Buf()]
        odds = [TR.new_dsem(), TR.new_dsem()]
        omds = [TR.new_dsem(), TR.new_dsem()]
        omds2 = [TR.new_dsem(), TR.new_dsem()]
        PH = PA([128, 2], BF16, "PH")
        sq1 = PA([128, 256], F32, "sq1")
        ss1 = PA([128, 4], F32, "ss1")
        rs1 = PA([128, 4], F32, "rs1")
        oh = PA([128, 256], BF16, "oh")
        PHB, sq1B, ss1B, rs1B, ohB = (Buf() for _ in range(5))
        ohds = TR.new_dsem()

        steps = []
        jn = 0
        for S in sorted(set(seq_lens), key=lambda v: seq_lens.index(v)):
            qis = [qi for qi in range(NSEQ) if seq_lens[qi] == S]
            for kt in range(S // 1024):
                for ii, qi in enumerate(qis):
                    steps.append((S, kt, qi, ii == 0, jn))
                jn += 1

        def groups(n, part):
            S, kt, qi, first, j = steps[n]
            nS = S // 128
            sl = n % 2
            tl = j % 2
            a = Atm[qi]
            if first and part == 0:
                TR.dma("sp", cosb[tl][:, 0:nS * 512], cos_d[S][kt], w=[cosB[tl]], dsem=cds_[tl])
                TR.dma("sp", sinb[tl][:, 0:nS * 512], sin_d[S][kt], w=[sinB[tl]], dsem=sds_[tl])
            for c in range(2):
                if c != part:
                    continue
                for which in range(2):
                    bk = sl * 4 + c * 2 + which
                    tab = cosb[tl] if which == 0 else sinb[tl]
                    tB = cosB[tl] if which == 0 else sinB[tl]
                    for st in range(nS):
                        mm(bank(bk), a[:, st, c * 128:(c + 1) * 128], tab[:, st * 512:(st + 1) * 512], st == 0, st == nS - 1,
                           r=[AtmB[qi], tB], w=[bankB[bk]], sig=(st == nS - 1))

        def evacs(n):
            sl = n % 2
            for c in range(2):
                for which in range(2):
                    bk = sl * 4 + c * 2 + which
                    evac("act" if which == 0 else "dve", PQ[sl][c * 2 + which], bank(bk), r=[bankB[bk]], w=[PQB[sl][c * 2 + which]])

        def stage2_main(n):
            S, kt, qi, first, j = steps[n]
            sl = n % 2
            for sub in range(4):
                bk = sl * 4 + sub
                u = sub
                for c in range(2):
                    o = bank(bk)[:, c * 256:(c + 1) * 256]
                    mm(o, PQ[sl][c * 2][:, sub * 128:(sub + 1) * 128], bdc[S], True, False,
                       r=[PQB[sl][c * 2], constB], w=[bankB[bk]], sig=False)
                    mm(o, PQ[sl][c * 2 + 1][:, sub * 128:(sub + 1) * 128], bds[S], False, True,
                       r=[PQB[sl][c * 2 + 1], constB], w=[bankB[bk]], sig=(c == 1))
                O5 = bank(bk).rearrange("p (c m g d) -> p c m g d", c=2, m=2, g=2)
                rs5 = rsf[u].rearrange("p (c m g) -> p c m g", c=2, m=2)
                act(sqf[u], bank(bk), AF.Square, r=[bankB[bk]], w=[sqfB[u]])
                TR.op("dve", lambda e, o=ssf[u], i=sqf[u]: e.tensor_reduce(out=o, in_=i.rearrange("p (a d) -> p a d", a=8), axis=AX.X, op=ALU.add),
                      r=[sqfB[u]], w=[ssfB[u]])
                TR.op("act", lambda e, o=rsf[u], i=ssf[u]: e.activation(out=o, in_=i, func=AF.Sqrt, bias=epsc[:, 0:1], scale=1.0 / HD),
                      r=[ssfB[u], constB], w=[rsfB[u]])
                TR.op("dve", lambda e, o=rsf[u]: e.reciprocal(out=o, in_=o), r=[rsfB[u]], w=[rsfB[u]])
                tt("dve", ofd[sl][:, sub, :].rearrange("p (c g d) -> p c g d", c=2, g=2), O5[:, :, 0],
                   rs5[:, :, 0].unsqueeze(3).to_broadcast([128, 2, 2, HD]), ALU.mult, r=[bankB[bk], rsfB[u]], w=[ofdB[sl]])
                tt("dve", om[u].rearrange("p (c g d) -> p c g d", c=2, g=2), O5[:, :, 1],
                   rs5[:, :, 1].unsqueeze(3).to_broadcast([128, 2, 2, HD]), ALU.mult, r=[bankB[bk], rsfB[u]], w=[omB[u]])

        def stage2_tail(n):
            S, kt, qi, first, j = steps[n]
            nS = S // 128
            sl = n % 2
            k0 = kt * 512
            for sub in range(4):
                bk = sl * 4 + sub
                u = sub
                mm(bank(bk)[:, 0:256], jrev, om[u], True, True, r=[omB[u], constB], w=[bankB[bk]], sig=True)
                evac("act" if sub % 2 == 0 else "dve", ofm[sl][:, 3 - sub, :], bank(bk)[:, 0:256], r=[bankB[bk]], w=[ofmB[sl]])
            t0 = seq_off[qi]
            TR.dma("sp", of_d[t0 + k0:t0 + k0 + 512, :].rearrange("(s p) d -> p s d", p=128), ofd[sl], r=[ofdB[sl]], dsem=odds[sl])
            base = t0 + S - k0 - 511
            if kt > 0:
                TR.dma("sp", of_d[base:base + 512, :].rearrange("(s p) d -> p s d", p=128), ofm[sl], r=[ofmB[sl]], dsem=omds[sl])
            else:
                TR.dma("sp", of_d[base:base + 384, :].rearrange("(s p) d -> p s d", p=128), ofm[sl][:, 0:3, :], r=[ofmB[sl]], dsem=omds[sl])
                TR.dma("sp", of_d[base + 384:base + 511, :], ofm[sl][0:127, 3, :], r=[ofmB[sl]], dsem=omds2[sl])
                bk = sl * 4
                a = Atm[qi]
                for c in range(2):
                    for st in range(nS):
                        mm(bank(bk)[:, c:c + 1], a[:, st, c * 128:(c + 1) * 128], alt[:, 0:1], st == 0, st == nS - 1,
                           r=[AtmB[qi], constB], w=[bankB[bk]], sig=(c == 1 and st == nS - 1))
                evac("act", PH, bank(bk)[:, 0:2], r=[bankB[bk]], w=[PHB])
                for c in range(2):
                    mm(bank(bk)[0:1, 256 + c * 128:256 + (c + 1) * 128], PH[:, c:c + 1], bdc[S][:, 0:128], True, True,
                       r=[PHB, constB], w=[bankB[bk]], sig=(c == 1))
                O1 = bank(bk)[0:1, 256:512]
                act(sq1[0:1, :], O1, AF.Square, r=[bankB[bk]], w=[sq1B])
                TR.op("dve", lambda e: e.tensor_reduce(out=ss1[0:1, :], in_=sq1[0:1, :].rearrange("p (g d) -> p g d", g=4), axis=AX.X, op=ALU.add),
                      r=[sq1B], w=[ss1B])
                ts("pool", rs1[0:1, :], ss1[0:1, :], 1.0 / HD, RMS_EPS, ALU.mult, ALU.add, r=[ss1B], w=[rs1B])
                tt("pool", rs1[0:1, :], rs1[0:1, :], mhalf[0:1, 0:4], ALU.pow, r=[rs1B, constB], w=[rs1B])
                tt("dve", oh[0:1, :].rearrange("p (g d) -> p g d", g=4), O1.rearrange("p (g d) -> p g d", g=4),
                   rs1[0:1, :].unsqueeze(2).to_broadcast([1, 4, HD]), ALU.mult, r=[bankB[bk], rs1B], w=[ohB])
                TR.dma("sp", of_d[t0 + S // 2:t0 + S // 2 + 1, :], oh[0:1, :], r=[ohB], dsem=ohds)

        NJ = len(steps)
        groups(0, 0)
        groups(0, 1)
        evacs(0)
        for n in range(NJ):
            if n + 1 < NJ:
                groups(n + 1, 0)
            stage2_main(n)
            if n + 1 < NJ:
                groups(n + 1, 1)
            stage2_tail(n)
            if n + 1 < NJ:
                evacs(n + 1)
        TR.barrier()

    def phase_B(l):
        PA.reset()
        TR.dsem_next = 0
        src = xin if l == 0 else xa_d
        NCH = NSMAX // 4
        KT = PA([128, NCH, 4, 512], BF16, "KT")
        VA = PA([128, NSMAX, NH, VW], BF16, "VA")
        KVB = [Buf() for _ in range(NCH)]
        kds = [TR.new_dsem() for _ in range(NCH)]
        vds = [TR.new_dsem() for _ in range(NCH)]
        E = PA([128, NH * NMAT, 128], BF16, "E")
        EB = [Buf() for _ in range(NH)]
        eds = TR.new_dsem()
        for h in range(NH):
            Eh = E[:, h * NMAT:(h + 1) * NMAT, :].rearrange("p m q -> p (m q)")
            TR.dma("sp", Eh, biasmat[l, :, h * NMAT * 128:(h + 1) * NMAT * 128], w=[EB[h]], dsem=eds)
        Wo = PA([128, NKC, D], BF16, "Wo")
        mgt = PA([128, NKC], F32, "mgt")
        mgB = Buf()
        TR.dma("sp", mgt, mix_gT[l], w=[mgB], dsem=TR.new_dsem())
        WoG = [Buf(), Buf()]
        for g4 in range(2):
            TR.dma("pool", Wo[:, g4 * 4:(g4 + 1) * 4, :], w_out[l, g4 * 512:(g4 + 1) * 512, :].rearrange("(k p) c -> p k c", p=128),
                   w=[WoG[g4]], dsem=TR.new_swsem())
            tt("dve", Wo[:, g4 * 4:(g4 + 1) * 4, :], Wo[:, g4 * 4:(g4 + 1) * 4, :],
               mgt[:, g4 * 4:(g4 + 1) * 4].unsqueeze(2).to_broadcast([128, 4, D]), ALU.mult, r=[WoG[g4], mgB], w=[WoG[g4]])
        WoB = [WoG[kc // 4] for kc in range(NKC)]
        cw = PA([128, 2, CW], F32, "cw")
        cwB = Buf()
        TR.dma("sp", cw, conv_wT[l].rearrange("(c p) j -> p c j", p=128), w=[cwB], dsem=TR.new_dsem())
        Dg = PA([128, 2, CW, 128], BF16, "Dg")
        DgB = Buf()
        for c in range(2):
            for j in range(CW):
                ts("dve", Dg[:, c, j, :], ident, cw[:, c, j:j + 1], None, ALU.mult, None, r=[identB, cwB], w=[DgB])
        cbb = PA([128, 256], F32, "cbb")
        lgb = PA([128, 256], F32, "lgb")
        lbb = PA([128, 256], F32, "lbb")
        vecB = Buf()
        vds_ = TR.new_dsem()
        bcast_load(cbb, conv_b[l:l + 1, :], vecB, vds_)
        bcast_load(lgb, conv_ln_g[l:l + 1, :], vecB, vds_)
        bcast_load(lbb, conv_ln_b[l:l + 1, :], vecB, vds_)
        LAG = 2
        NQ = 3
        Qz = [PA([128, NH, 128], BF16, "Qz") for _ in range(NQ)]
        QzB = [Buf() for _ in range(NQ)]
        qds = [TR.new_dsem() for _ in range(NQ)]
        gTt = [PA([128, 2, 512 + 2 * GP], BF16, "gTt") for _ in range(2)]
        gTB = [Buf(), Buf()]
        gds = [TR.new_dsem(), TR.new_dsem()]
        xs = [PA([128, D], F32, "x") for _ in range(NQ)]
        xB = [Buf() for _ in range(NQ)]
        xds = [TR.new_dsem() for _ in range(NQ)]
        sds_ = [TR.new_dsem() for _ in range(NQ)]
        obf = [PA([128, D], BF16, "obf") for _ in range(NQ)]
        obfB = [Buf() for _ in range(NQ)]
        ofds = [TR.new_dsem() for _ in range(NQ)]
        ocat = [PA([128, 768], F32, "ocat") for _ in range(2)]
        ocA = [Buf(), Buf()]
        ocC = [Buf(), Buf()]
        sq = PA([128, 768], F32, "sq")
        sqB = Buf()
        ssg = PA([128, 12], F32, "ssg")
        rsg = PA([128, 12], F32, "rsg")
        ssgB, rsgB = Buf(), Buf()
        rden = PA([128, NH], F32, "rden")
        rdenB = Buf()
        oT = [PA([128, NKC, 128], BF16, "oT") for _ in range(2)]
        oTB = [Buf(), Buf()]
        P0 = [PA([128, 640], BF16, "P0") for _ in range(2)]
        P1 = [PA([128, 640], BF16, "P1") for _ in range(2)]
        P0B = [Buf(), Buf()]
        P1B = [Buf(), Buf()]
        yb = [PA([128, 256], F32, "yb") for _ in range(2)]
        ybB = [Buf(), Buf()]
        yn = PA([128, 256], F32, "yn")
        ey = PA([128, 256], F32, "ey")
        ynB, eyB = Buf(), Buf()
        bst = PA([128, 6], F32, "bst")
        mv = PA([128, 2], F32, "mv")
        rsl = PA([128, 1], F32, "rsl")
        bstB, mvB, rslB = Buf(), Buf(), Buf()
        TR.op("pool", lambda e: e.memset(VA, 1.0), w=KVB)

        MB = [Buf(), Buf()]
        XB = Buf()
        OB = [Buf(), Buf()]
        YB = Buf()
        TB = Buf()
        PB7 = Buf()
        O = PS[:, 3 * 512:5 * 512].rearrange("p (h d) -> p h d", h=NH)
        Y = PS[:, 5 * 512:5 * 512 + 256]

        subs = []
        for qi, S in enumerate(seq_lens):
            for i in range(S // 128):
                subs.append((qi, i, seq_off[qi] + i * 128))
        NSUB = len(subs)

        def load_kv(qi, ch):
            tok0 = seq_off[qi] + ch * 512
            TR.dma("sp", KT[:, ch, :, :], kt_d[:, :, tok0:tok0 + 512].rearrange("c p t -> p c t"), w=[KVB[ch]], dsem=kds[ch])
            TR.dma("sp", VA[:, ch * 4:(ch + 1) * 4, :, :].rearrange("p s h d -> p s (h d)"),
                   v_d[tok0:tok0 + 512, :].rearrange("(s p) d -> p s d", p=128), w=[KVB[ch]], dsem=vds[ch])

        def FE(n):
            qi, i, tok0 = subs[n]
            sl = n % NQ
            if i % 4 == 0:
                t = i // 4
                go = gpad_off[qi] + t * 512
                TR.dma("sp", gTt[t % 2], gt_d[:, :, go:go + 512 + 2 * GP].rearrange("c p t -> p c t"), w=[gTB[t % 2]], dsem=gds[t % 2])
            TR.dma("sp", Qz[sl].rearrange("p h t -> p (h t)"), qz_d[tok0 // 128], w=[QzB[sl]], dsem=qds[sl])

        def S1(n, fill=None):
            qi, i, tok0 = subs[n]
            S = seq_lens[qi]
            npairs = S // 128
            sl = n % NQ
            u = n % 2
            js, m0 = _pair_class(i, npairs)
            nj = len(js)
            n4 = min(nj, 4)
            t = i // 4
            sub = i % 4
            g = gTt[t % 2]
            convq = [(c, j) for c in range(2) for j in range(CW)]
            cpos = [0]
            ydone = [False]

            def conv_some(cnt):
                for _ in range(cnt):
                    if cpos[0] >= len(convq):
                        return
                    c, j = convq[cpos[0]]
                    cpos[0] += 1
                    o0 = sub * 128 + j + 1
                    mm(Y[:, c * 128:(c + 1) * 128], g[:, c, o0:o0 + 128], Dg[:, c, j, :], j == 0, j == CW - 1,
                       r=[gTB[t % 2], DgB], w=[YB], sig=(c == 1 and j == CW - 1))

            def qk(h):
                c = h // 2
                hb = h % 2
                for jj, j in enumerate(js):
                    if jj < 4:
                        o = PS[:, hb * 512 + jj * 128: hb * 512 + (jj + 1) * 128]
                        wb = MB[hb]
                    else:
                        conv_some(4)
                        o = PS[:, 2 * 512 + hb * 128: 2 * 512 + (hb + 1) * 128]
                        wb = XB
                    mm(o, KT[:, j // 4, c, (j % 4) * 128:(j % 4 + 1) * 128],
                       Qz[sl][:, h, :], True, False, r=[KVB[j // 4], QzB[sl]], w=[wb], sig=False)
                    mm(o, ident, E[:, h * NMAT + m0 + jj, :], False, True, r=[identB, EB[h]], w=[wb],
                       sig=(jj == nj - 1 or jj == 3))
                act(P1[hb][:, 0:n4 * 128], PS[:, hb * 512: hb * 512 + n4 * 128], AF.Exp, r=[MB[hb]], w=[P1B[hb]])
                if nj > 4:
                    act(P1[hb][:, 512:640], PS[:, 2 * 512 + hb * 128: 2 * 512 + (hb + 1) * 128], AF.Exp, r=[XB], w=[P1B[hb]])

            def pv(h):
                hb = h % 2
                for jj, j in enumerate(js):
                    mm(O[:, h, 0:HD + 1], P1[hb][:, jj * 128:(jj + 1) * 128], VA[:, j, h, 0:HD + 1], jj == 0, jj == nj - 1,
                       r=[P1B[hb], KVB[j // 4]], w=[OB[h // 4]], sig=(jj == nj - 1))

            if fill is not None:
                for f in fill.get(-1, ()):
                    f()
            conv_some(12)
            qk(0)
            for h in range(NH):
                if h + 1 < NH:
                    qk(h + 1)
                conv_some(8)
                pv(h)
                if cpos[0] >= len(convq) and not ydone[0]:
                    ydone[0] = True
                    tt("dve", yb[u], Y, cbb, ALU.add, r=[YB, vecB], w=[ybB[u]])
                if fill is not None:
                    for f in fill.get(h, ()):
                        f()
            conv_some(len(convq))
            for hf in range(2):
                TR.op("dve", lambda e, hf=hf: e.reciprocal(out=rden[:, hf * 4:hf * 4 + 4], in_=O[:, hf * 4:hf * 4 + 4, HD]),
                      r=[OB[hf]], w=[rdenB])
                tt("dve", ocat[u][:, hf * 256:(hf + 1) * 256].rearrange("p (h d) -> p h d", h=4), O[:, hf * 4:hf * 4 + 4, 0:HD],
                   rden[:, hf * 4:hf * 4 + 4].unsqueeze(2).to_broadcast([128, 4, HD]), ALU.mult, r=[OB[hf], rdenB], w=[ocA[u]])
            if not ydone[0]:
                tt("dve", yb[u], Y, cbb, ALU.add, r=[YB, vecB], w=[ybB[u]])
            if fill is not None:
                for f in fill.get(8, ()):
                    f()
            if i % 4 == 3 and qi + 1 < len(seq_lens):
                tcur = i // 4
                nt = S // 512
                chs = [tcur - 1, tcur] if tcur == nt - 1 else [tcur - 1]
                for chn in chs:
                    if 0 <= chn < seq_lens[qi + 1] // 512:
                        load_kv(qi + 1, chn)

        def S2(n, part=None):
            qi, i, tok0 = subs[n]
            sl = n % NQ
            u = n % 2
            if part is None or part == 0:
                S2a(n, sl, u, tok0)
            if part is None or part == 1:
                S2b(n, sl, u)
            if part is None or part == 2:
                S2c(n, sl, u)

        def S2a(n, sl, u, tok0):
            TR.dma("sp", xs[sl], src[tok0:tok0 + 128, :], w=[xB[sl]], dsem=xds[sl])
            TR.dma("sp", obf[sl][:, 0:256], of_d[tok0:tok0 + 128, :], w=[obfB[sl]], dsem=ofds[sl])
            TR.op("dve", lambda e: e.bn_stats(out=bst, in_=yb[u]), r=[ybB[u]], w=[bstB])
            TR.op("dve", lambda e: e.bn_aggr(out=mv, in_=bst), r=[bstB], w=[mvB])
            rsqrt_pool(rsl, mv[:, 1:2], 1.0, LN_EPS, r=[mvB], w=[rslB])
            ts("dve", yn, yb[u], mv[:, 0:1], rsl[:, 0:1], ALU.subtract, ALU.mult, r=[ybB[u], mvB, rslB], w=[ynB])
            tt("pool", yn, yn, lgb, ALU.mult, r=[ynB, vecB], w=[ynB])
            tt("pool", yn, yn, lbb, ALU.add, r=[ynB, vecB], w=[ynB])

        def S2b(n, sl, u):
            act(ey, yn, AF.Exp, r=[ynB], w=[eyB], scale=-1.0)
            ts("pool", ey, ey, 1.0, 1.0, ALU.add, ALU.mult, r=[eyB], w=[eyB])
            TR.op("dve", lambda e: e.reciprocal(out=ey, in_=ey), r=[eyB], w=[eyB])
            tt("dve", ocat[u][:, 512:768], yn, ey, ALU.mult, r=[ynB, eyB], w=[ocC[u]])

        def S2c(n, sl, u):
            act(sq, ocat[u], AF.Square, r=[ocA[u], ocC[u]], w=[sqB])
            TR.op("dve", lambda e: e.tensor_reduce(out=ssg, in_=sq.rearrange("p (g d) -> p g d", g=12), axis=AX.X, op=ALU.add),
                  r=[sqB], w=[ssgB])
            rsqrt_pool(rsg, ssg, 1.0 / HD, RMS_EPS, r=[ssgB], w=[rsgB])
            tt("dve", obf[sl][:, 256:D].rearrange("p (g d) -> p g d", g=12), ocat[u].rearrange("p (g d) -> p g d", g=12),
               rsg.unsqueeze(2).to_broadcast([128, 12, HD]), ALU.mult, r=[ocA[u], ocC[u], rsgB], w=[obfB[sl]])

        def S3(n, part=None):
            qi, i, tok0 = subs[n]
            sl = n % NQ
            u = n % 2
            if part is None or part == 0:
                for kc in range(NKC):
                    tr(bankb(6)[:, kc * 128:(kc + 1) * 128], obf[sl][:, kc * 128:(kc + 1) * 128], r=[obfB[sl]], w=[TB], sig=(kc == NKC - 1))
                evac("act", oT[u].rearrange("p a b -> p (a b)"), bankb(6), r=[TB], w=[oTB[u]])
            for half in range(2):
                if part is not None and part != half + 1:
                    continue
                bk = 7 - half
                bB = PB7 if half == 0 else TB
                for kc in range(NKC):
                    mm(bank(bk), oT[u][:, kc, :], Wo[:, kc, half * 512:(half + 1) * 512], kc == 0, kc == NKC - 1,
                       r=[oTB[u], WoB[kc]], w=[bB], sig=(kc == NKC - 1))
                tt("dve", xs[sl][:, half * 512:(half + 1) * 512], xs[sl][:, half * 512:(half + 1) * 512], bank(bk), ALU.add,
                   r=[xB[sl], bB], w=[xB[sl]])
            if part is None or part == 2:
                TR.dma("sp", xb_d[tok0:tok0 + 128, :], xs[sl], r=[xB[sl]], dsem=sds_[sl])

        for ch in range(seq_lens[0] // 512):
            load_kv(0, ch)
        done_extra = set()

        def extra_loads(qi):
            if qi + 1 < len(seq_lens) and qi not in done_extra:
                done_extra.add(qi)
                for chn in range(seq_lens[qi] // 512, seq_lens[qi + 1] // 512):
                    load_kv(qi + 1, chn)

        FE(0)
        if NSUB > 1:
            FE(1)
        S1(0)
        for k in range(NSUB + 1):
            if k + 2 < NSUB:
                FE(k + 2)
            if k < NSUB and subs[k][1] == 0:
                extra_loads(subs[k][0])
            n2 = k
            n3 = k - 1
            v2 = 0 <= n2 < NSUB
            v3 = 0 <= n3 < NSUB
            if k + 1 < NSUB:
                hooks = {}
                if v2:
                    hooks.setdefault(-1, []).append(lambda n=n2: S2(n, 0))
                    hooks.setdefault(1, []).append(lambda n=n2: S2(n, 1))
                    hooks.setdefault(3, []).append(lambda n=n2: S2(n, 2))
                if v3:
                    hooks.setdefault(2, []).append(lambda n=n3: S3(n, 0))
                    hooks.setdefault(4, []).append(lambda n=n3: S3(n, 1))
                    hooks.setdefault(8, []).append(lambda n=n3: S3(n, 2))
                S1(k + 1, hooks)
            else:
                if v3:
                    S3(n3)
                if v2:
                    S2(n2)
        TR.barrier()

    def phase_C(l):
        PA.reset()
        TR.dsem_next = 0
        last = (l == L - 1)
        dst = y if last else xa_d
        W1 = PA([128, NKC, DFF], BF16, "W1")
        W2 = PA([128, 32, D], BF16, "W2")
        g2t = PA([128, NKC], F32, "g2t")
        g2B = Buf()
        TR.dma("sp", g2t, norm2_gT[l], w=[g2B], dsem=TR.new_dsem())
        w1cb = load_fold_cols(W1, w_ff_in[l], [(b * 512, (b + 1) * 512) for b in range(8)], g2t, g2B)
        W2G = [Buf() for _ in range(8)]
        for g4 in range(8):
            TR.dma("pool", W2[:, g4 * 4:(g4 + 1) * 4, :], w_ff_out[l, g4 * 512:(g4 + 1) * 512, :].rearrange("(k p) c -> p k c", p=128),
                   w=[W2G[g4]], dsem=TR.new_swsem())
        W2B = [W2G[kc // 4] for kc in range(32)]
        if last:
            gfb = PA([128, D], F32, "gfb")
            gfB = Buf()
            bcast_load(gfb, final_norm_g[0:1, :], gfB, TR.new_dsem())
        xs = [PA([128, D], F32, "x") for _ in range(2)]
        xB = [Buf(), Buf()]
        xds = [TR.new_dsem(), TR.new_dsem()]
        hs = [PA([128, D], BF16, "h") for _ in range(4)]
        hB = [Buf() for _ in range(4)]
        ss = PA([128, 4], F32, "ss")
        rs = PA([128, 4], F32, "rs")
        ssB = [Buf() for _ in range(4)]
        rsB = [Buf() for _ in range(4)]
        hT = PA([128, NKC, 512], BF16, "hT")
        hTB = [Buf() for _ in range(4)]
        fT = PA([128, 32, 512], BF16, "fT")
        fTB = [Buf() for _ in range(32)]
        rt = [PA([128, 512], F32, "rt") for _ in range(2)]
        rtB = [Buf(), Buf()]
        xr = [PA([128, D], F32, "xr") for _ in range(2)]
        xrB = [Buf(), Buf()]
        xrds = [TR.new_dsem(), TR.new_dsem()]
        ods = [TR.new_dsem(), TR.new_dsem()]
        s2 = PA([128, 2], F32, "s2")
        r2 = PA([128, 2], F32, "r2")
        s2B = [Buf(), Buf()]
        r2B = [Buf(), Buf()]
        jk = PA([128, D], BF16, "jk")
        jkB = Buf()
        bctr = [0]
        xctr = [0]

        def FE_el(k):
            qi, t, tok0 = tiles[k]
            for s in range(4):
                xi = xctr[0] % 2
                xctr[0] += 1
                TR.dma("sp", xs[xi], xb_d[tok0 + s * 128: tok0 + (s + 1) * 128, :], w=[xB[xi]], dsem=xds[xi])
                act(hs[s], xs[xi], AF.Square, r=[xB[xi]], w=[hB[s], ssB[s]], accum=ss[:, s:s + 1])
                rsqrt_pool(rs[:, s:s + 1], ss[:, s:s + 1], 1.0 / D, RMS_EPS, r=[ssB[s]], w=[rsB[s]])
                ts("dve", hs[s], xs[xi], rs[:, s:s + 1], None, ALU.mult, None, r=[xB[xi], rsB[s]], w=[hB[s]])

        def FE_pe(k):
            for pr in range(4):
                bk = 6 + (pr % 2)
                for kk in range(2):
                    kc = pr * 2 + kk
                    for s in range(4):
                        tr(bankb(bk)[:, kk * 512 + s * 128: kk * 512 + (s + 1) * 128], hs[s][:, kc * 128:(kc + 1) * 128],
                           r=[hB[s]], w=[bankB[bk]], sig=(kk == 1 and s == 3))
                evac(evq(), hT[:, pr * 2:pr * 2 + 2, :].rearrange("p a b -> p (a b)"), bankb(bk), r=[bankB[bk]], w=[hTB[pr]])

        def BE1(k, hook):
            for fc in range(32):
                if fc == 10 and hook is not None:
                    hook()
                bk = bctr[0] % 6
                bctr[0] += 1
                for kc in range(NKC):
                    mm(bank(bk), W1[:, kc, fc * 128:(fc + 1) * 128], hT[:, kc, :], kc == 0, kc == NKC - 1,
                       r=[w1cb[((fc // 4) * 512, (fc // 4) * 512 + 512)], hTB[kc // 2]], w=[bankB[bk]], sig=(kc == NKC - 1))
                u = fc % 2
                act(rt[u], bank(bk), AF.Relu, r=[bankB[bk]], w=[rtB[u]])
                tt("dve", fT[:, fc, :], rt[u], rt[u], ALU.mult, r=[rtB[u]], w=[fTB[fc]])

        def BE2(k):
            qi, t, tok0 = tiles[k]
            for s in range(4):
                u = s % 2
                TR.dma("sp", xr[u], xb_d[tok0 + s * 128: tok0 + (s + 1) * 128, :], w=[xrB[u]], dsem=xrds[u])
                for half in range(2):
                    bk = bctr[0] % 6
                    bctr[0] += 1
                    for kc in range(32):
                        mm(bank(bk), fT[:, kc, s * 128:(s + 1) * 128], W2[:, kc, half * 512:(half + 1) * 512], kc == 0, kc == 31,
                           r=[fTB[kc], W2B[kc]], w=[bankB[bk]], sig=(kc == 31))
                    tt("dve", xr[u][:, half * 512:(half + 1) * 512], xr[u][:, half * 512:(half + 1) * 512], bank(bk), ALU.add,
                       r=[xrB[u], bankB[bk]], w=[xrB[u]])
                if last:
                    act(jk, xr[u], AF.Square, r=[xrB[u]], w=[jkB, s2B[u]], accum=s2[:, u:u + 1])
                    rsqrt_pool(r2[:, u:u + 1], s2[:, u:u + 1], 1.0 / D, RMS_EPS, r=[s2B[u]], w=[r2B[u]])
                    stt(xr[u], xr[u], r2[:, u:u + 1], gfb, ALU.mult, ALU.mult, r=[xrB[u], r2B[u], gfB], w=[xrB[u]])
                TR.dma("sp", dst[tok0 + s * 128: tok0 + (s + 1) * 128, :], xr[u], r=[xrB[u]], dsem=ods[u])

        FE_el(0)
        FE_pe(0)
        for k in range(NT):
            BE1(k, (lambda kk=k: FE_el(kk + 1)) if k + 1 < NT else None)
            if k + 1 < NT:
                FE_pe(k + 1)
            BE2(k)
        TR.barrier()

    import os as _os
    _ph = _os.environ.get("MK_PHASES", "AFBC")
    for l in range(L):
        if "A" in _ph:
            phase_A(l)
        if "F" in _ph:
            phase_F(l)
        if "B" in _ph:
            phase_B(l)
        if "C" in _ph:
            phase_C(l)
    TR.replay()
    return nc, TR, PA


_CACHE = {}


def _const_inputs(seq_lens):
    key = tuple(sorted(set(seq_lens)))
    if key not in _CACHE:
        d = {"ident": np.eye(128, dtype=np.float32).astype(ml_dtypes.bfloat16),
             "jrev": np.eye(128, dtype=np.float32)[::-1].copy().astype(ml_dtypes.bfloat16),
             "alt": np.stack([(-1.0) ** np.arange(128)] * 2, axis=1).astype(ml_dtypes.bfloat16)}
        for S in key:
            c, s, bc, bs = _dft_tables(S)
            d["cos%d" % S] = c
            d["sin%d" % S] = s
            d["bdc%d" % S] = bc
            d["bds%d" % S] = bs
        _CACHE[key] = d
    return _CACHE[key]


def make_weight_inputs(depth, norm1_g, w_in, rpb, conv_w, conv_b, conv_ln_g, conv_ln_b, mix_norm_g, w_out,
                       norm2_g, w_ff_in, w_ff_out, final_norm_g):
    f = lambda a: np.ascontiguousarray(np.asarray(a, dtype=np.float32))
    gT = lambda a: np.ascontiguousarray(f(a).reshape(depth, NKC, 128).transpose(0, 2, 1))
    DR, DC, VAL = _bias_index_tables()
    rpb = f(rpb)
    g = rpb[:, :, DR, DC]
    g = np.where(VAL[None, None], g, np.float32(NEG))
    g = np.ascontiguousarray(g.transpose(0, 3, 1, 2, 4)).reshape(depth, 128, NH * NMAT * 128)
    return {
        "norm1_gT": gT(norm1_g), "w_in": f(w_in), "biasmat": g.astype(ml_dtypes.bfloat16),
        "conv_wT": np.ascontiguousarray(f(conv_w).transpose(0, 2, 1)), "conv_b": f(conv_b),
        "conv_ln_g": f(conv_ln_g), "conv_ln_b": f(conv_ln_b), "mix_gT": gT(mix_norm_g), "w_out": f(w_out),
        "norm2_gT": gT(norm2_g), "w_ff_in": f(w_ff_in), "w_ff_out": f(w_ff_out),
        "final_norm_g": f(final_norm_g).reshape(1, D),
    }


_PROG = {}


def get_program(seq_lens, depth):
    key = (tuple(seq_lens), depth)
    if key not in _PROG:
        _PROG[key] = build_program(list(seq_lens), depth)[0]
    return _PROG[key]


def kernel(x_prompt, x_sample, norm1_g, w_in, rpb, conv_w, conv_b, conv_ln_g, conv_ln_b, mix_norm_g, w_out,
           norm2_g, w_ff_in, w_ff_out, final_norm_g):
    x_prompt = np.asarray(x_prompt, dtype=np.float32)
    x_sample = np.asarray(x_sample, dtype=np.float32)
    n = 8
    depth = int(np.asarray(w_in).shape[0])
    BP, SP_, _ = x_prompt.shape
    BS, SS, _ = x_sample.shape
    pp = BP // n
    sp = BS // n
    seq_lens = [SP_] * pp + [SS] * sp
    nc = get_program(seq_lens, depth)
    wts = make_weight_inputs(depth, norm1_g, w_in, rpb, conv_w, conv_b, conv_ln_g, conv_ln_b, mix_norm_g, w_out,
                             norm2_g, w_ff_in, w_ff_out, final_norm_g)
    consts = _const_inputs(seq_lens)
    in_maps = []
    for c in range(n):
        xin = np.concatenate([x_prompt[c * pp:(c + 1) * pp].reshape(pp * SP_, D),
                              x_sample[c * sp:(c + 1) * sp].reshape(sp * SS, D)], axis=0)
        m = {"xin": np.ascontiguousarray(xin)}
        m.update(wts)
        m.update(consts)
        in_maps.append(m)
    res = run_bass_kernel_spmd(nc, in_maps, core_ids=list(range(n)))
    yp = np.empty_like(x_prompt)
    ys = np.empty_like(x_sample)
    for c in range(n):
        yc = np.asarray(res.results[c]["y"], dtype=np.float32)
        yp[c * pp:(c + 1) * pp] = yc[:pp * SP_].reshape(pp, SP_, D)
        ys[c * sp:(c + 1) * sp] = yc[pp * SP_:].reshape(sp, SS, D)
    return (yp, ys)
```

```python
import numpy as np
import ml_dtypes
import concourse.bass as bass
import concourse.mybir as mybir
from concourse.bass_utils import run_bass_kernel_spmd

F32 = mybir.dt.float32
BF16 = mybir.dt.bfloat16
AF = mybir.ActivationFunctionType
ALU = mybir.AluOpType
AX = mybir.AxisListType

D = 1024
NKC = 8
DIN = 2304
DFF = 4096
NH = 8
HD = 64
CW = 31
RMS_EPS = 1e-6
LN_EPS = 1e-5
NEG = -30000.0
NMAT = 21
VW = 72
GP = 16


class Sem:
    def __init__(self, nc, name):
        self.h = nc.alloc_semaphore(name)
        self.v = 0


class Buf:
    __slots__ = ("name", "w", "r")

    def __init__(self, name=""):
        self.name = name
        self.w = {}
        self.r = {}


class Q:
    def __init__(self, name, sem):
        self.name = name
        self.sem = sem
        self.seen = {}
        self.prog = []
        self.pending = False


class Tracker:
    def __init__(self, nc):
        self.nc = nc
        self.q = {}
        for n in ("pe", "act", "dve", "pool", "sp"):
            self.q[n] = Q(n, Sem(nc, "q_" + n))
        self.bar = Sem(nc, "bar")
        self.dsems = []
        self.dsem_next = 0
        self.swsems = []
        self.swsem_next = 0
        self.ninst = 0

    def new_dsem(self):
        if self.dsem_next == len(self.dsems):
            self.dsems.append(Sem(self.nc, "d%d" % len(self.dsems)))
        s = self.dsems[self.dsem_next]
        self.dsem_next += 1
        return s

    def new_swsem(self):
        if self.swsem_next == len(self.swsems):
            self.swsems.append(Sem(self.nc, "w%d" % len(self.swsems)))
        s = self.swsems[self.swsem_next]
        self.swsem_next += 1
        return s

    def _waits(self, q, needs):
        for s, v in needs.items():
            if v > 0 and q.seen.get(s, 0) < v:
                q.prog.append(("wait", s, v))
                q.seen[s] = v

    def op(self, qn, fn, r=(), w=(), sig=True):
        q = self.q[qn]
        needs = {}
        is_pe = qn == "pe"
        for b in r:
            for s, v in b.w.items():
                if s is q.sem and is_pe:
                    continue
                if needs.get(s, 0) < v:
                    needs[s] = v
        for b in w:
            for s, v in b.w.items():
                if s is q.sem:
                    continue
                if needs.get(s, 0) < v:
                    needs[s] = v
            for s, v in b.r.items():
                if s is q.sem:
                    continue
                if needs.get(s, 0) < v:
                    needs[s] = v
        self._waits(q, needs)
        if sig:
            q.sem.v += 1
            tv = q.sem.v
            q.prog.append(("inst", fn, q.sem, 1))
            q.pending = False
        else:
            assert is_pe
            tv = q.sem.v + 1
            q.prog.append(("inst", fn, None, 0))
            q.pending = True
        s = q.sem
        for b in r:
            if b.r.get(s, 0) < tv:
                b.r[s] = tv
        for b in w:
            if b.w.get(s, 0) < tv:
                b.w[s] = tv
        self.ninst += 1

    def dma(self, qn, out, in_, r=(), w=(), dsem=None, **kw):
        q = self.q[qn]
        needs = {}
        for b in r:
            for s, v in b.w.items():
                if needs.get(s, 0) < v:
                    needs[s] = v
        for b in w:
            for s, v in b.w.items():
                if needs.get(s, 0) < v:
                    needs[s] = v
            for s, v in b.r.items():
                if needs.get(s, 0) < v:
                    needs[s] = v
        if needs.get(dsem, 0) < dsem.v:
            needs[dsem] = dsem.v
        self._waits(q, needs)
        dsem.v += 16
        tv = dsem.v

        def fn(e, out=out, in_=in_, kw=kw):
            return e.dma_start(out=out, in_=in_, **kw)
        q.prog.append(("inst", fn, dsem, 16))
        for b in r:
            if b.r.get(dsem, 0) < tv:
                b.r[dsem] = tv
        for b in w:
            if b.w.get(dsem, 0) < tv:
                b.w[dsem] = tv
        self.ninst += 1

    def barrier(self):
        sp = self.q["sp"]
        needs = {}
        for n, q in self.q.items():
            assert not q.pending, n
            if n != "sp":
                needs[q.sem] = q.sem.v
        for s in self.dsems + self.swsems:
            needs[s] = s.v
        self._waits(sp, needs)
        self.bar.v += 1
        sp.prog.append(("seminc", self.bar, 1))
        for n, q in self.q.items():
            if n != "sp":
                q.prog.append(("wait", self.bar, self.bar.v))
            for s, v in needs.items():
                q.seen[s] = max(q.seen.get(s, 0), v)
        self.dsem_next = 0
        self.swsem_next = 0

    def replay(self):
        nc = self.nc
        progs = self.q

        def run(e, prog):
            for it in prog:
                if it[0] == "wait":
                    e.wait_ge(it[1].h, it[2])
                elif it[0] == "inst":
                    ins = it[1](e)
                    if it[2] is not None:
                        ins.then_inc(it[2].h, it[3])
                else:
                    e.sem_inc(it[1].h, it[2])

        with nc.Block() as block:
            @block.sync
            def _(e):
                run(e, progs["sp"].prog)

            @block.tensor
            def _(e):
                run(e, progs["pe"].prog)

            @block.scalar
            def _(e):
                run(e, progs["act"].prog)

            @block.vector
            def _(e):
                run(e, progs["dve"].prog)

            @block.gpsimd
            def _(e):
                run(e, progs["pool"].prog)


class Alloc:
    def __init__(self, nc, base, limit):
        self.nc = nc
        self.base = base
        self.off = base
        self.limit = limit
        self.cnt = 0
        self.peak = base

    def reset(self):
        self.off = self.base

    def __call__(self, shape, dtype, name="t"):
        nbytes = int(np.prod(shape[1:])) * (4 if dtype == F32 else 2)
        nbytes = (nbytes + 63) // 64 * 64
        assert self.off + nbytes <= self.limit, ("SBUF overflow", name, self.off, nbytes, self.limit)
        self.cnt += 1
        h = self.nc.alloc_sbuf_tensor_at("%s_%d" % (name, self.cnt), list(shape), dtype, offset=self.off)
        self.off += nbytes
        self.peak = max(self.peak, self.off)
        return h.ap()


def _bias_index_tables():
    R = 16
    cls = [(0, [0, 1, 2, 3]), (1, [-1, 0, 1, 2]), (3, [-2, -1, 0, 1, 2]),
           (R // 2 - 2, [-2, -1, 0, 1]), (R // 2 - 1, [-3, -2, -1, 0])]
    DR = np.zeros((NMAT, 128, 128), np.int64)
    DC = np.zeros((NMAT, 128, 128), np.int64)
    VAL = np.zeros((NMAT, 128, 128), bool)
    kk = np.arange(128)
    krl, kc = kk // 64, kk % 64
    qrl, qc = kk // 64, kk % 64
    m = 0
    for (i, deltas) in cls:
        for dl in deltas:
            j = i + dl
            r = (2 * i + qrl)[None, :]
            kr = (2 * j + krl)[:, None]
            rs = np.clip(r - 4, 0, R - 8)
            vrow = (kr >= rs) & (kr < rs + 8)
            cs = np.clip(qc - 8, 0, 48)[None, :]
            vcol = (kc[:, None] >= cs) & (kc[:, None] < cs + 16)
            DR[m] = np.clip(kr - r + 7, 0, 14)
            DC[m] = np.clip(kc[:, None] - qc[None, :] + 15, 0, 30)
            VAL[m] = vrow & vcol
            m += 1
    assert m == NMAT
    return DR, DC, VAL


def _pair_class(i, npairs):
    if i == 0:
        return [0, 1, 2, 3], 0
    if i == 1:
        return [0, 1, 2, 3], 4
    if i == npairs - 2:
        return [i - 2, i - 1, i, i + 1], 13
    if i == npairs - 1:
        return [i - 3, i - 2, i - 1, i], 17
    return [i - 2, i - 1, i, i + 1, i + 2], 8


def _dft_tables(S):
    nS, nK = S // 128, S // 512
    idx = (np.arange(S, dtype=np.int64)[:, None] * np.arange(S, dtype=np.int64)[None, :]) % S
    ang = 2.0 * np.pi * idx.astype(np.float64) / S
    out = []
    for tab in (np.cos(ang), np.sin(ang)):
        t = tab.reshape(nS, 128, nK, 512).transpose(2, 1, 0, 3).reshape(nK, 128, nS * 512)[:nK // 2]
        out.append(np.ascontiguousarray(t).astype(ml_dtypes.bfloat16))
    a64 = 2.0 * np.pi * ((np.arange(64)[:, None] * np.arange(64)[None, :]) % 64) / 64.0
    sc = 1.0 / np.sqrt(S * 64.0)
    bdc = np.zeros((128, 128))
    bds = np.zeros((128, 128))
    for g in range(2):
        bdc[g * 64:(g + 1) * 64, g * 64:(g + 1) * 64] = np.cos(a64) * sc
        bds[g * 64:(g + 1) * 64, g * 64:(g + 1) * 64] = -np.sin(a64) * sc
    bdc2 = np.concatenate([bdc, bdc], axis=1)
    bds2 = np.concatenate([bds, -bds], axis=1)
    return out[0], out[1], bdc2.astype(ml_dtypes.bfloat16), bds2.astype(ml_dtypes.bfloat16)


def build_program(seq_lens, depth):
    nc = bass.Bass("TRN2", target_bir_lowering=False)
    L = depth
    T = sum(seq_lens)
    seq_off = [sum(seq_lens[:i]) for i in range(len(seq_lens))]
    gpad_off = [sum(s + 2 * GP for s in seq_lens[:i]) for i in range(len(seq_lens))]
    TP = sum(s + 2 * GP for s in seq_lens)
    Sset = sorted(set(seq_lens))
    SMAX = max(seq_lens)
    NSMAX = SMAX // 128

    def din(name, shape, dt=F32):
        return nc.dram_tensor(name, list(shape), dt, kind="ExternalInput").ap()

    def dtmp(name, shape, dt):
        return nc.dram_tensor(name, list(shape), dt).ap()

    xin = din("xin", [T, D])
    norm1_gT = din("norm1_gT", [L, 128, NKC])
    w_in = din("w_in", [L, D, DIN])
    biasmat = din("biasmat", [L, 128, NH * NMAT * 128], BF16)
    conv_wT = din("conv_wT", [L, 256, CW])
    conv_b = din("conv_b", [L, 256])
    conv_ln_g = din("conv_ln_g", [L, 256])
    conv_ln_b = din("conv_ln_b", [L, 256])
    mix_gT = din("mix_gT", [L, 128, NKC])
    w_out = din("w_out", [L, D, D])
    norm2_gT = din("norm2_gT", [L, 128, NKC])
    w_ff_in = din("w_ff_in", [L, D, DFF])
    w_ff_out = din("w_ff_out", [L, DFF, D])
    final_norm_g = din("final_norm_g", [1, D])
    ident_d = din("ident", [128, 128], BF16)
    cos_d, sin_d, bdc_d, bds_d = {}, {}, {}, {}
    for S in Sset:
        cos_d[S] = din("cos%d" % S, [S // 1024, 128, (S // 128) * 512], BF16)
        sin_d[S] = din("sin%d" % S, [S // 1024, 128, (S // 128) * 512], BF16)
        bdc_d[S] = din("bdc%d" % S, [128, 256], BF16)
        bds_d[S] = din("bds%d" % S, [128, 256], BF16)
    jrev_d = din("jrev", [128, 128], BF16)
    alt_d = din("alt", [128, 2], BF16)
    y = nc.dram_tensor("y", [T, D], F32, kind="ExternalOutput").ap()

    xa_d = dtmp("xa_d", [T, D], F32)
    xb_d = dtmp("xb_d", [T, D], F32)
    qz_d = dtmp("qz_d", [T // 128, 128, NH * 128], BF16)
    kt_d = dtmp("kt_d", [4, 128, T], BF16)
    v_d = dtmp("v_d", [T, NH * VW], BF16)
    a_d = dtmp("a_d", [T, 256], BF16)
    gt_d = dtmp("gt_d", [2, 128, TP], BF16)
    of_d = dtmp("of_d", [T, 256], BF16)

    TR = Tracker(nc)
    PS = nc.alloc_psum_tensor("ps", [128, 4096], F32).ap()
    PSB = PS.bitcast(BF16)
    bankB = [Buf("bank%d" % i) for i in range(8)]

    def bank(i):
        return PS[:, i * 512:(i + 1) * 512]

    def bankb(i):
        return PSB[:, i * 1024:(i + 1) * 1024]

    SB_BASE = 16512
    SB_LIMIT = 229344
    CA = Alloc(nc, SB_BASE, SB_BASE + 8192)
    ident = CA([128, 128], BF16, "ident")
    identB = Buf("ident")
    mhalf = CA([128, 16], F32, "mhalf")
    zpad = CA([128, 2, GP], BF16, "zpad")
    epsc = CA([128, 2], F32, "epsc")
    constB = Buf("const")
    bdc, bds = {}, {}
    for S in Sset:
        bdc[S] = CA([128, 256], BF16, "bdc")
        bds[S] = CA([128, 256], BF16, "bds")
    jrev = CA([128, 128], BF16, "jrev")
    alt = CA([128, 2], BF16, "alt")
    PA = Alloc(nc, CA.off, SB_LIMIT)

    def mm(out, lhsT, rhs, start, stop, r, w, sig):
        TR.op("pe", lambda e, o=out, a=lhsT, b=rhs, s0=start, s1=stop: e.matmul(o, lhsT=a, rhs=b, start=s0, stop=s1),
              r=r, w=w, sig=sig)

    def tr(out, in_, r, w, sig):
        TR.op("pe", lambda e, o=out, a=in_: e.transpose(o, a, ident), r=list(r) + [identB], w=w, sig=sig)

    def act(out, in_, func, r, w, scale=1.0, accum=None):
        if accum is None:
            TR.op("act", lambda e, o=out, i=in_, f=func, s=scale: e.activation(out=o, in_=i, func=f, scale=s), r=r, w=w)
        else:
            TR.op("act", lambda e, o=out, i=in_, f=func, s=scale, a=accum: e.activation(out=o, in_=i, func=f, scale=s, accum_out=a),
                  r=r, w=w)

    def vcopy(qn, out, in_, r, w):
        TR.op(qn, lambda e, o=out, i=in_: e.tensor_copy(out=o, in_=i), r=r, w=w)

    def tt(qn, out, in0, in1, op, r, w):
        TR.op(qn, lambda e, o=out, a=in0, b=in1, p=op: e.tensor_tensor(out=o, in0=a, in1=b, op=p), r=r, w=w)

    def ts(qn, out, in0, s1, s2, op0, op1, r, w):
        if s2 is None:
            TR.op(qn, lambda e, o=out, a=in0, x=s1, p=op0: e.tensor_scalar(out=o, in0=a, scalar1=x, scalar2=None, op0=p), r=r, w=w)
        else:
            TR.op(qn, lambda e, o=out, a=in0, x=s1, y=s2, p=op0, p1=op1: e.tensor_scalar(out=o, in0=a, scalar1=x, scalar2=y, op0=p, op1=p1),
                  r=r, w=w)

    def stt(out, in0, scalar, in1, op0, op1, r, w):
        TR.op("dve", lambda e, o=out, a=in0, s=scalar, b=in1, p0=op0, p1=op1:
              e.scalar_tensor_tensor(out=o, in0=a, scalar=s, in1=b, op0=p0, op1=p1), r=r, w=w)

    def rsqrt_pool(out, in_, mul, add, r, w):
        n = out.shape[1]
        ts("pool", out, in_, mul, add, ALU.mult, ALU.add, r=r, w=w)
        tt("pool", out, out, mhalf[:, 0:n], ALU.pow, r=list(w) + [constB], w=w)

    def bcast_load(dst, src_row, buf, dsem):
        TR.dma("sp", dst, src_row.partition_broadcast(128), w=[buf], dsem=dsem)

    ebal = [0]

    def evq():
        ebal[0] += 1
        return "act" if ebal[0] % 2 else "dve"

    def evac(qn, out, in_, r, w, scale=None):
        if qn == "act":
            act(out, in_, AF.Copy, r, w, scale=1.0 if scale is None else scale)
        elif scale is None:
            vcopy(qn, out, in_, r, w)
        else:
            ts(qn, out, in_, scale, None, ALU.mult, None, r, w)

    cds = TR.new_dsem()
    TR.dma("sp", ident, ident_d, w=[identB], dsem=cds)
    for S in Sset:
        TR.dma("sp", bdc[S], bdc_d[S], w=[constB], dsem=cds)
        TR.dma("sp", bds[S], bds_d[S], w=[constB], dsem=cds)
    TR.dma("sp", jrev, jrev_d, w=[constB], dsem=cds)
    TR.dma("sp", alt, alt_d, w=[constB], dsem=cds)
    TR.op("pool", lambda e: e.memset(mhalf, -0.5), w=[constB])
    TR.op("pool", lambda e: e.memset(zpad, 0.0), w=[constB])
    TR.op("pool", lambda e: e.memset(epsc, RMS_EPS), w=[constB])
    for qi, S in enumerate(seq_lens):
        for side in range(2):
            o = gpad_off[qi] + (0 if side == 0 else GP + S)
            TR.dma("sp", gt_d[:, :, o:o + GP].rearrange("c p t -> p c t"), zpad, r=[constB], dsem=cds)
    TR.barrier()

    tiles = []
    for qi, S in enumerate(seq_lens):
        for t in range(S // 512):
            tiles.append((qi, t, seq_off[qi] + t * 512))
    NT = len(tiles)

    def load_fold_weight(W, WBs, src_rows, gt, gB, dsem, nsplit, width):
        for kc in range(len(WBs)):
            if nsplit > 1:
                TR.dma("pool", W[:, kc, :].rearrange("p (a b) -> p a b", a=nsplit),
                       src_rows(kc).rearrange("p (a b) -> p a b", a=nsplit), w=[WBs[kc]], dsem=dsem)
            else:
                TR.dma("pool", W[:, kc, :], src_rows(kc), w=[WBs[kc]], dsem=dsem)
            if gt is not None:
                ts("dve", W[:, kc, :], W[:, kc, :], gt[:, kc:kc + 1], None, ALU.mult, None, r=[WBs[kc], gB], w=[WBs[kc]])

    def load_fold_cols(W, src2d, blocks, gt, gB, dsem=None):
        bufs = {}
        for (c0, c1) in blocks:
            b = Buf()
            bufs[(c0, c1)] = b
            TR.dma("pool", W[:, :, c0:c1], src2d[:, c0:c1].rearrange("(k p) c -> p k c", p=128), w=[b], dsem=TR.new_swsem())
            tt("dve", W[:, :, c0:c1], W[:, :, c0:c1], gt.unsqueeze(2).to_broadcast([128, NKC, c1 - c0]), ALU.mult,
               r=[b, gB], w=[b])
        return bufs

    def phase_A(l):
        PA.reset()
        TR.dsem_next = 0
        src = xin if l == 0 else xa_d
        Win = PA([128, NKC, DIN], BF16, "Win")
        g1t = PA([128, NKC], F32, "g1t")
        g1B = Buf()
        TR.dma("sp", g1t, norm1_gT[l], w=[g1B], dsem=TR.new_dsem())
        order = [1, 2, 3, 4, 8, 7, 5, 6, 0]
        wcb = load_fold_cols(Win, w_in[l], [(b * 256, (b + 1) * 256) for b in order], g1t, g1B)
        WC = lambda col: wcb[((col // 256) * 256, (col // 256) * 256 + 256)]
        xs = [PA([128, D], F32, "x") for _ in range(8)]
        xB = [Buf() for _ in range(8)]
        xds = [TR.new_dsem() for _ in range(8)]
        hs = [PA([128, D], BF16, "h") for _ in range(8)]
        hB = [Buf() for _ in range(8)]
        ss = PA([128, 8], F32, "ss")
        rs = PA([128, 8], F32, "rs")
        ssB = [Buf(), Buf()]
        rsB = [Buf(), Buf()]
        hT = [PA([128, NKC, 512], BF16, "hT") for _ in range(2)]
        hTB = [[Buf() for _ in range(4)] for _ in range(2)]
        Qz = [PA([128, NH, 512], BF16, "Qz") for _ in range(2)]
        Kst = [PA([128, 4, 512], BF16, "Kst") for _ in range(2)]
        Gst = [PA([128, 2, 512], BF16, "Gst") for _ in range(2)]
        Vst = [PA([128, 4, NH, VW], BF16, "Vst") for _ in range(2)]
        Ast = [PA([128, 4, 256], BF16, "Ast") for _ in range(2)]
        QzB, KstB, GstB, VstB, AstB = ([Buf(), Buf()] for _ in range(5))
        stds = [[TR.new_dsem() for _ in range(5)] for _ in range(2)]
        et = [PA([128, 512], F32, "et") for _ in range(2)]
        etB = [Buf(), Buf()]
        for sl in range(2):
            TR.op("pool", lambda e, a=Qz[sl]: e.memset(a, 0.0), w=[QzB[sl]])
            TR.op("pool", lambda e, a=Vst[sl]: e.memset(a, 1.0), w=[VstB[sl]])
        bctr = [0]

        def nbank():
            b = bctr[0] % 6
            bctr[0] += 1
            return b

        def FE_el(k):
            qi, t, tok0 = tiles[k]
            sl = k % 2
            for s in range(4):
                i = sl * 4 + s
                TR.dma("sp", xs[i], src[tok0 + s * 128: tok0 + (s + 1) * 128, :], w=[xB[i]], dsem=xds[i])
            for s in range(4):
                i = sl * 4 + s
                act(hs[i], xs[i], AF.Square, r=[xB[i]], w=[hB[i], ssB[sl]], accum=ss[:, i:i + 1])
            rsqrt_pool(rs[:, sl * 4:sl * 4 + 4], ss[:, sl * 4:sl * 4 + 4], 1.0 / D, RMS_EPS, r=[ssB[sl]], w=[rsB[sl]])
            for s in range(4):
                i = sl * 4 + s
                ts("dve", hs[i], xs[i], rs[:, i:i + 1], None, ALU.mult, None, r=[xB[i], rsB[sl]], w=[hB[i]])

        def FE_pe(k):
            sl = k % 2
            for pr in range(4):
                bk = 6 + (pr % 2)
                for kk in range(2):
                    kc = pr * 2 + kk
                    for s in range(4):
                        i = sl * 4 + s
                        tr(bankb(bk)[:, kk * 512 + s * 128: kk * 512 + (s + 1) * 128], hs[i][:, kc * 128:(kc + 1) * 128],
                           r=[hB[i]], w=[bankB[bk]], sig=(kk == 1 and s == 3))
                evac(evq(), hT[sl][:, pr * 2:pr * 2 + 2, :].rearrange("p a b -> p (a b)"), bankb(bk), r=[bankB[bk]], w=[hTB[sl][pr]])

        def fm_group(sl, col):
            bk = nbank()
            for kc in range(NKC):
                mm(bank(bk), Win[:, kc, col:col + 128], hT[sl][:, kc, :], kc == 0, kc == NKC - 1,
                   r=[WC(col), hTB[sl][kc // 2]], w=[bankB[bk]], sig=(kc == NKC - 1))
            return bk

        def BE(k, hook):
            qi, t, tok0 = tiles[k]
            sl = k % 2
            for c in range(4):
                bk = fm_group(sl, 256 + c * 128)
                act(Qz[sl][0:64, 2 * c, :], bank(bk)[0:64, :], AF.Copy, r=[bankB[bk]], w=[QzB[sl]], scale=0.125)
                act(Qz[sl][64:128, 2 * c + 1, :], bank(bk)[64:128, :], AF.Copy, r=[bankB[bk]], w=[QzB[sl]], scale=0.125)
            for c in range(4):
                bk = fm_group(sl, 768 + c * 128)
                vcopy("dve", Kst[sl][:, c, :], bank(bk), r=[bankB[bk]], w=[KstB[sl]])
            if hook is not None:
                hook()
            for c in range(2):
                bk = fm_group(sl, 2048 + c * 128)
                act(et[c], bank(bk), AF.Exp, r=[bankB[bk]], w=[etB[c]], scale=-1.0)
                ts("pool", et[c], et[c], 1.0, 1.0, ALU.add, ALU.mult, r=[etB[c]], w=[etB[c]])
                TR.op("dve", lambda e, a=et[c]: e.reciprocal(out=a, in_=a), r=[etB[c]], w=[etB[c]])
                bk = fm_group(sl, 1792 + c * 128)
                tt("dve", Gst[sl][:, c, :], bank(bk), et[c], ALU.mult, r=[bankB[bk], etB[c]], w=[GstB[sl]])
            for s in range(4):
                bk = nbank()
                for kc in range(NKC):
                    mm(bank(bk), hT[sl][:, kc, s * 128:(s + 1) * 128], Win[:, kc, 1280:1792], kc == 0, kc == NKC - 1,
                       r=[WC(1280), WC(1536), hTB[sl][kc // 2]], w=[bankB[bk]], sig=(kc == NKC - 1))
                act(Vst[sl][:, s, :, 0:64], bank(bk).rearrange("p (h d) -> p h d", h=NH), AF.Copy, r=[bankB[bk]], w=[VstB[sl]])
                bk = nbank()
                for kc in range(NKC):
                    mm(bank(bk)[:, 0:256], hT[sl][:, kc, s * 128:(s + 1) * 128], Win[:, kc, 0:256], kc == 0, kc == NKC - 1,
                       r=[WC(0), hTB[sl][kc // 2]], w=[bankB[bk]], sig=(kc == NKC - 1))
                vcopy("dve", Ast[sl][:, s, :], bank(bk)[:, 0:256], r=[bankB[bk]], w=[AstB[sl]])
            ds = stds[sl]
            TR.dma("sp", qz_d[tok0 // 128: tok0 // 128 + 4, :, :].rearrange("s p (h t) -> p h s t", h=NH),
                   Qz[sl].rearrange("p h (s t) -> p h s t", s=4), r=[QzB[sl]], dsem=ds[0])
            TR.dma("sp", kt_d[:, :, tok0:tok0 + 512].rearrange("c p t -> p c t"), Kst[sl], r=[KstB[sl]], dsem=ds[1])
            go = gpad_off[qi] + GP + t * 512
            TR.dma("sp", gt_d[:, :, go:go + 512].rearrange("c p t -> p c t"), Gst[sl], r=[GstB[sl]], dsem=ds[2])
            TR.dma("sp", v_d[tok0:tok0 + 512, :].rearrange("(s p) d -> p s d", p=128), Vst[sl].rearrange("p s h d -> p s (h d)"),
                   r=[VstB[sl]], dsem=ds[3])
            TR.dma("sp", a_d[tok0:tok0 + 512, :].rearrange("(s p) d -> p s d", p=128), Ast[sl], r=[AstB[sl]], dsem=ds[4])

        FE_el(0)
        FE_pe(0)
        if NT > 1:
            FE_el(1)
        for k in range(NT):
            if k + 1 < NT:
                FE_pe(k + 1)
            BE(k, (lambda kk=k: FE_el(kk + 2)) if k + 2 < NT else None)
        TR.barrier()

    def phase_F(l):
        PA.reset()
        TR.dsem_next = 0
        NSEQ = len(seq_lens)
        Atm = [PA([128, seq_lens[qi] // 128, 256], BF16, "Atm") for qi in range(NSEQ)]
        AtmB = [Buf() for _ in range(NSEQ)]
        for qi in range(NSEQ):
            S = seq_lens[qi]
            TR.dma("sp", Atm[qi], a_d[seq_off[qi]:seq_off[qi] + S, :].rearrange("(t p) c -> p t c", p=128),
                   w=[AtmB[qi]], dsem=TR.new_dsem())
        cosb = [PA([128, NSMAX * 512], BF16, "cos") for _ in range(2)]
        sinb = [PA([128, NSMAX * 512], BF16, "sin") for _ in range(2)]
        cosB = [Buf(), Buf()]
        sinB = [Buf(), Buf()]
        cds_ = [TR.new_dsem(), TR.new_dsem()]
        sds_ = [TR.new_dsem(), TR.new_dsem()]
        PQ = [[PA([128, 512], BF16, "pq") for _ in range(4)] for _ in range(2)]
        PQB = [[Buf() for _ in range(4)] for _ in range(2)]
        sqf = [PA([128, 512], F32, "sqf") for _ in range(4)]
        ssf = [PA([128, 8], F32, "ssf") for _ in range(4)]
        rsf = [PA([128, 8], F32, "rsf") for _ in range(4)]
        om = [PA([128, 256], BF16, "om") for _ in range(4)]
        sqfB, ssfB, rsfB, omB = ([Buf() for _ in range(4)] for _ in range(4))
        ofd = [PA([128, 4, 256], BF16, "ofd") for _ in range(2)]
        ofm = [PA([128, 4, 256], BF16, "ofm") for _ in range(2)]
        ofdB = [Buf(), Buf()]
        ofmB = [Buf(), Buf()]
        odds = [TR.new_dsem(), TR.new_dsem()]
        omds = [TR.new_dsem(), TR.new_dsem()]
        omds2 = [TR.new_dsem(), TR.new_dsem()]
        PH = PA([128, 2], BF16, "PH")
        sq1 = PA([128, 256], F32, "sq1")
        ss1 = PA([128, 4], F32, "ss1")
        rs1 = PA([128, 4], F32, "rs1")
        oh = PA([128, 256], BF16, "oh")
        PHB, sq1B, ss1B, rs1B, ohB = (Buf() for _ in range(5))
        ohds = TR.new_dsem()

        steps = []
        jn = 0
        for S in sorted(set(seq_lens), key=lambda v: seq_lens.index(v)):
            qis = [qi for qi in range(NSEQ) if seq_lens[qi] == S]
            for kt in range(S // 1024):
                for ii, qi in enumerate(qis):
                    steps.append((S, kt, qi, ii == 0, jn))
                jn += 1

        def groups(n, part):
            S, kt, qi, first, j = steps[n]
            nS = S // 128
            sl = n % 2
            tl = j % 2
            a = Atm[qi]
            if first and part == 0:
                TR.dma("sp", cosb[tl][:, 0:nS * 512], cos_d[S][kt], w=[cosB[tl]], dsem=cds_[tl])
                TR.dma("sp", sinb[tl][:, 0:nS * 512], sin_d[S][kt], w=[sinB[tl]], dsem=sds_[tl])
            for c in range(2):
                if c != part:
                    continue
                for which in range(2):
                    bk = sl * 4 + c * 2 + which
                    tab = cosb[tl] if which == 0 else sinb[tl]
                    tB = cosB[tl] if which == 0 else sinB[tl]
                    for st in range(nS):
                        mm(bank(bk), a[:, st, c * 128:(c + 1) * 128], tab[:, st * 512:(st + 1) * 512], st == 0, st == nS - 1,
                           r=[AtmB[qi], tB], w=[bankB[bk]], sig=(st == nS - 1))

        def evacs(n):
            sl = n % 2
            for c in range(2):
                for which in range(2):
                    bk = sl * 4 + c * 2 + which
                    evac("act" if which == 0 else "dve", PQ[sl][c * 2 + which], bank(bk), r=[bankB[bk]], w=[PQB[sl][c * 2 + which]])

        def stage2_main(n):
            S, kt, qi, first, j = steps[n]
            sl = n % 2
            for sub in range(4):
                bk = sl * 4 + sub
                u = sub
                for c in range(2):
                    o = bank(bk)[:, c * 256:(c + 1) * 256]
                    mm(o, PQ[sl][c * 2][:, sub * 128:(sub + 1) * 128], bdc[S], True, False,
                       r=[PQB[sl][c * 2], constB], w=[bankB[bk]], sig=False)
                    mm(o, PQ[sl][c * 2 + 1][:, sub * 128:(sub + 1) * 128], bds[S], False, True,
                       r=[PQB[sl][c * 2 + 1], constB], w=[bankB[bk]], sig=(c == 1))
                O5 = bank(bk).rearrange("p (c m g d) -> p c m g d", c=2, m=2, g=2)
                rs5 = rsf[u].rearrange("p (c m g) -> p c m g", c=2, m=2)
                act(sqf[u], bank(bk), AF.Square, r=[bankB[bk]], w=[sqfB[u]])
                TR.op("dve", lambda e, o=ssf[u], i=sqf[u]: e.tensor_reduce(out=o, in_=i.rearrange("p (a d) -> p a d", a=8), axis=AX.X, op=ALU.add),
                      r=[sqfB[u]], w=[ssfB[u]])
                TR.op("act", lambda e, o=rsf[u], i=ssf[u]: e.activation(out=o, in_=i, func=AF.Sqrt, bias=epsc[:, 0:1], scale=1.0 / HD),
                      r=[ssfB[u], constB], w=[rsfB[u]])
                TR.op("dve", lambda e, o=rsf[u]: e.reciprocal(out=o, in_=o), r=[rsfB[u]], w=[rsfB[u]])
                tt("dve", ofd[sl][:, sub, :].rearrange("p (c g d) -> p c g d", c=2, g=2), O5[:, :, 0],
                   rs5[:, :, 0].unsqueeze(3).to_broadcast([128, 2, 2, HD]), ALU.mult, r=[bankB[bk], rsfB[u]], w=[ofdB[sl]])
                tt("dve", om[u].rearrange("p (c g d) -> p c g d", c=2, g=2), O5[:, :, 1],
                   rs5[:, :, 1].unsqueeze(3).to_broadcast([128, 2, 2, HD]), ALU.mult, r=[bankB[bk], rsfB[u]], w=[omB[u]])

        def stage2_tail(n):
            S, kt, qi, first, j = steps[n]
            nS = S // 128
            sl = n % 2
            k0 = kt * 512
            for sub in range(4):
                bk = sl * 4 + sub
                u = sub
                mm(bank(bk)[:, 0:256], jrev, om[u], True, True, r=[omB[u], constB], w=[bankB[bk]], sig=True)
                evac("act" if sub % 2 == 0 else "dve", ofm[sl][:, 3 - sub, :], bank(bk)[:, 0:256], r=[bankB[bk]], w=[ofmB[sl]])
            t0 = seq_off[qi]
            TR.dma("sp", of_d[t0 + k0:t0 + k0 + 512, :].rearrange("(s p) d -> p s d", p=128), ofd[sl], r=[ofdB[sl]], dsem=odds[sl])
            base = t0 + S - k0 - 511
            if kt > 0:
                TR.dma("sp", of_d[base:base + 512, :].rearrange("(s p) d -> p s d", p=128), ofm[sl], r=[ofmB[sl]], dsem=omds[sl])
            else:
                TR.dma("sp", of_d[base:base + 384, :].rearrange("(s p) d -> p s d", p=128), ofm[sl][:, 0:3, :], r=[ofmB[sl]], dsem=omds[sl])
                TR.dma("sp", of_d[base + 384:base + 511, :], ofm[sl][0:127, 3, :], r=[ofmB[sl]], dsem=omds2[sl])
                bk = sl * 4
                a = Atm[qi]
                for c in range(2):
                    for st in range(nS):
                        mm(bank(bk)[:, c:c + 1], a[:, st, c * 128:(c + 1) * 128], alt[:, 0:1], st == 0, st == nS - 1,
                           r=[AtmB[qi], constB], w=[bankB[bk]], sig=(c == 1 and st == nS - 1))
                evac("act", PH, bank(bk)[:, 0:2], r=[bankB[bk]], w=[PHB])
                for c in range(2):
                    mm(bank(bk)[0:1, 256 + c * 128:256 + (c + 1) * 128], PH[:, c:c + 1], bdc[S][:, 0:128], True, True,
                       r=[PHB, constB], w=[bankB[bk]], sig=(c == 1))
                O1 = bank(bk)[0:1, 256:512]
                act(sq1[0:1, :], O1, AF.Square, r=[bankB[bk]], w=[sq1B])
                TR.op("dve", lambda e: e.tensor_reduce(out=ss1[0:1, :], in_=sq1[0:1, :].rearrange("p (g d) -> p g d", g=4), axis=AX.X, op=ALU.add),
                      r=[sq1B], w=[ss1B])
                ts("pool", rs1[0:1, :], ss1[0:1, :], 1.0 / HD, RMS_EPS, ALU.mult, ALU.add, r=[ss1B], w=[rs1B])
                tt("pool", rs1[0:1, :], rs1[0:1, :], mhalf[0:1, 0:4], ALU.pow, r=[rs1B, constB], w=[rs1B])
                tt("dve", oh[0:1, :].rearrange("p (g d) -> p g d", g=4), O1.rearrange("p (g d) -> p g d", g=4),
                   rs1[0:1, :].unsqueeze(2).to_broadcast([1, 4, HD]), ALU.mult, r=[bankB[bk], rs1B], w=[ohB])
                TR.dma("sp", of_d[t0 + S // 2:t0 + S // 2 + 1, :], oh[0:1, :], r=[ohB], dsem=ohds)

        NJ = len(steps)
        groups(0, 0)
        groups(0, 1)
        evacs(0)
        for n in range(NJ):
            if n + 1 < NJ:
                groups(n + 1, 0)
            stage2_main(n)
            if n + 1 < NJ:
                groups(n + 1, 1)
            stage2_tail(n)
            if n + 1 < NJ:
                evacs(n + 1)
        TR.barrier()

    def phase_B(l):
        PA.reset()
        TR.dsem_next = 0
        src = xin if l == 0 else xa_d
        NCH = NSMAX // 4
        KT = PA([128, NCH, 4, 512], BF16, "KT")
        VA = PA([128, NSMAX, NH, VW], BF16, "VA")
        KVB = [Buf() for _ in range(NCH)]
        kds = [TR.new_dsem() for _ in range(NCH)]
        vds = [TR.new_dsem() for _ in range(NCH)]
        E = PA([128, NH * NMAT, 128], BF16, "E")
        EB = [Buf() for _ in range(NH)]
        eds = [TR.new_dsem() for _ in range(NH)]

        def load_E():
            for h in range(NH):
                Eh = E[:, h * NMAT:(h + 1) * NMAT, :].rearrange("p m q -> p (m q)")
                TR.dma("sp", Eh, biasmat[l, :, h * NMAT * 128:(h + 1) * NMAT * 128], w=[EB[h]], dsem=eds[h])
        Wo = PA([128, NKC, D], BF16, "Wo")
        mgt = PA([128, NKC], F32, "mgt")
        mgB = Buf()
        TR.dma("sp", mgt, mix_gT[l], w=[mgB], dsem=TR.new_dsem())
        WoG = [Buf(), Buf()]
        for g4 in range(2):
            TR.dma("pool", Wo[:, g4 * 4:(g4 + 1) * 4, :], w_out[l, g4 * 512:(g4 + 1) * 512, :].rearrange("(k p) c -> p k c", p=128),
                   w=[WoG[g4]], dsem=TR.new_swsem())
            tt("dve", Wo[:, g4 * 4:(g4 + 1) * 4, :], Wo[:, g4 * 4:(g4 + 1) * 4, :],
               mgt[:, g4 * 4:(g4 + 1) * 4].unsqueeze(2).to_broadcast([128, 4, D]), ALU.mult, r=[WoG[g4], mgB], w=[WoG[g4]])
        WoB = [WoG[kc // 4] for kc in range(NKC)]
        cw = PA([128, 2, CW], F32, "cw")
        cwB = Buf()
        TR.dma("sp", cw, conv_wT[l].rearrange("(c p) j -> p c j", p=128), w=[cwB], dsem=TR.new_dsem())
        Dg = PA([128, 2, CW, 128], BF16, "Dg")
        DgB = Buf()
        for c in range(2):
            for j in range(CW):
                ts("dve", Dg[:, c, j, :], ident, cw[:, c, j:j + 1], None, ALU.mult, None, r=[identB, cwB], w=[DgB])
        cbb = PA([128, 256], F32, "cbb")
        lgb = PA([128, 256], F32, "lgb")
        lbb = PA([128, 256], F32, "lbb")
        vecB = Buf()
        vds_ = TR.new_dsem()
        bcast_load(cbb, conv_b[l:l + 1, :], vecB, vds_)
        bcast_load(lgb, conv_ln_g[l:l + 1, :], vecB, vds_)
        bcast_load(lbb, conv_ln_b[l:l + 1, :], vecB, vds_)
        LAG = 2
        NQ = 3
        Qz = [PA([128, NH, 128], BF16, "Qz") for _ in range(NQ)]
        QzB = [Buf() for _ in range(NQ)]
        qds = [TR.new_dsem() for _ in range(NQ)]
        gTt = [PA([128, 2, 512 + 2 * GP], BF16, "gTt") for _ in range(2)]
        gTB = [Buf(), Buf()]
        gds = [TR.new_dsem(), TR.new_dsem()]
        xs = [PA([128, D], F32, "x") for _ in range(NQ)]
        xB = [Buf() for _ in range(NQ)]
        xds = [TR.new_dsem() for _ in range(NQ)]
        sds_ = [TR.new_dsem() for _ in range(NQ)]
        obf = [PA([128, D], BF16, "obf") for _ in range(NQ)]
        obfB = [Buf() for _ in range(NQ)]
        ofds = [TR.new_dsem() for _ in range(NQ)]
        ocat = [PA([128, 768], F32, "ocat") for _ in range(2)]
        ocA = [Buf(), Buf()]
        ocC = [Buf(), Buf()]
        sq = PA([128, 768], F32, "sq")
        sqB = Buf()
        ssg = PA([128, 12], F32, "ssg")
        rsg = PA([128, 12], F32, "rsg")
        ssgB, rsgB = Buf(), Buf()
        rden = PA([128, NH], F32, "rden")
        rdenB = Buf()
        oT = [PA([128, NKC, 128], BF16, "oT") for _ in range(2)]
        oTB = [[Buf(), Buf()], [Buf(), Buf()]]
        P0 = [PA([128, 640], BF16, "P0") for _ in range(2)]
        P1 = [PA([128, 640], BF16, "P1") for _ in range(2)]
        P0B = [Buf(), Buf()]
        P1B = [Buf(), Buf()]
        yb = [PA([128, 256], F32, "yb") for _ in range(2)]
        ybB = [Buf(), Buf()]
        yn = PA([128, 256], F32, "yn")
        ey = PA([128, 256], F32, "ey")
        ynB, eyB = Buf(), Buf()
        bst = PA([128, 6], F32, "bst")
        mv = PA([128, 2], F32, "mv")
        rsl = PA([128, 1], F32, "rsl")
        bstB, mvB, rslB = Buf(), Buf(), Buf()

        MB = [Buf(), Buf()]
        XB = Buf()
        OB = [Buf(), Buf()]
        YB = Buf()
        TB = Buf()
        PB7 = Buf()
        O = PS[:, 3 * 512:5 * 512].rearrange("p (h d) -> p h d", h=NH)
        Y = PS[:, 5 * 512:5 * 512 + 256]

        subs = []
        for qi, S in enumerate(seq_lens):
            for i in range(S // 128):
                subs.append((qi, i, seq_off[qi] + i * 128))
        NSUB = len(subs)

        def load_kv(qi, ch):
            tok0 = seq_off[qi] + ch * 512
            TR.dma("sp", KT[:, ch, :, :], kt_d[:, :, tok0:tok0 + 512].rearrange("c p t -> p c t"), w=[KVB[ch]], dsem=kds[ch])
            TR.dma("sp", VA[:, ch * 4:(ch + 1) * 4, :, :].rearrange("p s h d -> p s (h d)"),
                   v_d[tok0:tok0 + 512, :].rearrange("(s p) d -> p s d", p=128), w=[KVB[ch]], dsem=vds[ch])

        def FE(n):
            qi, i, tok0 = subs[n]
            sl = n % NQ
            if i % 4 == 0:
                t = i // 4
                go = gpad_off[qi] + t * 512
                TR.dma("sp", gTt[t % 2], gt_d[:, :, go:go + 512 + 2 * GP].rearrange("c p t -> p c t"), w=[gTB[t % 2]], dsem=gds[t % 2])
            TR.dma("sp", Qz[sl].rearrange("p h t -> p (h t)"), qz_d[tok0 // 128], w=[QzB[sl]], dsem=qds[sl])

        def S1(n, fill=None):
            qi, i, tok0 = subs[n]
            S = seq_lens[qi]
            npairs = S // 128
            sl = n % NQ
            u = n % 2
            js, m0 = _pair_class(i, npairs)
            nj = len(js)
            n4 = min(nj, 4)
            t = i // 4
            sub = i % 4
            g = gTt[t % 2]
            convq = [(c, j) for c in range(2) for j in range(CW)]
            cpos = [0]
            ydone = [False]

            def conv_some(cnt):
                for _ in range(cnt):
                    if cpos[0] >= len(convq):
                        return
                    c, j = convq[cpos[0]]
                    cpos[0] += 1
                    o0 = sub * 128 + j + 1
                    mm(Y[:, c * 128:(c + 1) * 128], g[:, c, o0:o0 + 128], Dg[:, c, j, :], j == 0, j == CW - 1,
                       r=[gTB[t % 2], DgB], w=[YB], sig=(c == 1 and j == CW - 1))

            def qk(h):
                c = h // 2
                hb = h % 2
                for jj, j in enumerate(js):
                    if jj < 4:
                        o = PS[:, hb * 512 + jj * 128: hb * 512 + (jj + 1) * 128]
                        wb = MB[hb]
                    else:
                        conv_some(4)
                        o = PS[:, 2 * 512 + hb * 128: 2 * 512 + (hb + 1) * 128]
                        wb = XB
                    mm(o, KT[:, j // 4, c, (j % 4) * 128:(j % 4 + 1) * 128],
                       Qz[sl][:, h, :], True, False, r=[KVB[j // 4], QzB[sl]], w=[wb], sig=False)
                    mm(o, ident, E[:, h * NMAT + m0 + jj, :], False, True, r=[identB, EB[h]], w=[wb],
                       sig=(jj == nj - 1 or jj == 3))
                act(P1[hb][:, 0:n4 * 128], PS[:, hb * 512: hb * 512 + n4 * 128], AF.Exp, r=[MB[hb]], w=[P1B[hb]])
                if nj > 4:
                    act(P1[hb][:, 512:640], PS[:, 2 * 512 + hb * 128: 2 * 512 + (hb + 1) * 128], AF.Exp, r=[XB], w=[P1B[hb]])

            def pv(h):
                hb = h % 2
                for jj, j in enumerate(js):
                    mm(O[:, h, 0:HD + 1], P1[hb][:, jj * 128:(jj + 1) * 128], VA[:, j, h, 0:HD + 1], jj == 0, jj == nj - 1,
                       r=[P1B[hb], KVB[j // 4]], w=[OB[h // 4]], sig=(jj == nj - 1))

            if fill is not None:
                for f in fill.get(-1, ()):
                    f()
            conv_some(12)
            qk(0)
            for h in range(NH):
                if h + 1 < NH:
                    qk(h + 1)
                conv_some(8)
                pv(h)
                if cpos[0] >= len(convq) and not ydone[0]:
                    ydone[0] = True
                    tt("dve", yb[u], Y, cbb, ALU.add, r=[YB, vecB], w=[ybB[u]])
                if fill is not None:
                    for f in fill.get(h, ()):
                        f()
            conv_some(len(convq))
            for hf in range(2):
                TR.op("dve", lambda e, hf=hf: e.reciprocal(out=rden[:, hf * 4:hf * 4 + 4], in_=O[:, hf * 4:hf * 4 + 4, HD]),
                      r=[OB[hf]], w=[rdenB])
                tt("dve", ocat[u][:, hf * 256:(hf + 1) * 256].rearrange("p (h d) -> p h d", h=4), O[:, hf * 4:hf * 4 + 4, 0:HD],
                   rden[:, hf * 4:hf * 4 + 4].unsqueeze(2).to_broadcast([128, 4, HD]), ALU.mult, r=[OB[hf], rdenB], w=[ocA[u]])
            if not ydone[0]:
                tt("dve", yb[u], Y, cbb, ALU.add, r=[YB, vecB], w=[ybB[u]])
            if fill is not None:
                for f in fill.get(8, ()):
                    f()
            if i % 4 == 3 and qi + 1 < len(seq_lens):
                tcur = i // 4
                nt = S // 512
                chs = [tcur - 1, tcur] if tcur == nt - 1 else [tcur - 1]
                for chn in chs:
                    if 0 <= chn < seq_lens[qi + 1] // 512:
                        load_kv(qi + 1, chn)

        def S2(n, part=None):
            qi, i, tok0 = subs[n]
            sl = n % NQ
            u = n % 2
            if part is None or part == 0:
                S2a(n, sl, u, tok0)
            if part is None or part == 1:
                S2b(n, sl, u)
            if part is None or part == 2:
                S2c(n, sl, u)

        def S2a(n, sl, u, tok0):
            TR.dma("sp", xs[sl], src[tok0:tok0 + 128, :], w=[xB[sl]], dsem=xds[sl])
            TR.dma("sp", obf[sl][:, 0:256], of_d[tok0:tok0 + 128, :], w=[obfB[sl]], dsem=ofds[sl])
            TR.op("dve", lambda e: e.bn_stats(out=bst, in_=yb[u]), r=[ybB[u]], w=[bstB])
            TR.op("dve", lambda e: e.bn_aggr(out=mv, in_=bst), r=[bstB], w=[mvB])
            rsqrt_pool(rsl, mv[:, 1:2], 1.0, LN_EPS, r=[mvB], w=[rslB])
            ts("dve", yn, yb[u], mv[:, 0:1], rsl[:, 0:1], ALU.subtract, ALU.mult, r=[ybB[u], mvB, rslB], w=[ynB])
            tt("pool", yn, yn, lgb, ALU.mult, r=[ynB, vecB], w=[ynB])
            tt("pool", yn, yn, lbb, ALU.add, r=[ynB, vecB], w=[ynB])

        def S2b(n, sl, u):
            act(ey, yn, AF.Exp, r=[ynB], w=[eyB], scale=-1.0)
            ts("pool", ey, ey, 1.0, 1.0, ALU.add, ALU.mult, r=[eyB], w=[eyB])
            TR.op("dve", lambda e: e.reciprocal(out=ey, in_=ey), r=[eyB], w=[eyB])
            tt("dve", ocat[u][:, 512:768], yn, ey, ALU.mult, r=[ynB, eyB], w=[ocC[u]])

        def S2c(n, sl, u):
            act(sq, ocat[u], AF.Square, r=[ocA[u], ocC[u]], w=[sqB])
            TR.op("dve", lambda e: e.tensor_reduce(out=ssg, in_=sq.rearrange("p (g d) -> p g d", g=12), axis=AX.X, op=ALU.add),
                  r=[sqB], w=[ssgB])
            rsqrt_pool(rsg, ssg, 1.0 / HD, RMS_EPS, r=[ssgB], w=[rsgB])
            tt("dve", obf[sl][:, 256:D].rearrange("p (g d) -> p g d", g=12), ocat[u].rearrange("p (g d) -> p g d", g=12),
               rsg.unsqueeze(2).to_broadcast([128, 12, HD]), ALU.mult, r=[ocA[u], ocC[u], rsgB], w=[obfB[sl]])

        def S3(n, part=None):
            qi, i, tok0 = subs[n]
            sl = n % NQ
            u = n % 2
            if part is None or part == 0:
                for kc in range(NKC):
                    tr(bankb(6)[:, kc * 128:(kc + 1) * 128], obf[sl][:, kc * 128:(kc + 1) * 128], r=[obfB[sl]], w=[TB], sig=(kc == NKC - 1))
                evac("act", oT[u][:, 0:4, :].rearrange("p a b -> p (a b)"), bankb(6)[:, 0:512], r=[TB], w=[oTB[u][0]])
                evac("act", oT[u][:, 4:8, :].rearrange("p a b -> p (a b)"), bankb(6)[:, 512:1024], r=[TB], w=[oTB[u][1]])
            for half in range(2):
                if part is not None and part != half + 1:
                    continue
                bk = 7 - half
                bB = PB7 if half == 0 else TB
                for kc in range(NKC):
                    mm(bank(bk), oT[u][:, kc, :], Wo[:, kc, half * 512:(half + 1) * 512], kc == 0, kc == NKC - 1,
                       r=[oTB[u][kc // 4], WoB[kc]], w=[bB], sig=(kc == NKC - 1))
                tt("dve", xs[sl][:, half * 512:(half + 1) * 512], xs[sl][:, half * 512:(half + 1) * 512], bank(bk), ALU.add,
                   r=[xB[sl], bB], w=[xB[sl]])
            if part is None or part == 2:
                TR.dma("sp", xb_d[tok0:tok0 + 128, :], xs[sl], r=[xB[sl]], dsem=sds_[sl])

        load_kv(0, 0)
        load_E()
        for ch in range(1, seq_lens[0] // 512):
            load_kv(0, ch)
        done_extra = set()

        def extra_loads(qi):
            if qi + 1 < len(seq_lens) and qi not in done_extra:
                done_extra.add(qi)
                for chn in range(seq_lens[qi] // 512, seq_lens[qi + 1] // 512):
                    load_kv(qi + 1, chn)

        FE(0)
        if NSUB > 1:
            FE(1)
        S1(0)
        for k in range(NSUB + 1):
            if k + 2 < NSUB:
                FE(k + 2)
            if k < NSUB and subs[k][1] == 0:
                extra_loads(subs[k][0])
            n2 = k
            n3 = k - 1
            v2 = 0 <= n2 < NSUB
            v3 = 0 <= n3 < NSUB
            if k + 1 < NSUB:
                hooks = {}
                if v2:
                    hooks.setdefault(-1, []).append(lambda n=n2: S2(n, 0))
                    hooks.setdefault(1, []).append(lambda n=n2: S2(n, 1))
                    hooks.setdefault(3, []).append(lambda n=n2: S2(n, 2))
                if v3:
                    hooks.setdefault(2, []).append(lambda n=n3: S3(n, 0))
                    hooks.setdefault(4, []).append(lambda n=n3: S3(n, 1))
                    hooks.setdefault(8, []).append(lambda n=n3: S3(n, 2))
                S1(k + 1, hooks)
            else:
                if v3:
                    S3(n3)
                if v2:
                    S2(n2)
        TR.barrier()

    def phase_C(l):
        PA.reset()
        TR.dsem_next = 0
        last = (l == L - 1)
        dst = y if last else xa_d
        W1 = PA([128, NKC, DFF], BF16, "W1")
        W2 = PA([128, 32, D], BF16, "W2")
        g2t = PA([128, NKC], F32, "g2t")
        g2B = Buf()
        TR.dma("sp", g2t, norm2_gT[l], w=[g2B], dsem=TR.new_dsem())
        w1cb = load_fold_cols(W1, w_ff_in[l], [(b * 512, (b + 1) * 512) for b in range(8)], g2t, g2B)
        W2G = [Buf() for _ in range(8)]
        for g4 in range(8):
            TR.dma("pool", W2[:, g4 * 4:(g4 + 1) * 4, :], w_ff_out[l, g4 * 512:(g4 + 1) * 512, :].rearrange("(k p) c -> p k c", p=128),
                   w=[W2G[g4]], dsem=TR.new_swsem())
        W2B = [W2G[kc // 4] for kc in range(32)]
        if last:
            gfb = PA([128, D], F32, "gfb")
            gfB = Buf()
            bcast_load(gfb, final_norm_g[0:1, :], gfB, TR.new_dsem())
        xs = [PA([128, D], F32, "x") for _ in range(2)]
        xB = [Buf(), Buf()]
        xds = [TR.new_dsem(), TR.new_dsem()]
        hs = [PA([128, D], BF16, "h") for _ in range(4)]
        hB = [Buf() for _ in range(4)]
        ss = PA([128, 4], F32, "ss")
        rs = PA([128, 4], F32, "rs")
        ssB = [Buf() for _ in range(4)]
        rsB = [Buf() for _ in range(4)]
        hT = PA([128, NKC, 512], BF16, "hT")
        hTB = [Buf() for _ in range(4)]
        fT = PA([128, 32, 512], BF16, "fT")
        fTB = [Buf() for _ in range(32)]
        rt = [PA([128, 512], F32, "rt") for _ in range(2)]
        rtB = [Buf(), Buf()]
        xr = [PA([128, D], F32, "xr") for _ in range(2)]
        xrB = [Buf(), Buf()]
        xrds = [TR.new_dsem(), TR.new_dsem()]
        ods = [TR.new_dsem(), TR.new_dsem()]
        s2 = PA([128, 2], F32, "s2")
        r2 = PA([128, 2], F32, "r2")
        s2B = [Buf(), Buf()]
        r2B = [Buf(), Buf()]
        jk = PA([128, D], BF16, "jk")
        jkB = Buf()
        bctr = [0]
        xctr = [0]

        def FE_el(k):
            qi, t, tok0 = tiles[k]
            for s in range(4):
                xi = xctr[0] % 2
                xctr[0] += 1
                TR.dma("sp", xs[xi], xb_d[tok0 + s * 128: tok0 + (s + 1) * 128, :], w=[xB[xi]], dsem=xds[xi])
                act(hs[s], xs[xi], AF.Square, r=[xB[xi]], w=[hB[s], ssB[s]], accum=ss[:, s:s + 1])
                rsqrt_pool(rs[:, s:s + 1], ss[:, s:s + 1], 1.0 / D, RMS_EPS, r=[ssB[s]], w=[rsB[s]])
                ts("dve", hs[s], xs[xi], rs[:, s:s + 1], None, ALU.mult, None, r=[xB[xi], rsB[s]], w=[hB[s]])

        def FE_pe(k):
            for pr in range(4):
                bk = 6 + (pr % 2)
                for kk in range(2):
                    kc = pr * 2 + kk
                    for s in range(4):
                        tr(bankb(bk)[:, kk * 512 + s * 128: kk * 512 + (s + 1) * 128], hs[s][:, kc * 128:(kc + 1) * 128],
                           r=[hB[s]], w=[bankB[bk]], sig=(kk == 1 and s == 3))
                evac(evq(), hT[:, pr * 2:pr * 2 + 2, :].rearrange("p a b -> p (a b)"), bankb(bk), r=[bankB[bk]], w=[hTB[pr]])

        def BE1(k, hook):
            for fc in range(32):
                if fc == 10 and hook is not None:
                    hook()
                bk = bctr[0] % 6
                bctr[0] += 1
                for kc in range(NKC):
                    mm(bank(bk), W1[:, kc, fc * 128:(fc + 1) * 128], hT[:, kc, :], kc == 0, kc == NKC - 1,
                       r=[w1cb[((fc // 4) * 512, (fc // 4) * 512 + 512)], hTB[kc // 2]], w=[bankB[bk]], sig=(kc == NKC - 1))
                u = fc % 2
                act(rt[u], bank(bk), AF.Relu, r=[bankB[bk]], w=[rtB[u]])
                tt("dve", fT[:, fc, :], rt[u], rt[u], ALU.mult, r=[rtB[u]], w=[fTB[fc]])

        def BE2(k):
            qi, t, tok0 = tiles[k]
            for s in range(4):
                u = s % 2
                TR.dma("sp", xr[u], xb_d[tok0 + s * 128: tok0 + (s + 1) * 128, :], w=[xrB[u]], dsem=xrds[u])
                for half in range(2):
                    bk = bctr[0] % 6
                    bctr[0] += 1
                    for kc in range(32):
                        mm(bank(bk), fT[:, kc, s * 128:(s + 1) * 128], W2[:, kc, half * 512:(half + 1) * 512], kc == 0, kc == 31,
                           r=[fTB[kc], W2B[kc]], w=[bankB[bk]], sig=(kc == 31))
                    tt("dve", xr[u][:, half * 512:(half + 1) * 512], xr[u][:, half * 512:(half + 1) * 512], bank(bk), ALU.add,
                       r=[xrB[u], bankB[bk]], w=[xrB[u]])
                if last:
                    act(jk, xr[u], AF.Square, r=[xrB[u]], w=[jkB, s2B[u]], accum=s2[:, u:u + 1])
                    rsqrt_pool(r2[:, u:u + 1], s2[:, u:u + 1], 1.0 / D, RMS_EPS, r=[s2B[u]], w=[r2B[u]])
                    stt(xr[u], xr[u], r2[:, u:u + 1], gfb, ALU.mult, ALU.mult, r=[xrB[u], r2B[u], gfB], w=[xrB[u]])
                TR.dma("sp", dst[tok0 + s * 128: tok0 + (s + 1) * 128, :], xr[u], r=[xrB[u]], dsem=ods[u])

        FE_el(0)
        FE_pe(0)
        for k in range(NT):
            BE1(k, (lambda kk=k: FE_el(kk + 1)) if k + 1 < NT else None)
            if k + 1 < NT:
                FE_pe(k + 1)
            BE2(k)
        TR.barrier()

    import os as _os
    _ph = _os.environ.get("MK_PHASES", "AFBC")
    for l in range(L):
        if "A" in _ph:
            phase_A(l)
        if "F" in _ph:
            phase_F(l)
        if "B" in _ph:
            phase_B(l)
        if "C" in _ph:
            phase_C(l)
    TR.replay()
    return nc, TR, PA


_CACHE = {}


def _const_inputs(seq_lens):
    key = tuple(sorted(set(seq_lens)))
    if key not in _CACHE:
        d = {"ident": np.eye(128, dtype=np.float32).astype(ml_dtypes.bfloat16),
             "jrev": np.eye(128, dtype=np.float32)[::-1].copy().astype(ml_dtypes.bfloat16),
             "alt": np.stack([(-1.0) ** np.arange(128)] * 2, axis=1).astype(ml_dtypes.bfloat16)}
        for S in key:
            c, s, bc, bs = _dft_tables(S)
            d["cos%d" % S] = c
            d["sin%d" % S] = s
            d["bdc%d" % S] = bc
            d["bds%d" % S] = bs
        _CACHE[key] = d
    return _CACHE[key]


def make_weight_inputs(depth, norm1_g, w_in, rpb, conv_w, conv_b, conv_ln_g, conv_ln_b, mix_norm_g, w_out,
                       norm2_g, w_ff_in, w_ff_out, final_norm_g):
    f = lambda a: np.ascontiguousarray(np.asarray(a, dtype=np.float32))
    gT = lambda a: np.ascontiguousarray(f(a).reshape(depth, NKC, 128).transpose(0, 2, 1))
    DR, DC, VAL = _bias_index_tables()
    rpb = f(rpb)
    g = rpb[:, :, DR, DC]
    g = np.where(VAL[None, None], g, np.float32(NEG))
    g = np.ascontiguousarray(g.transpose(0, 3, 1, 2, 4)).reshape(depth, 128, NH * NMAT * 128)
    return {
        "norm1_gT": gT(norm1_g), "w_in": f(w_in), "biasmat": g.astype(ml_dtypes.bfloat16),
        "conv_wT": np.ascontiguousarray(f(conv_w).transpose(0, 2, 1)), "conv_b": f(conv_b),
        "conv_ln_g": f(conv_ln_g), "conv_ln_b": f(conv_ln_b), "mix_gT": gT(mix_norm_g), "w_out": f(w_out),
        "norm2_gT": gT(norm2_g), "w_ff_in": f(w_ff_in), "w_ff_out": f(w_ff_out),
        "final_norm_g": f(final_norm_g).reshape(1, D),
    }


_PROG = {}


def get_program(seq_lens, depth):
    key = (tuple(seq_lens), depth)
    if key not in _PROG:
        _PROG[key] = build_program(list(seq_lens), depth)[0]
    return _PROG[key]


def kernel(x_prompt, x_sample, norm1_g, w_in, rpb, conv_w, conv_b, conv_ln_g, conv_ln_b, mix_norm_g, w_out,
           norm2_g, w_ff_in, w_ff_out, final_norm_g):
    x_prompt = np.asarray(x_prompt, dtype=np.float32)
    x_sample = np.asarray(x_sample, dtype=np.float32)
    n = 8
    depth = int(np.asarray(w_in).shape[0])
    BP, SP_, _ = x_prompt.shape
    BS, SS, _ = x_sample.shape
    pp = BP // n
    sp = BS // n
    seq_lens = [SP_] * pp + [SS] * sp
    nc = get_program(seq_lens, depth)
    wts = make_weight_inputs(depth, norm1_g, w_in, rpb, conv_w, conv_b, conv_ln_g, conv_ln_b, mix_norm_g, w_out,
                             norm2_g, w_ff_in, w_ff_out, final_norm_g)
    consts = _const_inputs(seq_lens)
    in_maps = []
    for c in range(n):
        xin = np.concatenate([x_prompt[c * pp:(c + 1) * pp].reshape(pp * SP_, D),
                              x_sample[c * sp:(c + 1) * sp].reshape(sp * SS, D)], axis=0)
        m = {"xin": np.ascontiguousarray(xin)}
        m.update(wts)
        m.update(consts)
        in_maps.append(m)
    res = run_bass_kernel_spmd(nc, in_maps, core_ids=list(range(n)))
    yp = np.empty_like(x_prompt)
    ys = np.empty_like(x_sample)
    for c in range(n):
        yc = np.asarray(res.results[c]["y"], dtype=np.float32)
        yp[c * pp:(c + 1) * pp] = yc[:pp * SP_].reshape(pp, SP_, D)
        ys[c * sp:(c + 1) * sp] = yc[pp * SP_:].reshape(sp, SS, D)
    return (yp, ys)
```

```python
import numpy as np
import ml_dtypes
import concourse.bass as bass
import concourse.mybir as mybir
from concourse.bass_utils import run_bass_kernel_spmd

F32 = mybir.dt.float32
BF16 = mybir.dt.bfloat16
AF = mybir.ActivationFunctionType
ALU = mybir.AluOpType
AX = mybir.AxisListType

D = 1024
NKC = 8
DIN = 2304
DFF = 4096
NH = 8
HD = 64
CW = 31
RMS_EPS = 1e-6
LN_EPS = 1e-5
NEG = -30000.0
NMAT = 21
VW = 72
GP = 16


class Sem:
    def __init__(self, nc, name):
        self.h = nc.alloc_semaphore(name)
        self.v = 0


class Buf:
    __slots__ = ("name", "w", "r")

    def __init__(self, name=""):
        self.name = name
        self.w = {}
        self.r = {}


class Q:
    def __init__(self, name, sem):
        self.name = name
        self.sem = sem
        self.seen = {}
        self.prog = []
        self.pending = False


class Tracker:
    def __init__(self, nc):
        self.nc = nc
        self.q = {}
        for n in ("pe", "act", "dve", "pool", "sp"):
            self.q[n] = Q(n, Sem(nc, "q_" + n))
        self.bar = Sem(nc, "bar")
        self.dsems = []
        self.dsem_next = 0
        self.swsems = []
        self.swsem_next = 0
        self.ninst = 0

    def new_dsem(self):
        if self.dsem_next == len(self.dsems):
            self.dsems.append(Sem(self.nc, "d%d" % len(self.dsems)))
        s = self.dsems[self.dsem_next]
        self.dsem_next += 1
        return s

    def new_swsem(self):
        if self.swsem_next == len(self.swsems):
            self.swsems.append(Sem(self.nc, "w%d" % len(self.swsems)))
        s = self.swsems[self.swsem_next]
        self.swsem_next += 1
        return s

    def _waits(self, q, needs):
        for s, v in needs.items():
            if v > 0 and q.seen.get(s, 0) < v:
                q.prog.append(("wait", s, v))
                q.seen[s] = v

    def op(self, qn, fn, r=(), w=(), sig=True):
        q = self.q[qn]
        needs = {}
        is_pe = qn == "pe"
        for b in r:
            for s, v in b.w.items():
                if s is q.sem and is_pe:
                    continue
                if needs.get(s, 0) < v:
                    needs[s] = v
        for b in w:
            for s, v in b.w.items():
                if s is q.sem:
                    continue
                if needs.get(s, 0) < v:
                    needs[s] = v
            for s, v in b.r.items():
                if s is q.sem:
                    continue
                if needs.get(s, 0) < v:
                    needs[s] = v
        self._waits(q, needs)
        if sig:
            q.sem.v += 1
            tv = q.sem.v
            q.prog.append(("inst", fn, q.sem, 1))
            q.pending = False
        else:
            assert is_pe
            tv = q.sem.v + 1
            q.prog.append(("inst", fn, None, 0))
            q.pending = True
        s = q.sem
        for b in r:
            if b.r.get(s, 0) < tv:
                b.r[s] = tv
        for b in w:
            if b.w.get(s, 0) < tv:
                b.w[s] = tv
        self.ninst += 1

    def dma(self, qn, out, in_, r=(), w=(), dsem=None, **kw):
        q = self.q[qn]
        needs = {}
        for b in r:
            for s, v in b.w.items():
                if needs.get(s, 0) < v:
                    needs[s] = v
        for b in w:
            for s, v in b.w.items():
                if needs.get(s, 0) < v:
                    needs[s] = v
            for s, v in b.r.items():
                if needs.get(s, 0) < v:
                    needs[s] = v
        if needs.get(dsem, 0) < dsem.v:
            needs[dsem] = dsem.v
        self._waits(q, needs)
        dsem.v += 16
        tv = dsem.v

        def fn(e, out=out, in_=in_, kw=kw):
            return e.dma_start(out=out, in_=in_, **kw)
        q.prog.append(("inst", fn, dsem, 16))
        for b in r:
            if b.r.get(dsem, 0) < tv:
                b.r[dsem] = tv
        for b in w:
            if b.w.get(dsem, 0) < tv:
                b.w[dsem] = tv
        self.ninst += 1

    def barrier(self):
        sp = self.q["sp"]
        needs = {}
        for n, q in self.q.items():
            assert not q.pending, n
            if n != "sp":
                needs[q.sem] = q.sem.v
        for s in self.dsems + self.swsems:
            needs[s] = s.v
        self._waits(sp, needs)
        self.bar.v += 1
        sp.prog.append(("seminc", self.bar, 1))
        for n, q in self.q.items():
            if n != "sp":
                q.prog.append(("wait", self.bar, self.bar.v))
            for s, v in needs.items():
                q.seen[s] = max(q.seen.get(s, 0), v)
        self.dsem_next = 0
        self.swsem_next = 0

    def replay(self):
        nc = self.nc
        progs = self.q

        def run(e, prog):
            for it in prog:
                if it[0] == "wait":
                    e.wait_ge(it[1].h, it[2])
                elif it[0] == "inst":
                    ins = it[1](e)
                    if it[2] is not None:
                        ins.then_inc(it[2].h, it[3])
                else:
                    e.sem_inc(it[1].h, it[2])

        with nc.Block() as block:
            @block.sync
            def _(e):
                run(e, progs["sp"].prog)

            @block.tensor
            def _(e):
                run(e, progs["pe"].prog)

            @block.scalar
            def _(e):
                run(e, progs["act"].prog)

            @block.vector
            def _(e):
                run(e, progs["dve"].prog)

            @block.gpsimd
            def _(e):
                run(e, progs["pool"].prog)


class Alloc:
    def __init__(self, nc, base, limit):
        self.nc = nc
        self.base = base
        self.off = base
        self.limit = limit
        self.cnt = 0
        self.peak = base

    def reset(self):
        self.off = self.base

    def __call__(self, shape, dtype, name="t"):
        nbytes = int(np.prod(shape[1:])) * (4 if dtype == F32 else 2)
        nbytes = (nbytes + 63) // 64 * 64
        assert self.off + nbytes <= self.limit, ("SBUF overflow", name, self.off, nbytes, self.limit)
        self.cnt += 1
        h = self.nc.alloc_sbuf_tensor_at("%s_%d" % (name, self.cnt), list(shape), dtype, offset=self.off)
        self.off += nbytes
        self.peak = max(self.peak, self.off)
        return h.ap()


def _bias_index_tables():
    R = 16
    cls = [(0, [0, 1, 2, 3]), (1, [-1, 0, 1, 2]), (3, [-2, -1, 0, 1, 2]),
           (R // 2 - 2, [-2, -1, 0, 1]), (R // 2 - 1, [-3, -2, -1, 0])]
    DR = np.zeros((NMAT, 128, 128), np.int64)
    DC = np.zeros((NMAT, 128, 128), np.int64)
    VAL = np.zeros((NMAT, 128, 128), bool)
    kk = np.arange(128)
    krl, kc = kk // 64, kk % 64
    qrl, qc = kk // 64, kk % 64
    m = 0
    for (i, deltas) in cls:
        for dl in deltas:
            j = i + dl
            r = (2 * i + qrl)[None, :]
            kr = (2 * j + krl)[:, None]
            rs = np.clip(r - 4, 0, R - 8)
            vrow = (kr >= rs) & (kr < rs + 8)
            cs = np.clip(qc - 8, 0, 48)[None, :]
            vcol = (kc[:, None] >= cs) & (kc[:, None] < cs + 16)
            DR[m] = np.clip(kr - r + 7, 0, 14)
            DC[m] = np.clip(kc[:, None] - qc[None, :] + 15, 0, 30)
            VAL[m] = vrow & vcol
            m += 1
    assert m == NMAT
    return DR, DC, VAL


def _pair_class(i, npairs):
    if i == 0:
        return [0, 1, 2, 3], 0
    if i == 1:
        return [0, 1, 2, 3], 4
    if i == npairs - 2:
        return [i - 2, i - 1, i, i + 1], 13
    if i == npairs - 1:
        return [i - 3, i - 2, i - 1, i], 17
    return [i - 2, i - 1, i, i + 1, i + 2], 8


def _dft_tables(S):
    nS, nK = S // 128, S // 512
    idx = (np.arange(S, dtype=np.int64)[:, None] * np.arange(S, dtype=np.int64)[None, :]) % S
    ang = 2.0 * np.pi * idx.astype(np.float64) / S
    out = []
    for tab in (np.cos(ang), np.sin(ang)):
        t = tab.reshape(nS, 128, nK, 512).transpose(2, 1, 0, 3).reshape(nK, 128, nS * 512)[:nK // 2]
        out.append(np.ascontiguousarray(t).astype(ml_dtypes.bfloat16))
    a64 = 2.0 * np.pi * ((np.arange(64)[:, None] * np.arange(64)[None, :]) % 64) / 64.0
    sc = 1.0 / np.sqrt(S * 64.0)
    bdc = np.zeros((128, 128))
    bds = np.zeros((128, 128))
    for g in range(2):
        bdc[g * 64:(g + 1) * 64, g * 64:(g + 1) * 64] = np.cos(a64) * sc
        bds[g * 64:(g + 1) * 64, g * 64:(g + 1) * 64] = -np.sin(a64) * sc
    bdc2 = np.concatenate([bdc, bdc], axis=1)
    bds2 = np.concatenate([bds, -bds], axis=1)
    return out[0], out[1], bdc2.astype(ml_dtypes.bfloat16), bds2.astype(ml_dtypes.bfloat16)


def build_program(seq_lens, depth):
    nc = bass.Bass("TRN2", target_bir_lowering=False)
    L = depth
    T = sum(seq_lens)
    seq_off = [sum(seq_lens[:i]) for i in range(len(seq_lens))]
    gpad_off = [sum(s + 2 * GP for s in seq_lens[:i]) for i in range(len(seq_lens))]
    TP = sum(s + 2 * GP for s in seq_lens)
    Sset = sorted(set(seq_lens))
    SMAX = max(seq_lens)
    NSMAX = SMAX // 128

    def din(name, shape, dt=F32):
        return nc.dram_tensor(name, list(shape), dt, kind="ExternalInput").ap()

    def dtmp(name, shape, dt):
        return nc.dram_tensor(name, list(shape), dt).ap()

    xin = din("xin", [T, D])
    norm1_gT = din("norm1_gT", [L, 128, NKC])
    w_in = din("w_in", [L, D, DIN])
    biasmat = din("biasmat", [L, 128, NH * NMAT * 128], BF16)
    conv_wT = din("conv_wT", [L, 256, CW])
    conv_b = din("conv_b", [L, 256])
    conv_ln_g = din("conv_ln_g", [L, 256])
    conv_ln_b = din("conv_ln_b", [L, 256])
    mix_gT = din("mix_gT", [L, 128, NKC])
    w_out = din("w_out", [L, D, D])
    norm2_gT = din("norm2_gT", [L, 128, NKC])
    w_ff_in = din("w_ff_in", [L, D, DFF])
    w_ff_out = din("w_ff_out", [L, DFF, D])
    final_norm_g = din("final_norm_g", [1, D])
    ident_d = din("ident", [128, 128], BF16)
    cos_d, sin_d, bdc_d, bds_d = {}, {}, {}, {}
    for S in Sset:
        cos_d[S] = din("cos%d" % S, [S // 1024, 128, (S // 128) * 512], BF16)
        sin_d[S] = din("sin%d" % S, [S // 1024, 128, (S // 128) * 512], BF16)
        bdc_d[S] = din("bdc%d" % S, [128, 256], BF16)
        bds_d[S] = din("bds%d" % S, [128, 256], BF16)
    jrev_d = din("jrev", [128, 128], BF16)
    alt_d = din("alt", [128, 2], BF16)
    y = nc.dram_tensor("y", [T, D], F32, kind="ExternalOutput").ap()

    xa_d = dtmp("xa_d", [T, D], F32)
    xb_d = dtmp("xb_d", [T, D], F32)
    qz_d = dtmp("qz_d", [T // 128, 128, NH * 128], BF16)
    kt_d = dtmp("kt_d", [4, 128, T], BF16)
    v_d = dtmp("v_d", [T, NH * VW], BF16)
    a_d = dtmp("a_d", [T, 256], BF16)
    gt_d = dtmp("gt_d", [2, 128, TP], BF16)
    of_d = dtmp("of_d", [T, 256], BF16)

    TR = Tracker(nc)
    PS = nc.alloc_psum_tensor("ps", [128, 4096], F32).ap()
    PSB = PS.bitcast(BF16)
    bankB = [Buf("bank%d" % i) for i in range(8)]

    def bank(i):
        return PS[:, i * 512:(i + 1) * 512]

    def bankb(i):
        return PSB[:, i * 1024:(i + 1) * 1024]

    SB_BASE = 16512
    SB_LIMIT = 229344
    CA = Alloc(nc, SB_BASE, SB_BASE + 8192)
    ident = CA([128, 128], BF16, "ident")
    identB = Buf("ident")
    mhalf = CA([128, 16], F32, "mhalf")
    zpad = CA([128, 2, GP], BF16, "zpad")
    epsc = CA([128, 2], F32, "epsc")
    constB = Buf("const")
    bdc, bds = {}, {}
    for S in Sset:
        bdc[S] = CA([128, 256], BF16, "bdc")
        bds[S] = CA([128, 256], BF16, "bds")
    jrev = CA([128, 128], BF16, "jrev")
    alt = CA([128, 2], BF16, "alt")
    PA = Alloc(nc, CA.off, SB_LIMIT)

    def mm(out, lhsT, rhs, start, stop, r, w, sig):
        TR.op("pe", lambda e, o=out, a=lhsT, b=rhs, s0=start, s1=stop: e.matmul(o, lhsT=a, rhs=b, start=s0, stop=s1),
              r=r, w=w, sig=sig)

    def tr(out, in_, r, w, sig):
        TR.op("pe", lambda e, o=out, a=in_: e.transpose(o, a, ident), r=list(r) + [identB], w=w, sig=sig)

    def act(out, in_, func, r, w, scale=1.0, accum=None):
        if accum is None:
            TR.op("act", lambda e, o=out, i=in_, f=func, s=scale: e.activation(out=o, in_=i, func=f, scale=s), r=r, w=w)
        else:
            TR.op("act", lambda e, o=out, i=in_, f=func, s=scale, a=accum: e.activation(out=o, in_=i, func=f, scale=s, accum_out=a),
                  r=r, w=w)

    def vcopy(qn, out, in_, r, w):
        TR.op(qn, lambda e, o=out, i=in_: e.tensor_copy(out=o, in_=i), r=r, w=w)

    def tt(qn, out, in0, in1, op, r, w):
        TR.op(qn, lambda e, o=out, a=in0, b=in1, p=op: e.tensor_tensor(out=o, in0=a, in1=b, op=p), r=r, w=w)

    def ts(qn, out, in0, s1, s2, op0, op1, r, w):
        if s2 is None:
            TR.op(qn, lambda e, o=out, a=in0, x=s1, p=op0: e.tensor_scalar(out=o, in0=a, scalar1=x, scalar2=None, op0=p), r=r, w=w)
        else:
            TR.op(qn, lambda e, o=out, a=in0, x=s1, y=s2, p=op0, p1=op1: e.tensor_scalar(out=o, in0=a, scalar1=x, scalar2=y, op0=p, op1=p1),
                  r=r, w=w)

    def stt(out, in0, scalar, in1, op0, op1, r, w):
        TR.op("dve", lambda e, o=out, a=in0, s=scalar, b=in1, p0=op0, p1=op1:
              e.scalar_tensor_tensor(out=o, in0=a, scalar=s, in1=b, op0=p0, op1=p1), r=r, w=w)

    def rsqrt_pool(out, in_, mul, add, r, w):
        n = out.shape[1]
        ts("pool", out, in_, mul, add, ALU.mult, ALU.add, r=r, w=w)
        tt("pool", out, out, mhalf[:, 0:n], ALU.pow, r=list(w) + [constB], w=w)

    def bcast_load(dst, src_row, buf, dsem):
        TR.dma("sp", dst, src_row.partition_broadcast(128), w=[buf], dsem=dsem)

    ebal = [0]

    def evq():
        ebal[0] += 1
        return "act" if ebal[0] % 2 else "dve"

    def evac(qn, out, in_, r, w, scale=None):
        if qn == "act":
            act(out, in_, AF.Copy, r, w, scale=1.0 if scale is None else scale)
        elif scale is None:
            vcopy(qn, out, in_, r, w)
        else:
            ts(qn, out, in_, scale, None, ALU.mult, None, r, w)

    cds = TR.new_dsem()
    TR.dma("sp", ident, ident_d, w=[identB], dsem=cds)
    for S in Sset:
        TR.dma("sp", bdc[S], bdc_d[S], w=[constB], dsem=cds)
        TR.dma("sp", bds[S], bds_d[S], w=[constB], dsem=cds)
    TR.dma("sp", jrev, jrev_d, w=[constB], dsem=cds)
    TR.dma("sp", alt, alt_d, w=[constB], dsem=cds)
    TR.op("pool", lambda e: e.memset(mhalf, -0.5), w=[constB])
    TR.op("pool", lambda e: e.memset(zpad, 0.0), w=[constB])
    TR.op("pool", lambda e: e.memset(epsc, RMS_EPS), w=[constB])
    for qi, S in enumerate(seq_lens):
        for side in range(2):
            o = gpad_off[qi] + (0 if side == 0 else GP + S)
            TR.dma("sp", gt_d[:, :, o:o + GP].rearrange("c p t -> p c t"), zpad, r=[constB], dsem=cds)
    TR.barrier()

    tiles = []
    for qi, S in enumerate(seq_lens):
        for t in range(S // 512):
            tiles.append((qi, t, seq_off[qi] + t * 512))
    NT = len(tiles)

    def load_fold_weight(W, WBs, src_rows, gt, gB, dsem, nsplit, width):
        for kc in range(len(WBs)):
            if nsplit > 1:
                TR.dma("pool", W[:, kc, :].rearrange("p (a b) -> p a b", a=nsplit),
                       src_rows(kc).rearrange("p (a b) -> p a b", a=nsplit), w=[WBs[kc]], dsem=dsem)
            else:
                TR.dma("pool", W[:, kc, :], src_rows(kc), w=[WBs[kc]], dsem=dsem)
            if gt is not None:
                ts("dve", W[:, kc, :], W[:, kc, :], gt[:, kc:kc + 1], None, ALU.mult, None, r=[WBs[kc], gB], w=[WBs[kc]])

    def load_fold_cols(W, src2d, blocks, gt, gB, dsem=None):
        bufs = {}
        for (c0, c1) in blocks:
            b = Buf()
            bufs[(c0, c1)] = b
            TR.dma("pool", W[:, :, c0:c1], src2d[:, c0:c1].rearrange("(k p) c -> p k c", p=128), w=[b], dsem=TR.new_swsem())
            tt("dve", W[:, :, c0:c1], W[:, :, c0:c1], gt.unsqueeze(2).to_broadcast([128, NKC, c1 - c0]), ALU.mult,
               r=[b, gB], w=[b])
        return bufs

    def phase_A(l):
        PA.reset()
        TR.dsem_next = 0
        src = xin if l == 0 else xa_d
        Win = PA([128, NKC, DIN], BF16, "Win")
        g1t = PA([128, NKC], F32, "g1t")
        g1B = Buf()
        TR.dma("sp", g1t, norm1_gT[l], w=[g1B], dsem=TR.new_dsem())
        order = [1, 2, 3, 4, 8, 7, 5, 6, 0]
        wcb = {}
        WC = lambda col: wcb[((col // 256) * 256, (col // 256) * 256 + 256)]
        xs = [PA([128, D], F32, "x") for _ in range(8)]
        xB = [Buf() for _ in range(8)]
        xds = [TR.new_dsem() for _ in range(8)]
        hs = [PA([128, D], BF16, "h") for _ in range(8)]
        hB = [Buf() for _ in range(8)]
        ss = PA([128, 8], F32, "ss")
        rs = PA([128, 8], F32, "rs")
        ssB = [Buf(), Buf()]
        rsB = [Buf(), Buf()]
        hT = [PA([128, NKC, 512], BF16, "hT") for _ in range(2)]
        hTB = [[Buf() for _ in range(4)] for _ in range(2)]
        Qz = [PA([128, NH, 512], BF16, "Qz") for _ in range(2)]
        Kst = [PA([128, 4, 512], BF16, "Kst") for _ in range(2)]
        Gst = [PA([128, 2, 512], BF16, "Gst") for _ in range(2)]
        Vst = [PA([128, 4, NH, VW], BF16, "Vst") for _ in range(2)]
        Ast = [PA([128, 4, 256], BF16, "Ast") for _ in range(2)]
        QzB, KstB, GstB, VstB, AstB = ([Buf(), Buf()] for _ in range(5))
        stds = [[TR.new_dsem() for _ in range(5)] for _ in range(2)]
        et = [PA([128, 512], F32, "et") for _ in range(2)]
        etB = [Buf(), Buf()]
        for sl in range(2):
            TR.op("pool", lambda e, a=Qz[sl]: e.memset(a, 0.0), w=[QzB[sl]])
            TR.op("pool", lambda e, a=Vst[sl]: e.memset(a, 1.0), w=[VstB[sl]])
        bctr = [0]

        def nbank():
            b = bctr[0] % 6
            bctr[0] += 1
            return b

        def FE_el(k):
            qi, t, tok0 = tiles[k]
            sl = k % 2
            for s in range(4):
                i = sl * 4 + s
                TR.dma("sp", xs[i], src[tok0 + s * 128: tok0 + (s + 1) * 128, :], w=[xB[i]], dsem=xds[i])
            for s in range(4):
                i = sl * 4 + s
                act(hs[i], xs[i], AF.Square, r=[xB[i]], w=[hB[i], ssB[sl]], accum=ss[:, i:i + 1])
            rsqrt_pool(rs[:, sl * 4:sl * 4 + 4], ss[:, sl * 4:sl * 4 + 4], 1.0 / D, RMS_EPS, r=[ssB[sl]], w=[rsB[sl]])
            for s in range(4):
                i = sl * 4 + s
                ts("dve", hs[i], xs[i], rs[:, i:i + 1], None, ALU.mult, None, r=[xB[i], rsB[sl]], w=[hB[i]])

        def FE_pe(k):
            sl = k % 2
            for pr in range(4):
                bk = 6 + (pr % 2)
                for kk in range(2):
                    kc = pr * 2 + kk
                    for s in range(4):
                        i = sl * 4 + s
                        tr(bankb(bk)[:, kk * 512 + s * 128: kk * 512 + (s + 1) * 128], hs[i][:, kc * 128:(kc + 1) * 128],
                           r=[hB[i]], w=[bankB[bk]], sig=(kk == 1 and s == 3))
                evac(evq(), hT[sl][:, pr * 2:pr * 2 + 2, :].rearrange("p a b -> p (a b)"), bankb(bk), r=[bankB[bk]], w=[hTB[sl][pr]])

        def fm_group(sl, col):
            bk = nbank()
            for kc in range(NKC):
                mm(bank(bk), Win[:, kc, col:col + 128], hT[sl][:, kc, :], kc == 0, kc == NKC - 1,
                   r=[WC(col), hTB[sl][kc // 2]], w=[bankB[bk]], sig=(kc == NKC - 1))
            return bk

        def BE(k, hook):
            qi, t, tok0 = tiles[k]
            sl = k % 2
            for c in range(4):
                bk = fm_group(sl, 256 + c * 128)
                act(Qz[sl][0:64, 2 * c, :], bank(bk)[0:64, :], AF.Copy, r=[bankB[bk]], w=[QzB[sl]], scale=0.125)
                act(Qz[sl][64:128, 2 * c + 1, :], bank(bk)[64:128, :], AF.Copy, r=[bankB[bk]], w=[QzB[sl]], scale=0.125)
            for c in range(4):
                bk = fm_group(sl, 768 + c * 128)
                vcopy("dve", Kst[sl][:, c, :], bank(bk), r=[bankB[bk]], w=[KstB[sl]])
            if hook is not None:
                hook()
            for c in range(2):
                bk = fm_group(sl, 2048 + c * 128)
                act(et[c], bank(bk), AF.Exp, r=[bankB[bk]], w=[etB[c]], scale=-1.0)
                ts("pool", et[c], et[c], 1.0, 1.0, ALU.add, ALU.mult, r=[etB[c]], w=[etB[c]])
                TR.op("dve", lambda e, a=et[c]: e.reciprocal(out=a, in_=a), r=[etB[c]], w=[etB[c]])
                bk = fm_group(sl, 1792 + c * 128)
                tt("dve", Gst[sl][:, c, :], bank(bk), et[c], ALU.mult, r=[bankB[bk], etB[c]], w=[GstB[sl]])
            for s in range(4):
                bk = nbank()
                for kc in range(NKC):
                    mm(bank(bk), hT[sl][:, kc, s * 128:(s + 1) * 128], Win[:, kc, 1280:1792], kc == 0, kc == NKC - 1,
                       r=[WC(1280), WC(1536), hTB[sl][kc // 2]], w=[bankB[bk]], sig=(kc == NKC - 1))
                act(Vst[sl][:, s, :, 0:64], bank(bk).rearrange("p (h d) -> p h d", h=NH), AF.Copy, r=[bankB[bk]], w=[VstB[sl]])
                bk = nbank()
                for kc in range(NKC):
                    mm(bank(bk)[:, 0:256], hT[sl][:, kc, s * 128:(s + 1) * 128], Win[:, kc, 0:256], kc == 0, kc == NKC - 1,
                       r=[WC(0), hTB[sl][kc // 2]], w=[bankB[bk]], sig=(kc == NKC - 1))
                vcopy("dve", Ast[sl][:, s, :], bank(bk)[:, 0:256], r=[bankB[bk]], w=[AstB[sl]])
            ds = stds[sl]
            TR.dma("sp", qz_d[tok0 // 128: tok0 // 128 + 4, :, :].rearrange("s p (h t) -> p h s t", h=NH),
                   Qz[sl].rearrange("p h (s t) -> p h s t", s=4), r=[QzB[sl]], dsem=ds[0])
            TR.dma("sp", kt_d[:, :, tok0:tok0 + 512].rearrange("c p t -> p c t"), Kst[sl], r=[KstB[sl]], dsem=ds[1])
            go = gpad_off[qi] + GP + t * 512
            TR.dma("sp", gt_d[:, :, go:go + 512].rearrange("c p t -> p c t"), Gst[sl], r=[GstB[sl]], dsem=ds[2])
            TR.dma("sp", v_d[tok0:tok0 + 512, :].rearrange("(s p) d -> p s d", p=128), Vst[sl].rearrange("p s h d -> p s (h d)"),
                   r=[VstB[sl]], dsem=ds[3])
            TR.dma("sp", a_d[tok0:tok0 + 512, :].rearrange("(s p) d -> p s d", p=128), Ast[sl], r=[AstB[sl]], dsem=ds[4])

        wcb.update(load_fold_cols(Win, w_in[l], [(b * 256, (b + 1) * 256) for b in order[:1]], g1t, g1B))
        FE_el(0)
        wcb.update(load_fold_cols(Win, w_in[l], [(b * 256, (b + 1) * 256) for b in order[1:]], g1t, g1B))
        FE_pe(0)
        if NT > 1:
            FE_el(1)
        for k in range(NT):
            if k + 1 < NT:
                FE_pe(k + 1)
            BE(k, (lambda kk=k: FE_el(kk + 2)) if k + 2 < NT else None)
        TR.barrier()

    def phase_F(l):
        PA.reset()
        TR.dsem_next = 0
        NSEQ = len(seq_lens)
        Atm = [PA([128, seq_lens[qi] // 128, 256], BF16, "Atm") for qi in range(NSEQ)]
        AtmB = [Buf() for _ in range(NSEQ)]
        for qi in range(NSEQ):
            S = seq_lens[qi]
            TR.dma("sp", Atm[qi], a_d[seq_off[qi]:seq_off[qi] + S, :].rearrange("(t p) c -> p t c", p=128),
                   w=[AtmB[qi]], dsem=TR.new_dsem())
        cosb = [PA([128, NSMAX * 512], BF16, "cos") for _ in range(2)]
        sinb = [PA([128, NSMAX * 512], BF16, "sin") for _ in range(2)]
        cosB = [Buf(), Buf()]
        sinB = [Buf(), Buf()]
        cds_ = [TR.new_dsem(), TR.new_dsem()]
        sds_ = [TR.new_dsem(), TR.new_dsem()]
        PQ = [[PA([128, 512], BF16, "pq") for _ in range(4)] for _ in range(2)]
        PQB = [[Buf() for _ in range(4)] for _ in range(2)]
        sqf = [PA([128, 512], F32, "sqf") for _ in range(4)]
        ssf = [PA([128, 8], F32, "ssf") for _ in range(4)]
        rsf = [PA([128, 8], F32, "rsf") for _ in range(4)]
        om = [PA([128, 256], BF16, "om") for _ in range(4)]
        sqfB, ssfB, rsfB, omB = ([Buf() for _ in range(4)] for _ in range(4))
        ofd = [PA([128, 4, 256], BF16, "ofd") for _ in range(2)]
        ofm = [PA([128, 4, 256], BF16, "ofm") for _ in range(2)]
        ofdB = [Buf(), Buf()]
        ofmB = [Buf(), Buf()]
        odds = [TR.new_dsem(), TR.new_dsem()]
        omds = [TR.new_dsem(), TR.new_dsem()]
        omds2 = [TR.new_dsem(), TR.new_dsem()]
        PH = PA([128, 2], BF16, "PH")
        sq1 = PA([128, 256], F32, "sq1")
        ss1 = PA([128, 4], F32, "ss1")
        rs1 = PA([128, 4], F32, "rs1")
        oh = PA([128, 256], BF16, "oh")
        PHB, sq1B, ss1B, rs1B, ohB = (Buf() for _ in range(5))
        ohds = TR.new_dsem()

        steps = []
        jn = 0
        for S in sorted(set(seq_lens), key=lambda v: seq_lens.index(v)):
            qis = [qi for qi in range(NSEQ) if seq_lens[qi] == S]
            for kt in range(S // 1024):
                for ii, qi in enumerate(qis):
                    steps.append((S, kt, qi, ii == 0, jn))
                jn += 1

        def groups(n, part):
            S, kt, qi, first, j = steps[n]
            nS = S // 128
            sl = n % 2
            tl = j % 2
            a = Atm[qi]
            if first and part == 0:
                TR.dma("sp", cosb[tl][:, 0:nS * 512], cos_d[S][kt], w=[cosB[tl]], dsem=cds_[tl])
                TR.dma("sp", sinb[tl][:, 0:nS * 512], sin_d[S][kt], w=[sinB[tl]], dsem=sds_[tl])
            for c in range(2):
                if c != part:
                    continue
                for which in range(2):
                    bk = sl * 4 + c * 2 + which
                    tab = cosb[tl] if which == 0 else sinb[tl]
                    tB = cosB[tl] if which == 0 else sinB[tl]
                    for st in range(nS):
                        mm(bank(bk), a[:, st, c * 128:(c + 1) * 128], tab[:, st * 512:(st + 1) * 512], st == 0, st == nS - 1,
                           r=[AtmB[qi], tB], w=[bankB[bk]], sig=(st == nS - 1))

        def evacs(n):
            sl = n % 2
            for c in range(2):
                for which in range(2):
                    bk = sl * 4 + c * 2 + which
                    evac("act" if which == 0 else "dve", PQ[sl][c * 2 + which], bank(bk), r=[bankB[bk]], w=[PQB[sl][c * 2 + which]])

        def stage2_main(n):
            S, kt, qi, first, j = steps[n]
            sl = n % 2
            for sub in range(4):
                bk = sl * 4 + sub
                u = sub
                for c in range(2):
                    o = bank(bk)[:, c * 256:(c + 1) * 256]
                    mm(o, PQ[sl][c * 2][:, sub * 128:(sub + 1) * 128], bdc[S], True, False,
                       r=[PQB[sl][c * 2], constB], w=[bankB[bk]], sig=False)
                    mm(o, PQ[sl][c * 2 + 1][:, sub * 128:(sub + 1) * 128], bds[S], False, True,
                       r=[PQB[sl][c * 2 + 1], constB], w=[bankB[bk]], sig=(c == 1))
                O5 = bank(bk).rearrange("p (c m g d) -> p c m g d", c=2, m=2, g=2)
                rs5 = rsf[u].rearrange("p (c m g) -> p c m g", c=2, m=2)
                act(sqf[u], bank(bk), AF.Square, r=[bankB[bk]], w=[sqfB[u]])
                TR.op("dve", lambda e, o=ssf[u], i=sqf[u]: e.tensor_reduce(out=o, in_=i.rearrange("p (a d) -> p a d", a=8), axis=AX.X, op=ALU.add),
                      r=[sqfB[u]], w=[ssfB[u]])
                TR.op("act", lambda e, o=rsf[u], i=ssf[u]: e.activation(out=o, in_=i, func=AF.Sqrt, bias=epsc[:, 0:1], scale=1.0 / HD),
                      r=[ssfB[u], constB], w=[rsfB[u]])
                TR.op("dve", lambda e, o=rsf[u]: e.reciprocal(out=o, in_=o), r=[rsfB[u]], w=[rsfB[u]])
                tt("dve", ofd[sl][:, sub, :].rearrange("p (c g d) -> p c g d", c=2, g=2), O5[:, :, 0],
                   rs5[:, :, 0].unsqueeze(3).to_broadcast([128, 2, 2, HD]), ALU.mult, r=[bankB[bk], rsfB[u]], w=[ofdB[sl]])
                tt("dve", om[u].rearrange("p (c g d) -> p c g d", c=2, g=2), O5[:, :, 1],
                   rs5[:, :, 1].unsqueeze(3).to_broadcast([128, 2, 2, HD]), ALU.mult, r=[bankB[bk], rsfB[u]], w=[omB[u]])

        def stage2_tail(n):
            S, kt, qi, first, j = steps[n]
            nS = S // 128
            sl = n % 2
            k0 = kt * 512
            for sub in range(4):
                bk = sl * 4 + sub
                u = sub
                mm(bank(bk)[:, 0:256], jrev, om[u], True, True, r=[omB[u], constB], w=[bankB[bk]], sig=True)
                evac("act" if sub % 2 == 0 else "dve", ofm[sl][:, 3 - sub, :], bank(bk)[:, 0:256], r=[bankB[bk]], w=[ofmB[sl]])
            t0 = seq_off[qi]
            TR.dma("sp", of_d[t0 + k0:t0 + k0 + 512, :].rearrange("(s p) d -> p s d", p=128), ofd[sl], r=[ofdB[sl]], dsem=odds[sl])
            base = t0 + S - k0 - 511
            if kt > 0:
                TR.dma("sp", of_d[base:base + 512, :].rearrange("(s p) d -> p s d", p=128), ofm[sl], r=[ofmB[sl]], dsem=omds[sl])
            else:
                TR.dma("sp", of_d[base:base + 384, :].rearrange("(s p) d -> p s d", p=128), ofm[sl][:, 0:3, :], r=[ofmB[sl]], dsem=omds[sl])
                TR.dma("sp", of_d[base + 384:base + 511, :], ofm[sl][0:127, 3, :], r=[ofmB[sl]], dsem=omds2[sl])
                bk = sl * 4
                a = Atm[qi]
                for c in range(2):
                    for st in range(nS):
                        mm(bank(bk)[:, c:c + 1], a[:, st, c * 128:(c + 1) * 128], alt[:, 0:1], st == 0, st == nS - 1,
                           r=[AtmB[qi], constB], w=[bankB[bk]], sig=(c == 1 and st == nS - 1))
                evac("act", PH, bank(bk)[:, 0:2], r=[bankB[bk]], w=[PHB])
                for c in range(2):
                    mm(bank(bk)[0:1, 256 + c * 128:256 + (c + 1) * 128], PH[:, c:c + 1], bdc[S][:, 0:128], True, True,
                       r=[PHB, constB], w=[bankB[bk]], sig=(c == 1))
                O1 = bank(bk)[0:1, 256:512]
                act(sq1[0:1, :], O1, AF.Square, r=[bankB[bk]], w=[sq1B])
                TR.op("dve", lambda e: e.tensor_reduce(out=ss1[0:1, :], in_=sq1[0:1, :].rearrange("p (g d) -> p g d", g=4), axis=AX.X, op=ALU.add),
                      r=[sq1B], w=[ss1B])
                ts("pool", rs1[0:1, :], ss1[0:1, :], 1.0 / HD, RMS_EPS, ALU.mult, ALU.add, r=[ss1B], w=[rs1B])
                tt("pool", rs1[0:1, :], rs1[0:1, :], mhalf[0:1, 0:4], ALU.pow, r=[rs1B, constB], w=[rs1B])
                tt("dve", oh[0:1, :].rearrange("p (g d) -> p g d", g=4), O1.rearrange("p (g d) -> p g d", g=4),
                   rs1[0:1, :].unsqueeze(2).to_broadcast([1, 4, HD]), ALU.mult, r=[bankB[bk], rs1B], w=[ohB])
                TR.dma("sp", of_d[t0 + S // 2:t0 + S // 2 + 1, :], oh[0:1, :], r=[ohB], dsem=ohds)

        NJ = len(steps)
        groups(0, 0)
        groups(0, 1)
        evacs(0)
        for n in range(NJ):
            if n + 1 < NJ:
                groups(n + 1, 0)
            stage2_main(n)
            if n + 1 < NJ:
                groups(n + 1, 1)
            stage2_tail(n)
            if n + 1 < NJ:
                evacs(n + 1)
        TR.barrier()

    def phase_B(l):
        PA.reset()
        TR.dsem_next = 0
        src = xin if l == 0 else xa_d
        NCH = NSMAX // 4
        KT = PA([128, NCH, 4, 512], BF16, "KT")
        VA = PA([128, NSMAX, NH, VW], BF16, "VA")
        KVB = [Buf() for _ in range(NCH)]
        kds = [TR.new_dsem() for _ in range(NCH)]
        vds = [TR.new_dsem() for _ in range(NCH)]
        E = PA([128, NH * NMAT, 128], BF16, "E")
        EB = [Buf() for _ in range(NH)]
        eds = [TR.new_dsem() for _ in range(NH)]

        def load_E():
            for h in range(NH):
                Eh = E[:, h * NMAT:(h + 1) * NMAT, :].rearrange("p m q -> p (m q)")
                TR.dma("sp", Eh, biasmat[l, :, h * NMAT * 128:(h + 1) * NMAT * 128], w=[EB[h]], dsem=eds[h])
        Wo = PA([128, NKC, D], BF16, "Wo")
        mgt = PA([128, NKC], F32, "mgt")
        mgB = Buf()
        TR.dma("sp", mgt, mix_gT[l], w=[mgB], dsem=TR.new_dsem())
        WoG = [Buf(), Buf()]
        for g4 in range(2):
            TR.dma("pool", Wo[:, g4 * 4:(g4 + 1) * 4, :], w_out[l, g4 * 512:(g4 + 1) * 512, :].rearrange("(k p) c -> p k c", p=128),
                   w=[WoG[g4]], dsem=TR.new_swsem())
            tt("dve", Wo[:, g4 * 4:(g4 + 1) * 4, :], Wo[:, g4 * 4:(g4 + 1) * 4, :],
               mgt[:, g4 * 4:(g4 + 1) * 4].unsqueeze(2).to_broadcast([128, 4, D]), ALU.mult, r=[WoG[g4], mgB], w=[WoG[g4]])
        WoB = [WoG[kc // 4] for kc in range(NKC)]
        cw = PA([128, 2, CW], F32, "cw")
        cwB = Buf()
        TR.dma("sp", cw, conv_wT[l].rearrange("(c p) j -> p c j", p=128), w=[cwB], dsem=TR.new_dsem())
        Dg = PA([128, 2, CW, 128], BF16, "Dg")
        DgB = Buf()
        for c in range(2):
            for j in range(CW):
                ts("dve", Dg[:, c, j, :], ident, cw[:, c, j:j + 1], None, ALU.mult, None, r=[identB, cwB], w=[DgB])
        cbb = PA([128, 256], F32, "cbb")
        lgb = PA([128, 256], F32, "lgb")
        lbb = PA([128, 256], F32, "lbb")
        vecB = Buf()
        vds_ = TR.new_dsem()
        bcast_load(cbb, conv_b[l:l + 1, :], vecB, vds_)
        bcast_load(lgb, conv_ln_g[l:l + 1, :], vecB, vds_)
        bcast_load(lbb, conv_ln_b[l:l + 1, :], vecB, vds_)
        LAG = 2
        NQ = 3
        Qz = [PA([128, NH, 128], BF16, "Qz") for _ in range(NQ)]
        QzB = [Buf() for _ in range(NQ)]
        qds = [TR.new_dsem() for _ in range(NQ)]
        gTt = [PA([128, 2, 512 + 2 * GP], BF16, "gTt") for _ in range(2)]
        gTB = [Buf(), Buf()]
        gds = [TR.new_dsem(), TR.new_dsem()]
        xs = [PA([128, D], F32, "x") for _ in range(NQ)]
        xB = [Buf() for _ in range(NQ)]
        xds = [TR.new_dsem() for _ in range(NQ)]
        sds_ = [TR.new_dsem() for _ in range(NQ)]
        obf = [PA([128, D], BF16, "obf") for _ in range(NQ)]
        obfB = [Buf() for _ in range(NQ)]
        ofds = [TR.new_dsem() for _ in range(NQ)]
        ocat = [PA([128, 768], F32, "ocat") for _ in range(2)]
        ocA = [Buf(), Buf()]
        ocC = [Buf(), Buf()]
        sq = PA([128, 768], F32, "sq")
        sqB = Buf()
        ssg = PA([128, 12], F32, "ssg")
        rsg = PA([128, 12], F32, "rsg")
        ssgB, rsgB = Buf(), Buf()
        rden = PA([128, NH], F32, "rden")
        rdenB = Buf()
        oT = [PA([128, NKC, 128], BF16, "oT") for _ in range(2)]
        oTB = [[Buf(), Buf()], [Buf(), Buf()]]
        P0 = [PA([128, 640], BF16, "P0") for _ in range(2)]
        P1 = [PA([128, 640], BF16, "P1") for _ in range(2)]
        P0B = [Buf(), Buf()]
        P1B = [Buf(), Buf()]
        yb = [PA([128, 256], F32, "yb") for _ in range(2)]
        ybB = [Buf(), Buf()]
        yn = PA([128, 256], F32, "yn")
        ey = PA([128, 256], F32, "ey")
        ynB, eyB = Buf(), Buf()
        bst = PA([128, 6], F32, "bst")
        mv = PA([128, 2], F32, "mv")
        rsl = PA([128, 1], F32, "rsl")
        bstB, mvB, rslB = Buf(), Buf(), Buf()

        MB = [Buf(), Buf()]
        XB = Buf()
        OB = [Buf(), Buf()]
        YB = Buf()
        TB = Buf()
        PB7 = Buf()
        O = PS[:, 3 * 512:5 * 512].rearrange("p (h d) -> p h d", h=NH)
        Y = PS[:, 5 * 512:5 * 512 + 256]

        subs = []
        for qi, S in enumerate(seq_lens):
            for i in range(S // 128):
                subs.append((qi, i, seq_off[qi] + i * 128))
        NSUB = len(subs)

        def load_kv(qi, ch):
            tok0 = seq_off[qi] + ch * 512
            TR.dma("sp", KT[:, ch, :, :], kt_d[:, :, tok0:tok0 + 512].rearrange("c p t -> p c t"), w=[KVB[ch]], dsem=kds[ch])
            TR.dma("sp", VA[:, ch * 4:(ch + 1) * 4, :, :].rearrange("p s h d -> p s (h d)"),
                   v_d[tok0:tok0 + 512, :].rearrange("(s p) d -> p s d", p=128), w=[KVB[ch]], dsem=vds[ch])

        def FE(n):
            qi, i, tok0 = subs[n]
            sl = n % NQ
            if i % 4 == 0:
                t = i // 4
                go = gpad_off[qi] + t * 512
                TR.dma("sp", gTt[t % 2], gt_d[:, :, go:go + 512 + 2 * GP].rearrange("c p t -> p c t"), w=[gTB[t % 2]], dsem=gds[t % 2])
            TR.dma("sp", Qz[sl].rearrange("p h t -> p (h t)"), qz_d[tok0 // 128], w=[QzB[sl]], dsem=qds[sl])

        def S1(n, fill=None):
            qi, i, tok0 = subs[n]
            S = seq_lens[qi]
            npairs = S // 128
            sl = n % NQ
            u = n % 2
            js, m0 = _pair_class(i, npairs)
            nj = len(js)
            n4 = min(nj, 4)
            t = i // 4
            sub = i % 4
            g = gTt[t % 2]
            convq = [(c, j) for c in range(2) for j in range(CW)]
            cpos = [0]
            ydone = [False]

            def conv_some(cnt):
                for _ in range(cnt):
                    if cpos[0] >= len(convq):
                        return
                    c, j = convq[cpos[0]]
                    cpos[0] += 1
                    o0 = sub * 128 + j + 1
                    mm(Y[:, c * 128:(c + 1) * 128], g[:, c, o0:o0 + 128], Dg[:, c, j, :], j == 0, j == CW - 1,
                       r=[gTB[t % 2], DgB], w=[YB], sig=(c == 1 and j == CW - 1))

            def qk(h):
                c = h // 2
                hb = h % 2
                for jj, j in enumerate(js):
                    if jj < 4:
                        o = PS[:, hb * 512 + jj * 128: hb * 512 + (jj + 1) * 128]
                        wb = MB[hb]
                    else:
                        conv_some(4)
                        o = PS[:, 2 * 512 + hb * 128: 2 * 512 + (hb + 1) * 128]
                        wb = XB
                    mm(o, KT[:, j // 4, c, (j % 4) * 128:(j % 4 + 1) * 128],
                       Qz[sl][:, h, :], True, False, r=[KVB[j // 4], QzB[sl]], w=[wb], sig=False)
                    mm(o, ident, E[:, h * NMAT + m0 + jj, :], False, True, r=[identB, EB[h]], w=[wb],
                       sig=(jj == nj - 1 or jj == 3))
                act(P1[hb][:, 0:n4 * 128], PS[:, hb * 512: hb * 512 + n4 * 128], AF.Exp, r=[MB[hb]], w=[P1B[hb]])
                if nj > 4:
                    act(P1[hb][:, 512:640], PS[:, 2 * 512 + hb * 128: 2 * 512 + (hb + 1) * 128], AF.Exp, r=[XB], w=[P1B[hb]])

            def pv(h):
                hb = h % 2
                for jj, j in enumerate(js):
                    mm(O[:, h, 0:HD + 1], P1[hb][:, jj * 128:(jj + 1) * 128], VA[:, j, h, 0:HD + 1], jj == 0, jj == nj - 1,
                       r=[P1B[hb], KVB[j // 4]], w=[OB[h // 4]], sig=(jj == nj - 1))

            if fill is not None:
                for f in fill.get(-1, ()):
                    f()
            conv_some(12)
            qk(0)
            for h in range(NH):
                if h + 1 < NH:
                    qk(h + 1)
                conv_some(8)
                pv(h)
                if cpos[0] >= len(convq) and not ydone[0]:
                    ydone[0] = True
                    tt("dve", yb[u], Y, cbb, ALU.add, r=[YB, vecB], w=[ybB[u]])
                if fill is not None:
                    for f in fill.get(h, ()):
                        f()
            conv_some(len(convq))
            for hf in range(2):
                TR.op("dve", lambda e, hf=hf: e.reciprocal(out=rden[:, hf * 4:hf * 4 + 4], in_=O[:, hf * 4:hf * 4 + 4, HD]),
                      r=[OB[hf]], w=[rdenB])
                tt("dve", ocat[u][:, hf * 256:(hf + 1) * 256].rearrange("p (h d) -> p h d", h=4), O[:, hf * 4:hf * 4 + 4, 0:HD],
                   rden[:, hf * 4:hf * 4 + 4].unsqueeze(2).to_broadcast([128, 4, HD]), ALU.mult, r=[OB[hf], rdenB], w=[ocA[u]])
            if not ydone[0]:
                tt("dve", yb[u], Y, cbb, ALU.add, r=[YB, vecB], w=[ybB[u]])
            if fill is not None:
                for f in fill.get(8, ()):
                    f()
            if i % 4 == 3 and qi + 1 < len(seq_lens):
                tcur = i // 4
                nt = S // 512
                chs = [tcur - 1, tcur] if tcur == nt - 1 else [tcur - 1]
                for chn in chs:
                    if 0 <= chn < seq_lens[qi + 1] // 512:
                        load_kv(qi + 1, chn)

        def S2(n, part=None):
            qi, i, tok0 = subs[n]
            sl = n % NQ
            u = n % 2
            if part is None or part == 0:
                S2a(n, sl, u, tok0)
            if part is None or part == 1:
                S2b(n, sl, u)
            if part is None or part == 2:
                S2c(n, sl, u)

        def S2a(n, sl, u, tok0):
            TR.dma("sp", xs[sl], src[tok0:tok0 + 128, :], w=[xB[sl]], dsem=xds[sl])
            TR.dma("sp", obf[sl][:, 0:256], of_d[tok0:tok0 + 128, :], w=[obfB[sl]], dsem=ofds[sl])
            TR.op("dve", lambda e: e.bn_stats(out=bst, in_=yb[u]), r=[ybB[u]], w=[bstB])
            TR.op("dve", lambda e: e.bn_aggr(out=mv, in_=bst), r=[bstB], w=[mvB])
            rsqrt_pool(rsl, mv[:, 1:2], 1.0, LN_EPS, r=[mvB], w=[rslB])
            ts("dve", yn, yb[u], mv[:, 0:1], rsl[:, 0:1], ALU.subtract, ALU.mult, r=[ybB[u], mvB, rslB], w=[ynB])
            tt("pool", yn, yn, lgb, ALU.mult, r=[ynB, vecB], w=[ynB])
            tt("pool", yn, yn, lbb, ALU.add, r=[ynB, vecB], w=[ynB])

        def S2b(n, sl, u):
            act(ey, yn, AF.Exp, r=[ynB], w=[eyB], scale=-1.0)
            ts("pool", ey, ey, 1.0, 1.0, ALU.add, ALU.mult, r=[eyB], w=[eyB])
            TR.op("dve", lambda e: e.reciprocal(out=ey, in_=ey), r=[eyB], w=[eyB])
            tt("dve", ocat[u][:, 512:768], yn, ey, ALU.mult, r=[ynB, eyB], w=[ocC[u]])

        def S2c(n, sl, u):
            act(sq, ocat[u], AF.Square, r=[ocA[u], ocC[u]], w=[sqB])
            TR.op("dve", lambda e: e.tensor_reduce(out=ssg, in_=sq.rearrange("p (g d) -> p g d", g=12), axis=AX.X, op=ALU.add),
                  r=[sqB], w=[ssgB])
            rsqrt_pool(rsg, ssg, 1.0 / HD, RMS_EPS, r=[ssgB], w=[rsgB])
            tt("dve", obf[sl][:, 256:D].rearrange("p (g d) -> p g d", g=12), ocat[u].rearrange("p (g d) -> p g d", g=12),
               rsg.unsqueeze(2).to_broadcast([128, 12, HD]), ALU.mult, r=[ocA[u], ocC[u], rsgB], w=[obfB[sl]])

        def S3(n, part=None):
            qi, i, tok0 = subs[n]
            sl = n % NQ
            u = n % 2
            if part is None or part == 0:
                for kc in range(NKC):
                    tr(bankb(6)[:, kc * 128:(kc + 1) * 128], obf[sl][:, kc * 128:(kc + 1) * 128], r=[obfB[sl]], w=[TB], sig=(kc == NKC - 1))
                evac("act", oT[u][:, 0:4, :].rearrange("p a b -> p (a b)"), bankb(6)[:, 0:512], r=[TB], w=[oTB[u][0]])
                evac("act", oT[u][:, 4:8, :].rearrange("p a b -> p (a b)"), bankb(6)[:, 512:1024], r=[TB], w=[oTB[u][1]])
            for half in range(2):
                if part is not None and part != half + 1:
                    continue
                bk = 7 - half
                bB = PB7 if half == 0 else TB
                for kc in range(NKC):
                    mm(bank(bk), oT[u][:, kc, :], Wo[:, kc, half * 512:(half + 1) * 512], kc == 0, kc == NKC - 1,
                       r=[oTB[u][kc // 4], WoB[kc]], w=[bB], sig=(kc == NKC - 1))
                tt("dve", xs[sl][:, half * 512:(half + 1) * 512], xs[sl][:, half * 512:(half + 1) * 512], bank(bk), ALU.add,
                   r=[xB[sl], bB], w=[xB[sl]])
            if part is None or part == 2:
                TR.dma("sp", xb_d[tok0:tok0 + 128, :], xs[sl], r=[xB[sl]], dsem=sds_[sl])

        load_kv(0, 0)
        load_E()
        for ch in range(1, seq_lens[0] // 512):
            load_kv(0, ch)
        done_extra = set()

        def extra_loads(qi):
            if qi + 1 < len(seq_lens) and qi not in done_extra:
                done_extra.add(qi)
                for chn in range(seq_lens[qi] // 512, seq_lens[qi + 1] // 512):
                    load_kv(qi + 1, chn)

        FE(0)
        if NSUB > 1:
            FE(1)
        S1(0)
        for k in range(NSUB + 1):
            if k + 2 < NSUB:
                FE(k + 2)
            if k < NSUB and subs[k][1] == 0:
                extra_loads(subs[k][0])
            n2 = k
            n3 = k - 1
            v2 = 0 <= n2 < NSUB
            v3 = 0 <= n3 < NSUB
            if k + 1 < NSUB:
                hooks = {}
                if v2:
                    hooks.setdefault(-1, []).append(lambda n=n2: S2(n, 0))
                    hooks.setdefault(1, []).append(lambda n=n2: S2(n, 1))
                    hooks.setdefault(3, []).append(lambda n=n2: S2(n, 2))
                if v3:
                    hooks.setdefault(2, []).append(lambda n=n3: S3(n, 0))
                    hooks.setdefault(4, []).append(lambda n=n3: S3(n, 1))
                    hooks.setdefault(8, []).append(lambda n=n3: S3(n, 2))
                S1(k + 1, hooks)
            else:
                if v3:
                    S3(n3)
                if v2:
                    S2(n2)
        TR.barrier()

    def phase_C(l):
        PA.reset()
        TR.dsem_next = 0
        last = (l == L - 1)
        dst = y if last else xa_d
        W1 = PA([128, NKC, DFF], BF16, "W1")
        W2 = PA([128, 32, D], BF16, "W2")
        g2t = PA([128, NKC], F32, "g2t")
        g2B = Buf()
        TR.dma("sp", g2t, norm2_gT[l], w=[g2B], dsem=TR.new_dsem())
        w1cb = {}
        W2G = [Buf() for _ in range(8)]
        W2B = [W2G[kc // 4] for kc in range(32)]

        def load_rest_weights():
            w1cb.update(load_fold_cols(W1, w_ff_in[l], [(b * 512, (b + 1) * 512) for b in range(1, 8)], g2t, g2B))
            for g4 in range(8):
                TR.dma("pool", W2[:, g4 * 4:(g4 + 1) * 4, :], w_ff_out[l, g4 * 512:(g4 + 1) * 512, :].rearrange("(k p) c -> p k c", p=128),
                       w=[W2G[g4]], dsem=TR.new_swsem())
        if last:
            gfb = PA([128, D], F32, "gfb")
            gfB = Buf()
            bcast_load(gfb, final_norm_g[0:1, :], gfB, TR.new_dsem())
        xs = [PA([128, D], F32, "x") for _ in range(2)]
        xB = [Buf(), Buf()]
        xds = [TR.new_dsem(), TR.new_dsem()]
        hs = [PA([128, D], BF16, "h") for _ in range(4)]
        hB = [Buf() for _ in range(4)]
        ss = PA([128, 4], F32, "ss")
        rs = PA([128, 4], F32, "rs")
        ssB = [Buf() for _ in range(4)]
        rsB = [Buf() for _ in range(4)]
        hT = PA([128, NKC, 512], BF16, "hT")
        hTB = [Buf() for _ in range(4)]
        fT = PA([128, 32, 512], BF16, "fT")
        fTB = [Buf() for _ in range(32)]
        rt = [PA([128, 512], F32, "rt") for _ in range(2)]
        rtB = [Buf(), Buf()]
        xr = [PA([128, D], F32, "xr") for _ in range(2)]
        xrB = [Buf(), Buf()]
        xrds = [TR.new_dsem(), TR.new_dsem()]
        ods = [TR.new_dsem(), TR.new_dsem()]
        s2 = PA([128, 2], F32, "s2")
        r2 = PA([128, 2], F32, "r2")
        s2B = [Buf(), Buf()]
        r2B = [Buf(), Buf()]
        jk = PA([128, D], BF16, "jk")
        jkB = Buf()
        bctr = [0]
        xctr = [0]

        def FE_el(k):
            qi, t, tok0 = tiles[k]
            for s in range(4):
                xi = xctr[0] % 2
                xctr[0] += 1
                TR.dma("sp", xs[xi], xb_d[tok0 + s * 128: tok0 + (s + 1) * 128, :], w=[xB[xi]], dsem=xds[xi])
                act(hs[s], xs[xi], AF.Square, r=[xB[xi]], w=[hB[s], ssB[s]], accum=ss[:, s:s + 1])
                rsqrt_pool(rs[:, s:s + 1], ss[:, s:s + 1], 1.0 / D, RMS_EPS, r=[ssB[s]], w=[rsB[s]])
                ts("dve", hs[s], xs[xi], rs[:, s:s + 1], None, ALU.mult, None, r=[xB[xi], rsB[s]], w=[hB[s]])

        def FE_pe(k):
            for pr in range(4):
                bk = 6 + (pr % 2)
                for kk in range(2):
                    kc = pr * 2 + kk
                    for s in range(4):
                        tr(bankb(bk)[:, kk * 512 + s * 128: kk * 512 + (s + 1) * 128], hs[s][:, kc * 128:(kc + 1) * 128],
                           r=[hB[s]], w=[bankB[bk]], sig=(kk == 1 and s == 3))
                evac(evq(), hT[:, pr * 2:pr * 2 + 2, :].rearrange("p a b -> p (a b)"), bankb(bk), r=[bankB[bk]], w=[hTB[pr]])

        def BE1(k, hook):
            for fc in range(32):
                if fc == 10 and hook is not None:
                    hook()
                bk = bctr[0] % 6
                bctr[0] += 1
                for kc in range(NKC):
                    mm(bank(bk), W1[:, kc, fc * 128:(fc + 1) * 128], hT[:, kc, :], kc == 0, kc == NKC - 1,
                       r=[w1cb[((fc // 4) * 512, (fc // 4) * 512 + 512)], hTB[kc // 2]], w=[bankB[bk]], sig=(kc == NKC - 1))
                u = fc % 2
                act(rt[u], bank(bk), AF.Relu, r=[bankB[bk]], w=[rtB[u]])
                tt("dve", fT[:, fc, :], rt[u], rt[u], ALU.mult, r=[rtB[u]], w=[fTB[fc]])

        def BE2(k):
            qi, t, tok0 = tiles[k]
            for s in range(4):
                u = s % 2
                TR.dma("sp", xr[u], xb_d[tok0 + s * 128: tok0 + (s + 1) * 128, :], w=[xrB[u]], dsem=xrds[u])
                for half in range(2):
                    bk = bctr[0] % 6
                    bctr[0] += 1
                    for kc in range(32):
                        mm(bank(bk), fT[:, kc, s * 128:(s + 1) * 128], W2[:, kc, half * 512:(half + 1) * 512], kc == 0, kc == 31,
                           r=[fTB[kc], W2B[kc]], w=[bankB[bk]], sig=(kc == 31))
                    tt("dve", xr[u][:, half * 512:(half + 1) * 512], xr[u][:, half * 512:(half + 1) * 512], bank(bk), ALU.add,
                       r=[xrB[u], bankB[bk]], w=[xrB[u]])
                if last:
                    act(jk, xr[u], AF.Square, r=[xrB[u]], w=[jkB, s2B[u]], accum=s2[:, u:u + 1])
                    rsqrt_pool(r2[:, u:u + 1], s2[:, u:u + 1], 1.0 / D, RMS_EPS, r=[s2B[u]], w=[r2B[u]])
                    stt(xr[u], xr[u], r2[:, u:u + 1], gfb, ALU.mult, ALU.mult, r=[xrB[u], r2B[u], gfB], w=[xrB[u]])
                TR.dma("sp", dst[tok0 + s * 128: tok0 + (s + 1) * 128, :], xr[u], r=[xrB[u]], dsem=ods[u])

        w1cb.update(load_fold_cols(W1, w_ff_in[l], [(0, 512)], g2t, g2B))
        FE_el(0)
        load_rest_weights()
        FE_pe(0)
        for k in range(NT):
            BE1(k, (lambda kk=k: FE_el(kk + 1)) if k + 1 < NT else None)
            if k + 1 < NT:
                FE_pe(k + 1)
            BE2(k)
        TR.barrier()

    import os as _os
    _ph = _os.environ.get("MK_PHASES", "AFBC")
    for l in range(L):
        if "A" in _ph:
            phase_A(l)
        if "F" in _ph:
            phase_F(l)
        if "B" in _ph:
            phase_B(l)
        if "C" in _ph:
            phase_C(l)
    TR.replay()
    return nc, TR, PA


_CACHE = {}


def _const_inputs(seq_lens):
    key = tuple(sorted(set(seq_lens)))
    if key not in _CACHE:
        d = {"ident": np.eye(128, dtype=np.float32).astype(ml_dtypes.bfloat16),
             "jrev": np.eye(128, dtype=np.float32)[::-1].copy().astype(ml_dtypes.bfloat16),
             "alt": np.stack([(-1.0) ** np.arange(128)] * 2, axis=1).astype(ml_dtypes.bfloat16)}
        for S in key:
            c, s, bc, bs = _dft_tables(S)
            d["cos%d" % S] = c
            d["sin%d" % S] = s
            d["bdc%d" % S] = bc
            d["bds%d" % S] = bs
        _CACHE[key] = d
    return _CACHE[key]


def make_weight_inputs(depth, norm1_g, w_in, rpb, conv_w, conv_b, conv_ln_g, conv_ln_b, mix_norm_g, w_out,
                       norm2_g, w_ff_in, w_ff_out, final_norm_g):
    f = lambda a: np.ascontiguousarray(np.asarray(a, dtype=np.float32))
    gT = lambda a: np.ascontiguousarray(f(a).reshape(depth, NKC, 128).transpose(0, 2, 1))
    DR, DC, VAL = _bias_index_tables()
    rpb = f(rpb)
    g = rpb[:, :, DR, DC]
    g = np.where(VAL[None, None], g, np.float32(NEG))
    g = np.ascontiguousarray(g.transpose(0, 3, 1, 2, 4)).reshape(depth, 128, NH * NMAT * 128)
    return {
        "norm1_gT": gT(norm1_g), "w_in": f(w_in), "biasmat": g.astype(ml_dtypes.bfloat16),
        "conv_wT": np.ascontiguousarray(f(conv_w).transpose(0, 2, 1)), "conv_b": f(conv_b),
        "conv_ln_g": f(conv_ln_g), "conv_ln_b": f(conv_ln_b), "mix_gT": gT(mix_norm_g), "w_out": f(w_out),
        "norm2_gT": gT(norm2_g), "w_ff_in": f(w_ff_in), "w_ff_out": f(w_ff_out),
        "final_norm_g": f(final_norm_g).reshape(1, D),
    }


_PROG = {}


def get_program(seq_lens, depth):
    key = (tuple(seq_lens), depth)
    if key not in _PROG:
        _PROG[key] = build_program(list(seq_lens), depth)[0]
    return _PROG[key]


def kernel(x_prompt, x_sample, norm1_g, w_in, rpb, conv_w, conv_b, conv_ln_g, conv_ln_b, mix_norm_g, w_out,
           norm2_g, w_ff_in, w_ff_out, final_norm_g):
    x_prompt = np.asarray(x_prompt, dtype=np.float32)
    x_sample = np.asarray(x_sample, dtype=np.float32)
    n = 8
    depth = int(np.asarray(w_in).shape[0])
    BP, SP_, _ = x_prompt.shape
    BS, SS, _ = x_sample.shape
    pp = BP // n
    sp = BS // n
    seq_lens = [SP_] * pp + [SS] * sp
    nc = get_program(seq_lens, depth)
    wts = make_weight_inputs(depth, norm1_g, w_in, rpb, conv_w, conv_b, conv_ln_g, conv_ln_b, mix_norm_g, w_out,
                             norm2_g, w_ff_in, w_ff_out, final_norm_g)
    consts = _const_inputs(seq_lens)
    in_maps = []
    for c in range(n):
        xin = np.concatenate([x_prompt[c * pp:(c + 1) * pp].reshape(pp * SP_, D),
                              x_sample[c * sp:(c + 1) * sp].reshape(sp * SS, D)], axis=0)
        m = {"xin": np.ascontiguousarray(xin)}
        m.update(wts)
        m.update(consts)
        in_maps.append(m)
    res = run_bass_kernel_spmd(nc, in_maps, core_ids=list(range(n)))
    yp = np.empty_like(x_prompt)
    ys = np.empty_like(x_sample)
    for c in range(n):
        yc = np.asarray(res.results[c]["y"], dtype=np.float32)
        yp[c * pp:(c + 1) * pp] = yc[:pp * SP_].reshape(pp, SP_, D)
        ys[c * sp:(c + 1) * sp] = yc[pp * SP_:].reshape(sp, SS, D)
    return (yp, ys)
```

```python
import numpy as np
import ml_dtypes
import concourse.bass as bass
import concourse.mybir as mybir
from concourse.bass_utils import run_bass_kernel_spmd

F32 = mybir.dt.float32
BF16 = mybir.dt.bfloat16
AF = mybir.ActivationFunctionType
ALU = mybir.AluOpType
AX = mybir.AxisListType

D = 1024
NKC = 8
DIN = 2304
DFF = 4096
NH = 8
HD = 64
CW = 31
RMS_EPS = 1e-6
LN_EPS = 1e-5
NEG = -30000.0
NMAT = 21
VW = 72
GP = 16


class Sem:
    def __init__(self, nc, name):
        self.h = nc.alloc_semaphore(name)
        self.v = 0


class Buf:
    __slots__ = ("name", "w", "r")

    def __init__(self, name=""):
        self.name = name
        self.w = {}
        self.r = {}


class Q:
    def __init__(self, name, sem):
        self.name = name
        self.sem = sem
        self.seen = {}
        self.prog = []
        self.pending = False


class Tracker:
    def __init__(self, nc):
        self.nc = nc
        self.q = {}
        for n in ("pe", "act", "dve", "pool", "sp"):
            self.q[n] = Q(n, Sem(nc, "q_" + n))
        self.bar = Sem(nc, "bar")
        self.dsems = []
        self.dsem_next = 0
        self.swsems = []
        self.swsem_next = 0
        self.ninst = 0

    def new_dsem(self):
        if self.dsem_next == len(self.dsems):
            self.dsems.append(Sem(self.nc, "d%d" % len(self.dsems)))
        s = self.dsems[self.dsem_next]
        self.dsem_next += 1
        return s

    def new_swsem(self):
        if self.swsem_next == len(self.swsems):
            self.swsems.append(Sem(self.nc, "w%d" % len(self.swsems)))
        s = self.swsems[self.swsem_next]
        self.swsem_next += 1
        return s

    def _waits(self, q, needs):
        for s, v in needs.items():
            if v > 0 and q.seen.get(s, 0) < v:
                q.prog.append(("wait", s, v))
                q.seen[s] = v

    def op(self, qn, fn, r=(), w=(), sig=True):
        q = self.q[qn]
        needs = {}
        is_pe = qn == "pe"
        for b in r:
            for s, v in b.w.items():
                if s is q.sem and is_pe:
                    continue
                if needs.get(s, 0) < v:
                    needs[s] = v
        for b in w:
            for s, v in b.w.items():
                if s is q.sem:
                    continue
                if needs.get(s, 0) < v:
                    needs[s] = v
            for s, v in b.r.items():
                if s is q.sem:
                    continue
                if needs.get(s, 0) < v:
                    needs[s] = v
        self._waits(q, needs)
        if sig:
            q.sem.v += 1
            tv = q.sem.v
            q.prog.append(("inst", fn, q.sem, 1))
            q.pending = False
        else:
            assert is_pe
            tv = q.sem.v + 1
            q.prog.append(("inst", fn, None, 0))
            q.pending = True
        s = q.sem
        for b in r:
            if b.r.get(s, 0) < tv:
                b.r[s] = tv
        for b in w:
            if b.w.get(s, 0) < tv:
                b.w[s] = tv
        self.ninst += 1

    def dma(self, qn, out, in_, r=(), w=(), dsem=None, **kw):
        q = self.q[qn]
        needs = {}
        for b in r:
            for s, v in b.w.items():
                if needs.get(s, 0) < v:
                    needs[s] = v
        for b in w:
            for s, v in b.w.items():
                if needs.get(s, 0) < v:
                    needs[s] = v
            for s, v in b.r.items():
                if needs.get(s, 0) < v:
                    needs[s] = v
        if needs.get(dsem, 0) < dsem.v:
            needs[dsem] = dsem.v
        self._waits(q, needs)
        dsem.v += 16
        tv = dsem.v

        def fn(e, out=out, in_=in_, kw=kw):
            return e.dma_start(out=out, in_=in_, **kw)
        q.prog.append(("inst", fn, dsem, 16))
        for b in r:
            if b.r.get(dsem, 0) < tv:
                b.r[dsem] = tv
        for b in w:
            if b.w.get(dsem, 0) < tv:
                b.w[dsem] = tv
        self.ninst += 1

    def barrier(self):
        sp = self.q["sp"]
        needs = {}
        for n, q in self.q.items():
            assert not q.pending, n
            if n != "sp":
                needs[q.sem] = q.sem.v
        for s in self.dsems + self.swsems:
            needs[s] = s.v
        self._waits(sp, needs)
        self.bar.v += 1
        sp.prog.append(("seminc", self.bar, 1))
        for n, q in self.q.items():
            if n != "sp":
                q.prog.append(("wait", self.bar, self.bar.v))
            for s, v in needs.items():
                q.seen[s] = max(q.seen.get(s, 0), v)
        self.dsem_next = 0
        self.swsem_next = 0

    def replay(self):
        nc = self.nc
        progs = self.q

        def run(e, prog):
            for it in prog:
                if it[0] == "wait":
                    e.wait_ge(it[1].h, it[2])
                elif it[0] == "inst":
                    ins = it[1](e)
                    if it[2] is not None:
                        ins.then_inc(it[2].h, it[3])
                else:
                    e.sem_inc(it[1].h, it[2])

        with nc.Block() as block:
            @block.sync
            def _(e):
                run(e, progs["sp"].prog)

            @block.tensor
            def _(e):
                run(e, progs["pe"].prog)

            @block.scalar
            def _(e):
                run(e, progs["act"].prog)

            @block.vector
            def _(e):
                run(e, progs["dve"].prog)

            @block.gpsimd
            def _(e):
                run(e, progs["pool"].prog)


class Alloc:
    def __init__(self, nc, base, limit):
        self.nc = nc
        self.base = base
        self.off = base
        self.limit = limit
        self.cnt = 0
        self.peak = base

    def reset(self):
        self.off = self.base

    def __call__(self, shape, dtype, name="t"):
        nbytes = int(np.prod(shape[1:])) * (4 if dtype == F32 else 2)
        nbytes = (nbytes + 63) // 64 * 64
        assert self.off + nbytes <= self.limit, ("SBUF overflow", name, self.off, nbytes, self.limit)
        self.cnt += 1
        h = self.nc.alloc_sbuf_tensor_at("%s_%d" % (name, self.cnt), list(shape), dtype, offset=self.off)
        self.off += nbytes
        self.peak = max(self.peak, self.off)
        return h.ap()


def _bias_index_tables():
    R = 16
    cls = [(0, [0, 1, 2, 3]), (1, [-1, 0, 1, 2]), (3, [-2, -1, 0, 1, 2]),
           (R // 2 - 2, [-2, -1, 0, 1]), (R // 2 - 1, [-3, -2, -1, 0])]
    DR = np.zeros((NMAT, 128, 128), np.int64)
    DC = np.zeros((NMAT, 128, 128), np.int64)
    VAL = np.zeros((NMAT, 128, 128), bool)
    kk = np.arange(128)
    krl, kc = kk // 64, kk % 64
    qrl, qc = kk // 64, kk % 64
    m = 0
    for (i, deltas) in cls:
        for dl in deltas:
            j = i + dl
            r = (2 * i + qrl)[None, :]
            kr = (2 * j + krl)[:, None]
            rs = np.clip(r - 4, 0, R - 8)
            vrow = (kr >= rs) & (kr < rs + 8)
            cs = np.clip(qc - 8, 0, 48)[None, :]
            vcol = (kc[:, None] >= cs) & (kc[:, None] < cs + 16)
            DR[m] = np.clip(kr - r + 7, 0, 14)
            DC[m] = np.clip(kc[:, None] - qc[None, :] + 15, 0, 30)
            VAL[m] = vrow & vcol
            m += 1
    assert m == NMAT
    return DR, DC, VAL


def _pair_class(i, npairs):
    if i == 0:
        return [0, 1, 2, 3], 0
    if i == 1:
        return [0, 1, 2, 3], 4
    if i == npairs - 2:
        return [i - 2, i - 1, i, i + 1], 13
    if i == npairs - 1:
        return [i - 3, i - 2, i - 1, i], 17
    return [i - 2, i - 1, i, i + 1, i + 2], 8


def _dft_tables(S):
    nS, nK = S // 128, S // 512
    idx = (np.arange(S, dtype=np.int64)[:, None] * np.arange(S, dtype=np.int64)[None, :]) % S
    ang = 2.0 * np.pi * idx.astype(np.float64) / S
    out = []
    for tab in (np.cos(ang), np.sin(ang)):
        t = tab.reshape(nS, 128, nK, 512).transpose(2, 1, 0, 3).reshape(nK, 128, nS * 512)[:nK // 2]
        out.append(np.ascontiguousarray(t).astype(ml_dtypes.bfloat16))
    a64 = 2.0 * np.pi * ((np.arange(64)[:, None] * np.arange(64)[None, :]) % 64) / 64.0
    sc = 1.0 / np.sqrt(S * 64.0)
    bdc = np.zeros((128, 128))
    bds = np.zeros((128, 128))
    for g in range(2):
        bdc[g * 64:(g + 1) * 64, g * 64:(g + 1) * 64] = np.cos(a64) * sc
        bds[g * 64:(g + 1) * 64, g * 64:(g + 1) * 64] = -np.sin(a64) * sc
    bdc2 = np.concatenate([bdc, bdc], axis=1)
    bds2 = np.concatenate([bds, -bds], axis=1)
    return out[0], out[1], bdc2.astype(ml_dtypes.bfloat16), bds2.astype(ml_dtypes.bfloat16)


def build_program(seq_lens, depth):
    nc = bass.Bass("TRN2", target_bir_lowering=False)
    L = depth
    T = sum(seq_lens)
    seq_off = [sum(seq_lens[:i]) for i in range(len(seq_lens))]
    gpad_off = [sum(s + 2 * GP for s in seq_lens[:i]) for i in range(len(seq_lens))]
    TP = sum(s + 2 * GP for s in seq_lens)
    Sset = sorted(set(seq_lens))
    SMAX = max(seq_lens)
    NSMAX = SMAX // 128

    def din(name, shape, dt=F32):
        return nc.dram_tensor(name, list(shape), dt, kind="ExternalInput").ap()

    def dtmp(name, shape, dt):
        return nc.dram_tensor(name, list(shape), dt).ap()

    xin = din("xin", [T, D])
    norm1_gT = din("norm1_gT", [L, 128, NKC])
    w_in = din("w_in", [L, D, DIN])
    biasmat = din("biasmat", [L, 128, NH * NMAT * 128], BF16)
    conv_wT = din("conv_wT", [L, 256, CW])
    conv_b = din("conv_b", [L, 256])
    conv_ln_g = din("conv_ln_g", [L, 256])
    conv_ln_b = din("conv_ln_b", [L, 256])
    mix_gT = din("mix_gT", [L, 128, NKC])
    w_out = din("w_out", [L, D, D])
    norm2_gT = din("norm2_gT", [L, 128, NKC])
    w_ff_in = din("w_ff_in", [L, D, DFF])
    w_ff_out = din("w_ff_out", [L, DFF, D])
    final_norm_g = din("final_norm_g", [1, D])
    ident_d = din("ident", [128, 128], BF16)
    cos_d, sin_d, bdc_d, bds_d = {}, {}, {}, {}
    for S in Sset:
        cos_d[S] = din("cos%d" % S, [S // 1024, 128, (S // 128) * 512], BF16)
        sin_d[S] = din("sin%d" % S, [S // 1024, 128, (S // 128) * 512], BF16)
        bdc_d[S] = din("bdc%d" % S, [128, 256], BF16)
        bds_d[S] = din("bds%d" % S, [128, 256], BF16)
    jrev_d = din("jrev", [128, 128], BF16)
    alt_d = din("alt", [128, 2], BF16)
    y = nc.dram_tensor("y", [T, D], F32, kind="ExternalOutput").ap()

    xa_d = dtmp("xa_d", [T, D], F32)
    xb_d = dtmp("xb_d", [T, D], F32)
    qz_d = dtmp("qz_d", [T // 128, 128, NH * 128], BF16)
    kt_d = dtmp("kt_d", [4, 128, T], BF16)
    v_d = dtmp("v_d", [T, NH * VW], BF16)
    a_d = dtmp("a_d", [T, 256], BF16)
    gt_d = dtmp("gt_d", [2, 128, TP], BF16)
    of_d = dtmp("of_d", [T, 256], BF16)

    TR = Tracker(nc)
    PS = nc.alloc_psum_tensor("ps", [128, 4096], F32).ap()
    PSB = PS.bitcast(BF16)
    bankB = [Buf("bank%d" % i) for i in range(8)]

    def bank(i):
        return PS[:, i * 512:(i + 1) * 512]

    def bankb(i):
        return PSB[:, i * 1024:(i + 1) * 1024]

    SB_BASE = 16512
    SB_LIMIT = 229344
    CA = Alloc(nc, SB_BASE, SB_BASE + 8192)
    ident = CA([128, 128], BF16, "ident")
    identB = Buf("ident")
    mhalf = CA([128, 16], F32, "mhalf")
    zpad = CA([128, 2, GP], BF16, "zpad")
    epsc = CA([128, 2], F32, "epsc")
    constB = Buf("const")
    bdc, bds = {}, {}
    for S in Sset:
        bdc[S] = CA([128, 256], BF16, "bdc")
        bds[S] = CA([128, 256], BF16, "bds")
    jrev = CA([128, 128], BF16, "jrev")
    alt = CA([128, 2], BF16, "alt")
    PA = Alloc(nc, CA.off, SB_LIMIT)

    def mm(out, lhsT, rhs, start, stop, r, w, sig):
        TR.op("pe", lambda e, o=out, a=lhsT, b=rhs, s0=start, s1=stop: e.matmul(o, lhsT=a, rhs=b, start=s0, stop=s1),
              r=r, w=w, sig=sig)

    def tr(out, in_, r, w, sig):
        TR.op("pe", lambda e, o=out, a=in_: e.transpose(o, a, ident), r=list(r) + [identB], w=w, sig=sig)

    def act(out, in_, func, r, w, scale=1.0, accum=None):
        if accum is None:
            TR.op("act", lambda e, o=out, i=in_, f=func, s=scale: e.activation(out=o, in_=i, func=f, scale=s), r=r, w=w)
        else:
            TR.op("act", lambda e, o=out, i=in_, f=func, s=scale, a=accum: e.activation(out=o, in_=i, func=f, scale=s, accum_out=a),
                  r=r, w=w)

    def vcopy(qn, out, in_, r, w):
        TR.op(qn, lambda e, o=out, i=in_: e.tensor_copy(out=o, in_=i), r=r, w=w)

    def tt(qn, out, in0, in1, op, r, w):
        TR.op(qn, lambda e, o=out, a=in0, b=in1, p=op: e.tensor_tensor(out=o, in0=a, in1=b, op=p), r=r, w=w)

    def ts(qn, out, in0, s1, s2, op0, op1, r, w):
        if s2 is None:
            TR.op(qn, lambda e, o=out, a=in0, x=s1, p=op0: e.tensor_scalar(out=o, in0=a, scalar1=x, scalar2=None, op0=p), r=r, w=w)
        else:
            TR.op(qn, lambda e, o=out, a=in0, x=s1, y=s2, p=op0, p1=op1: e.tensor_scalar(out=o, in0=a, scalar1=x, scalar2=y, op0=p, op1=p1),
                  r=r, w=w)

    def stt(out, in0, scalar, in1, op0, op1, r, w):
        TR.op("dve", lambda e, o=out, a=in0, s=scalar, b=in1, p0=op0, p1=op1:
              e.scalar_tensor_tensor(out=o, in0=a, scalar=s, in1=b, op0=p0, op1=p1), r=r, w=w)

    def rsqrt_pool(out, in_, mul, add, r, w):
        n = out.shape[1]
        ts("pool", out, in_, mul, add, ALU.mult, ALU.add, r=r, w=w)
        tt("pool", out, out, mhalf[:, 0:n], ALU.pow, r=list(w) + [constB], w=w)

    def bcast_load(dst, src_row, buf, dsem):
        TR.dma("sp", dst, src_row.partition_broadcast(128), w=[buf], dsem=dsem)

    ebal = [0]

    def evq():
        ebal[0] += 1
        return "act" if ebal[0] % 2 else "dve"

    def evac(qn, out, in_, r, w, scale=None):
        if qn == "act":
            act(out, in_, AF.Copy, r, w, scale=1.0 if scale is None else scale)
        elif scale is None:
            vcopy(qn, out, in_, r, w)
        else:
            ts(qn, out, in_, scale, None, ALU.mult, None, r, w)

    cds = TR.new_dsem()
    TR.dma("sp", ident, ident_d, w=[identB], dsem=cds)
    for S in Sset:
        TR.dma("sp", bdc[S], bdc_d[S], w=[constB], dsem=cds)
        TR.dma("sp", bds[S], bds_d[S], w=[constB], dsem=cds)
    TR.dma("sp", jrev, jrev_d, w=[constB], dsem=cds)
    TR.dma("sp", alt, alt_d, w=[constB], dsem=cds)
    TR.op("pool", lambda e: e.memset(mhalf, -0.5), w=[constB])
    TR.op("pool", lambda e: e.memset(zpad, 0.0), w=[constB])
    TR.op("pool", lambda e: e.memset(epsc, RMS_EPS), w=[constB])
    for qi, S in enumerate(seq_lens):
        for side in range(2):
            o = gpad_off[qi] + (0 if side == 0 else GP + S)
            TR.dma("sp", gt_d[:, :, o:o + GP].rearrange("c p t -> p c t"), zpad, r=[constB], dsem=cds)
    TR.barrier()

    tiles = []
    for qi, S in enumerate(seq_lens):
        for t in range(S // 512):
            tiles.append((qi, t, seq_off[qi] + t * 512))
    NT = len(tiles)

    def load_fold_weight(W, WBs, src_rows, gt, gB, dsem, nsplit, width):
        for kc in range(len(WBs)):
            if nsplit > 1:
                TR.dma("pool", W[:, kc, :].rearrange("p (a b) -> p a b", a=nsplit),
                       src_rows(kc).rearrange("p (a b) -> p a b", a=nsplit), w=[WBs[kc]], dsem=dsem)
            else:
                TR.dma("pool", W[:, kc, :], src_rows(kc), w=[WBs[kc]], dsem=dsem)
            if gt is not None:
                ts("dve", W[:, kc, :], W[:, kc, :], gt[:, kc:kc + 1], None, ALU.mult, None, r=[WBs[kc], gB], w=[WBs[kc]])

    def load_fold_cols(W, src2d, blocks, gt, gB, dsem=None):
        bufs = {}
        for (c0, c1) in blocks:
            b = Buf()
            bufs[(c0, c1)] = b
            TR.dma("pool", W[:, :, c0:c1], src2d[:, c0:c1].rearrange("(k p) c -> p k c", p=128), w=[b], dsem=TR.new_swsem())
            folds.append(lambda W=W, c0=c0, c1=c1, b=b, gt=gt, gB=gB: tt(
                "pool", W[:, :, c0:c1], W[:, :, c0:c1], gt.unsqueeze(2).to_broadcast([128, NKC, c1 - c0]), ALU.mult,
                r=[b, gB], w=[b]))
        return bufs

    folds = []

    def run_folds():
        for f in folds:
            f()
        del folds[:]

    def phase_A(l):
        PA.reset()
        TR.dsem_next = 0
        src = xin if l == 0 else xa_d
        Win = PA([128, NKC, DIN], BF16, "Win")
        g1t = PA([128, NKC], F32, "g1t")
        g1B = Buf()
        TR.dma("sp", g1t, norm1_gT[l], w=[g1B], dsem=TR.new_dsem())
        order = [1, 2, 3, 4, 8, 7, 5, 6, 0]
        wcb = {}
        WC = lambda col: wcb[((col // 256) * 256, (col // 256) * 256 + 256)]
        xs = [PA([128, D], F32, "x") for _ in range(8)]
        xB = [Buf() for _ in range(8)]
        xds = [TR.new_dsem() for _ in range(8)]
        hs = [PA([128, D], BF16, "h") for _ in range(8)]
        hB = [Buf() for _ in range(8)]
        ss = PA([128, 8], F32, "ss")
        rs = PA([128, 8], F32, "rs")
        ssB = [Buf(), Buf()]
        rsB = [Buf(), Buf()]
        hT = [PA([128, NKC, 512], BF16, "hT") for _ in range(2)]
        hTB = [[Buf() for _ in range(4)] for _ in range(2)]
        Qz = [PA([128, NH, 512], BF16, "Qz") for _ in range(2)]
        Kst = [PA([128, 4, 512], BF16, "Kst") for _ in range(2)]
        Gst = [PA([128, 2, 512], BF16, "Gst") for _ in range(2)]
        Vst = [PA([128, 4, NH, VW], BF16, "Vst") for _ in range(2)]
        Ast = [PA([128, 4, 256], BF16, "Ast") for _ in range(2)]
        QzB, KstB, GstB, VstB, AstB = ([Buf(), Buf()] for _ in range(5))
        stds = [[TR.new_dsem() for _ in range(5)] for _ in range(2)]
        et = [PA([128, 512], F32, "et") for _ in range(2)]
        etB = [Buf(), Buf()]
        for sl in range(2):
            TR.op("pool", lambda e, a=Qz[sl]: e.memset(a, 0.0), w=[QzB[sl]])
            TR.op("pool", lambda e, a=Vst[sl]: e.memset(a, 1.0), w=[VstB[sl]])
        bctr = [0]

        def nbank():
            b = bctr[0] % 6
            bctr[0] += 1
            return b

        def FE_el(k):
            qi, t, tok0 = tiles[k]
            sl = k % 2
            for s in range(4):
                i = sl * 4 + s
                TR.dma("sp", xs[i], src[tok0 + s * 128: tok0 + (s + 1) * 128, :], w=[xB[i]], dsem=xds[i])
            for s in range(4):
                i = sl * 4 + s
                act(hs[i], xs[i], AF.Square, r=[xB[i]], w=[hB[i], ssB[sl]], accum=ss[:, i:i + 1])
            rsqrt_pool(rs[:, sl * 4:sl * 4 + 4], ss[:, sl * 4:sl * 4 + 4], 1.0 / D, RMS_EPS, r=[ssB[sl]], w=[rsB[sl]])
            for s in range(4):
                i = sl * 4 + s
                ts("dve", hs[i], xs[i], rs[:, i:i + 1], None, ALU.mult, None, r=[xB[i], rsB[sl]], w=[hB[i]])

        def FE_pe(k):
            sl = k % 2
            for pr in range(4):
                bk = 6 + (pr % 2)
                for kk in range(2):
                    kc = pr * 2 + kk
                    for s in range(4):
                        i = sl * 4 + s
                        tr(bankb(bk)[:, kk * 512 + s * 128: kk * 512 + (s + 1) * 128], hs[i][:, kc * 128:(kc + 1) * 128],
                           r=[hB[i]], w=[bankB[bk]], sig=(kk == 1 and s == 3))
                evac(evq(), hT[sl][:, pr * 2:pr * 2 + 2, :].rearrange("p a b -> p (a b)"), bankb(bk), r=[bankB[bk]], w=[hTB[sl][pr]])

        def fm_group(sl, col):
            bk = nbank()
            for kc in range(NKC):
                mm(bank(bk), Win[:, kc, col:col + 128], hT[sl][:, kc, :], kc == 0, kc == NKC - 1,
                   r=[WC(col), hTB[sl][kc // 2]], w=[bankB[bk]], sig=(kc == NKC - 1))
            return bk

        def BE(k, hook):
            qi, t, tok0 = tiles[k]
            sl = k % 2
            for c in range(4):
                bk = fm_group(sl, 256 + c * 128)
                act(Qz[sl][0:64, 2 * c, :], bank(bk)[0:64, :], AF.Copy, r=[bankB[bk]], w=[QzB[sl]], scale=0.125)
                act(Qz[sl][64:128, 2 * c + 1, :], bank(bk)[64:128, :], AF.Copy, r=[bankB[bk]], w=[QzB[sl]], scale=0.125)
            for c in range(4):
                bk = fm_group(sl, 768 + c * 128)
                vcopy("dve", Kst[sl][:, c, :], bank(bk), r=[bankB[bk]], w=[KstB[sl]])
            if hook is not None:
                hook()
            for c in range(2):
                bk = fm_group(sl, 2048 + c * 128)
                act(et[c], bank(bk), AF.Exp, r=[bankB[bk]], w=[etB[c]], scale=-1.0)
                ts("pool", et[c], et[c], 1.0, 1.0, ALU.add, ALU.mult, r=[etB[c]], w=[etB[c]])
                TR.op("dve", lambda e, a=et[c]: e.reciprocal(out=a, in_=a), r=[etB[c]], w=[etB[c]])
                bk = fm_group(sl, 1792 + c * 128)
                tt("dve", Gst[sl][:, c, :], bank(bk), et[c], ALU.mult, r=[bankB[bk], etB[c]], w=[GstB[sl]])
            for s in range(4):
                bk = nbank()
                for kc in range(NKC):
                    mm(bank(bk), hT[sl][:, kc, s * 128:(s + 1) * 128], Win[:, kc, 1280:1792], kc == 0, kc == NKC - 1,
                       r=[WC(1280), WC(1536), hTB[sl][kc // 2]], w=[bankB[bk]], sig=(kc == NKC - 1))
                act(Vst[sl][:, s, :, 0:64], bank(bk).rearrange("p (h d) -> p h d", h=NH), AF.Copy, r=[bankB[bk]], w=[VstB[sl]])
                bk = nbank()
                for kc in range(NKC):
                    mm(bank(bk)[:, 0:256], hT[sl][:, kc, s * 128:(s + 1) * 128], Win[:, kc, 0:256], kc == 0, kc == NKC - 1,
                       r=[WC(0), hTB[sl][kc // 2]], w=[bankB[bk]], sig=(kc == NKC - 1))
                vcopy("dve", Ast[sl][:, s, :], bank(bk)[:, 0:256], r=[bankB[bk]], w=[AstB[sl]])
            ds = stds[sl]
            TR.dma("sp", qz_d[tok0 // 128: tok0 // 128 + 4, :, :].rearrange("s p (h t) -> p h s t", h=NH),
                   Qz[sl].rearrange("p h (s t) -> p h s t", s=4), r=[QzB[sl]], dsem=ds[0])
            TR.dma("sp", kt_d[:, :, tok0:tok0 + 512].rearrange("c p t -> p c t"), Kst[sl], r=[KstB[sl]], dsem=ds[1])
            go = gpad_off[qi] + GP + t * 512
            TR.dma("sp", gt_d[:, :, go:go + 512].rearrange("c p t -> p c t"), Gst[sl], r=[GstB[sl]], dsem=ds[2])
            TR.dma("sp", v_d[tok0:tok0 + 512, :].rearrange("(s p) d -> p s d", p=128), Vst[sl].rearrange("p s h d -> p s (h d)"),
                   r=[VstB[sl]], dsem=ds[3])
            TR.dma("sp", a_d[tok0:tok0 + 512, :].rearrange("(s p) d -> p s d", p=128), Ast[sl], r=[AstB[sl]], dsem=ds[4])

        wcb.update(load_fold_cols(Win, w_in[l], [(b * 256, (b + 1) * 256) for b in order[:1]], g1t, g1B))
        FE_el(0)
        wcb.update(load_fold_cols(Win, w_in[l], [(b * 256, (b + 1) * 256) for b in order[1:]], g1t, g1B))
        run_folds()
        FE_pe(0)
        if NT > 1:
            FE_el(1)
        for k in range(NT):
            if k + 1 < NT:
                FE_pe(k + 1)
            BE(k, (lambda kk=k: FE_el(kk + 2)) if k + 2 < NT else None)
        TR.barrier()

    def phase_F(l):
        PA.reset()
        TR.dsem_next = 0
        NSEQ = len(seq_lens)
        Atm = [PA([128, seq_lens[qi] // 128, 256], BF16, "Atm") for qi in range(NSEQ)]
        AtmB = [Buf() for _ in range(NSEQ)]
        for qi in range(NSEQ):
            S = seq_lens[qi]
            TR.dma("sp", Atm[qi], a_d[seq_off[qi]:seq_off[qi] + S, :].rearrange("(t p) c -> p t c", p=128),
                   w=[AtmB[qi]], dsem=TR.new_dsem())
        cosb = [PA([128, NSMAX * 512], BF16, "cos") for _ in range(2)]
        sinb = [PA([128, NSMAX * 512], BF16, "sin") for _ in range(2)]
        cosB = [Buf(), Buf()]
        sinB = [Buf(), Buf()]
        cds_ = [TR.new_dsem(), TR.new_dsem()]
        sds_ = [TR.new_dsem(), TR.new_dsem()]
        PQ = [[PA([128, 512], BF16, "pq") for _ in range(4)] for _ in range(2)]
        PQB = [[Buf() for _ in range(4)] for _ in range(2)]
        sqf = [PA([128, 512], F32, "sqf") for _ in range(4)]
        ssf = [PA([128, 8], F32, "ssf") for _ in range(4)]
        rsf = [PA([128, 8], F32, "rsf") for _ in range(4)]
        om = [PA([128, 256], BF16, "om") for _ in range(4)]
        sqfB, ssfB, rsfB, omB = ([Buf() for _ in range(4)] for _ in range(4))
        ofd = [PA([128, 4, 256], BF16, "ofd") for _ in range(2)]
        ofm = [PA([128, 4, 256], BF16, "ofm") for _ in range(2)]
        ofdB = [Buf(), Buf()]
        ofmB = [Buf(), Buf()]
        odds = [TR.new_dsem(), TR.new_dsem()]
        omds = [TR.new_dsem(), TR.new_dsem()]
        omds2 = [TR.new_dsem(), TR.new_dsem()]
        PH = PA([128, 2], BF16, "PH")
        sq1 = PA([128, 256], F32, "sq1")
        ss1 = PA([128, 4], F32, "ss1")
        rs1 = PA([128, 4], F32, "rs1")
        oh = PA([128, 256], BF16, "oh")
        PHB, sq1B, ss1B, rs1B, ohB = (Buf() for _ in range(5))
        ohds = TR.new_dsem()

        steps = []
        jn = 0
        for S in sorted(set(seq_lens), key=lambda v: seq_lens.index(v)):
            qis = [qi for qi in range(NSEQ) if seq_lens[qi] == S]
            for kt in range(S // 1024):
                for ii, qi in enumerate(qis):
                    steps.append((S, kt, qi, ii == 0, jn))
                jn += 1

        def groups(n, part):
            S, kt, qi, first, j = steps[n]
            nS = S // 128
            sl = n % 2
            tl = j % 2
            a = Atm[qi]
            if first and part == 0:
                TR.dma("sp", cosb[tl][:, 0:nS * 512], cos_d[S][kt], w=[cosB[tl]], dsem=cds_[tl])
                TR.dma("sp", sinb[tl][:, 0:nS * 512], sin_d[S][kt], w=[sinB[tl]], dsem=sds_[tl])
            for c in range(2):
                if c != part:
                    continue
                for which in range(2):
                    bk = sl * 4 + c * 2 + which
                    tab = cosb[tl] if which == 0 else sinb[tl]
                    tB = cosB[tl] if which == 0 else sinB[tl]
                    for st in range(nS):
                        mm(bank(bk), a[:, st, c * 128:(c + 1) * 128], tab[:, st * 512:(st + 1) * 512], st == 0, st == nS - 1,
                           r=[AtmB[qi], tB], w=[bankB[bk]], sig=(st == nS - 1))

        def evacs(n):
            sl = n % 2
            for c in range(2):
                for which in range(2):
                    bk = sl * 4 + c * 2 + which
                    evac("act" if which == 0 else "dve", PQ[sl][c * 2 + which], bank(bk), r=[bankB[bk]], w=[PQB[sl][c * 2 + which]])

        def stage2_main(n):
            S, kt, qi, first, j = steps[n]
            sl = n % 2
            for sub in range(4):
                bk = sl * 4 + sub
                u = sub
                for c in range(2):
                    o = bank(bk)[:, c * 256:(c + 1) * 256]
                    mm(o, PQ[sl][c * 2][:, sub * 128:(sub + 1) * 128], bdc[S], True, False,
                       r=[PQB[sl][c * 2], constB], w=[bankB[bk]], sig=False)
                    mm(o, PQ[sl][c * 2 + 1][:, sub * 128:(sub + 1) * 128], bds[S], False, True,
                       r=[PQB[sl][c * 2 + 1], constB], w=[bankB[bk]], sig=(c == 1))
                O5 = bank(bk).rearrange("p (c m g d) -> p c m g d", c=2, m=2, g=2)
                rs5 = rsf[u].rearrange("p (c m g) -> p c m g", c=2, m=2)
                act(sqf[u], bank(bk), AF.Square, r=[bankB[bk]], w=[sqfB[u]])
                TR.op("dve", lambda e, o=ssf[u], i=sqf[u]: e.tensor_reduce(out=o, in_=i.rearrange("p (a d) -> p a d", a=8), axis=AX.X, op=ALU.add),
                      r=[sqfB[u]], w=[ssfB[u]])
                TR.op("act", lambda e, o=rsf[u], i=ssf[u]: e.activation(out=o, in_=i, func=AF.Sqrt, bias=epsc[:, 0:1], scale=1.0 / HD),
                      r=[ssfB[u], constB], w=[rsfB[u]])
                TR.op("dve", lambda e, o=rsf[u]: e.reciprocal(out=o, in_=o), r=[rsfB[u]], w=[rsfB[u]])
                tt("dve", ofd[sl][:, sub, :].rearrange("p (c g d) -> p c g d", c=2, g=2), O5[:, :, 0],
                   rs5[:, :, 0].unsqueeze(3).to_broadcast([128, 2, 2, HD]), ALU.mult, r=[bankB[bk], rsfB[u]], w=[ofdB[sl]])
                tt("dve", om[u].rearrange("p (c g d) -> p c g d", c=2, g=2), O5[:, :, 1],
                   rs5[:, :, 1].unsqueeze(3).to_broadcast([128, 2, 2, HD]), ALU.mult, r=[bankB[bk], rsfB[u]], w=[omB[u]])

        def stage2_tail(n):
            S, kt, qi, first, j = steps[n]
            nS = S // 128
            sl = n % 2
            k0 = kt * 512
            for sub in range(4):
                bk = sl * 4 + sub
                u = sub
                mm(bank(bk)[:, 0:256], jrev, om[u], True, True, r=[omB[u], constB], w=[bankB[bk]], sig=True)
                evac("act" if sub % 2 == 0 else "dve", ofm[sl][:, 3 - sub, :], bank(bk)[:, 0:256], r=[bankB[bk]], w=[ofmB[sl]])
            t0 = seq_off[qi]
            TR.dma("sp", of_d[t0 + k0:t0 + k0 + 512, :].rearrange("(s p) d -> p s d", p=128), ofd[sl], r=[ofdB[sl]], dsem=odds[sl])
            base = t0 + S - k0 - 511
            if kt > 0:
                TR.dma("sp", of_d[base:base + 512, :].rearrange("(s p) d -> p s d", p=128), ofm[sl], r=[ofmB[sl]], dsem=omds[sl])
            else:
                TR.dma("sp", of_d[base:base + 384, :].rearrange("(s p) d -> p s d", p=128), ofm[sl][:, 0:3, :], r=[ofmB[sl]], dsem=omds[sl])
                TR.dma("sp", of_d[base + 384:base + 511, :], ofm[sl][0:127, 3, :], r=[ofmB[sl]], dsem=omds2[sl])
                bk = sl * 4
                a = Atm[qi]
                for c in range(2):
                    for st in range(nS):
                        mm(bank(bk)[:, c:c + 1], a[:, st, c * 128:(c + 1) * 128], alt[:, 0:1], st == 0, st == nS - 1,
                           r=[AtmB[qi], constB], w=[bankB[bk]], sig=(c == 1 and st == nS - 1))
                evac("act", PH, bank(bk)[:, 0:2], r=[bankB[bk]], w=[PHB])
                for c in range(2):
                    mm(bank(bk)[0:1, 256 + c * 128:256 + (c + 1) * 128], PH[:, c:c + 1], bdc[S][:, 0:128], True, True,
                       r=[PHB, constB], w=[bankB[bk]], sig=(c == 1))
                O1 = bank(bk)[0:1, 256:512]
                act(sq1[0:1, :], O1, AF.Square, r=[bankB[bk]], w=[sq1B])
                TR.op("dve", lambda e: e.tensor_reduce(out=ss1[0:1, :], in_=sq1[0:1, :].rearrange("p (g d) -> p g d", g=4), axis=AX.X, op=ALU.add),
                      r=[sq1B], w=[ss1B])
                ts("pool", rs1[0:1, :], ss1[0:1, :], 1.0 / HD, RMS_EPS, ALU.mult, ALU.add, r=[ss1B], w=[rs1B])
                tt("pool", rs1[0:1, :], rs1[0:1, :], mhalf[0:1, 0:4], ALU.pow, r=[rs1B, constB], w=[rs1B])
                tt("dve", oh[0:1, :].rearrange("p (g d) -> p g d", g=4), O1.rearrange("p (g d) -> p g d", g=4),
                   rs1[0:1, :].unsqueeze(2).to_broadcast([1, 4, HD]), ALU.mult, r=[bankB[bk], rs1B], w=[ohB])
                TR.dma("sp", of_d[t0 + S // 2:t0 + S // 2 + 1, :], oh[0:1, :], r=[ohB], dsem=ohds)

        NJ = len(steps)
        groups(0, 0)
        groups(0, 1)
        evacs(0)
        for n in range(NJ):
            if n + 1 < NJ:
                groups(n + 1, 0)
            stage2_main(n)
            if n + 1 < NJ:
                groups(n + 1, 1)
            stage2_tail(n)
            if n + 1 < NJ:
                evacs(n + 1)
        TR.barrier()

    def phase_B(l):
        PA.reset()
        TR.dsem_next = 0
        src = xin if l == 0 else xa_d
        NCH = NSMAX // 4
        KT = PA([128, NCH, 4, 512], BF16, "KT")
        VA = PA([128, NSMAX, NH, VW], BF16, "VA")
        KVB = [Buf() for _ in range(NCH)]
        kds = [TR.new_dsem() for _ in range(NCH)]
        vds = [TR.new_dsem() for _ in range(NCH)]
        E = PA([128, NH * NMAT, 128], BF16, "E")
        EB = [Buf() for _ in range(NH)]
        eds = [TR.new_dsem() for _ in range(NH)]

        def load_E():
            for h in range(NH):
                Eh = E[:, h * NMAT:(h + 1) * NMAT, :].rearrange("p m q -> p (m q)")
                TR.dma("sp", Eh, biasmat[l, :, h * NMAT * 128:(h + 1) * NMAT * 128], w=[EB[h]], dsem=eds[h])
        Wo = PA([128, NKC, D], BF16, "Wo")
        mgt = PA([128, NKC], F32, "mgt")
        mgB = Buf()
        TR.dma("sp", mgt, mix_gT[l], w=[mgB], dsem=TR.new_dsem())
        WoG = [Buf(), Buf()]
        for g4 in range(2):
            TR.dma("pool", Wo[:, g4 * 4:(g4 + 1) * 4, :], w_out[l, g4 * 512:(g4 + 1) * 512, :].rearrange("(k p) c -> p k c", p=128),
                   w=[WoG[g4]], dsem=TR.new_swsem())
            tt("dve", Wo[:, g4 * 4:(g4 + 1) * 4, :], Wo[:, g4 * 4:(g4 + 1) * 4, :],
               mgt[:, g4 * 4:(g4 + 1) * 4].unsqueeze(2).to_broadcast([128, 4, D]), ALU.mult, r=[WoG[g4], mgB], w=[WoG[g4]])
        WoB = [WoG[kc // 4] for kc in range(NKC)]
        cw = PA([128, 2, CW], F32, "cw")
        cwB = Buf()
        TR.dma("sp", cw, conv_wT[l].rearrange("(c p) j -> p c j", p=128), w=[cwB], dsem=TR.new_dsem())
        Dg = PA([128, 2, CW, 128], BF16, "Dg")
        DgB = Buf()
        for c in range(2):
            for j in range(CW):
                ts("dve", Dg[:, c, j, :], ident, cw[:, c, j:j + 1], None, ALU.mult, None, r=[identB, cwB], w=[DgB])
        cbb = PA([128, 256], F32, "cbb")
        lgb = PA([128, 256], F32, "lgb")
        lbb = PA([128, 256], F32, "lbb")
        vecB = Buf()
        vds_ = TR.new_dsem()
        bcast_load(cbb, conv_b[l:l + 1, :], vecB, vds_)
        bcast_load(lgb, conv_ln_g[l:l + 1, :], vecB, vds_)
        bcast_load(lbb, conv_ln_b[l:l + 1, :], vecB, vds_)
        LAG = 2
        NQ = 3
        Qz = [PA([128, NH, 128], BF16, "Qz") for _ in range(NQ)]
        QzB = [Buf() for _ in range(NQ)]
        qds = [TR.new_dsem() for _ in range(NQ)]
        gTt = [PA([128, 2, 512 + 2 * GP], BF16, "gTt") for _ in range(2)]
        gTB = [Buf(), Buf()]
        gds = [TR.new_dsem(), TR.new_dsem()]
        xs = [PA([128, D], F32, "x") for _ in range(NQ)]
        xB = [Buf() for _ in range(NQ)]
        xds = [TR.new_dsem() for _ in range(NQ)]
        sds_ = [TR.new_dsem() for _ in range(NQ)]
        obf = [PA([128, D], BF16, "obf") for _ in range(NQ)]
        obfB = [Buf() for _ in range(NQ)]
        ofds = [TR.new_dsem() for _ in range(NQ)]
        ocat = [PA([128, 768], F32, "ocat") for _ in range(2)]
        ocA = [Buf(), Buf()]
        ocC = [Buf(), Buf()]
        sq = PA([128, 768], F32, "sq")
        sqB = Buf()
        ssg = PA([128, 12], F32, "ssg")
        rsg = PA([128, 12], F32, "rsg")
        ssgB, rsgB = Buf(), Buf()
        rden = PA([128, NH], F32, "rden")
        rdenB = Buf()
        oT = [PA([128, NKC, 128], BF16, "oT") for _ in range(2)]
        oTB = [[Buf(), Buf()], [Buf(), Buf()]]
        P0 = [PA([128, 640], BF16, "P0") for _ in range(2)]
        P1 = [PA([128, 640], BF16, "P1") for _ in range(2)]
        P0B = [Buf(), Buf()]
        P1B = [Buf(), Buf()]
        yb = [PA([128, 256], F32, "yb") for _ in range(2)]
        ybB = [Buf(), Buf()]
        yn = PA([128, 256], F32, "yn")
        ey = PA([128, 256], F32, "ey")
        ynB, eyB = Buf(), Buf()
        bst = PA([128, 6], F32, "bst")
        mv = PA([128, 2], F32, "mv")
        rsl = PA([128, 1], F32, "rsl")
        bstB, mvB, rslB = Buf(), Buf(), Buf()

        MB = [Buf(), Buf()]
        XB = Buf()
        OB = [Buf(), Buf()]
        YB = Buf()
        TB = Buf()
        PB7 = Buf()
        O = PS[:, 3 * 512:5 * 512].rearrange("p (h d) -> p h d", h=NH)
        Y = PS[:, 5 * 512:5 * 512 + 256]

        subs = []
        for qi, S in enumerate(seq_lens):
            for i in range(S // 128):
                subs.append((qi, i, seq_off[qi] + i * 128))
        NSUB = len(subs)

        def load_kv(qi, ch):
            tok0 = seq_off[qi] + ch * 512
            TR.dma("sp", KT[:, ch, :, :], kt_d[:, :, tok0:tok0 + 512].rearrange("c p t -> p c t"), w=[KVB[ch]], dsem=kds[ch])
            TR.dma("sp", VA[:, ch * 4:(ch + 1) * 4, :, :].rearrange("p s h d -> p s (h d)"),
                   v_d[tok0:tok0 + 512, :].rearrange("(s p) d -> p s d", p=128), w=[KVB[ch]], dsem=vds[ch])

        def FE(n):
            qi, i, tok0 = subs[n]
            sl = n % NQ
            if i % 4 == 0:
                t = i // 4
                go = gpad_off[qi] + t * 512
                TR.dma("sp", gTt[t % 2], gt_d[:, :, go:go + 512 + 2 * GP].rearrange("c p t -> p c t"), w=[gTB[t % 2]], dsem=gds[t % 2])
            TR.dma("sp", Qz[sl].rearrange("p h t -> p (h t)"), qz_d[tok0 // 128], w=[QzB[sl]], dsem=qds[sl])

        def S1(n, fill=None):
            qi, i, tok0 = subs[n]
            S = seq_lens[qi]
            npairs = S // 128
            sl = n % NQ
            u = n % 2
            js, m0 = _pair_class(i, npairs)
            nj = len(js)
            n4 = min(nj, 4)
            t = i // 4
            sub = i % 4
            g = gTt[t % 2]
            convq = [(c, j) for c in range(2) for j in range(CW)]
            cpos = [0]
            ydone = [False]

            def conv_some(cnt):
                for _ in range(cnt):
                    if cpos[0] >= len(convq):
                        return
                    c, j = convq[cpos[0]]
                    cpos[0] += 1
                    o0 = sub * 128 + j + 1
                    mm(Y[:, c * 128:(c + 1) * 128], g[:, c, o0:o0 + 128], Dg[:, c, j, :], j == 0, j == CW - 1,
                       r=[gTB[t % 2], DgB], w=[YB], sig=(c == 1 and j == CW - 1))

            def qk(h):
                c = h // 2
                hb = h % 2
                for jj, j in enumerate(js):
                    if jj < 4:
                        o = PS[:, hb * 512 + jj * 128: hb * 512 + (jj + 1) * 128]
                        wb = MB[hb]
                    else:
                        conv_some(4)
                        o = PS[:, 2 * 512 + hb * 128: 2 * 512 + (hb + 1) * 128]
                        wb = XB
                    mm(o, KT[:, j // 4, c, (j % 4) * 128:(j % 4 + 1) * 128],
                       Qz[sl][:, h, :], True, False, r=[KVB[j // 4], QzB[sl]], w=[wb], sig=False)
                    mm(o, ident, E[:, h * NMAT + m0 + jj, :], False, True, r=[identB, EB[h]], w=[wb],
                       sig=(jj == nj - 1 or jj == 3))
                act(P1[hb][:, 0:n4 * 128], PS[:, hb * 512: hb * 512 + n4 * 128], AF.Exp, r=[MB[hb]], w=[P1B[hb]])
                if nj > 4:
                    act(P1[hb][:, 512:640], PS[:, 2 * 512 + hb * 128: 2 * 512 + (hb + 1) * 128], AF.Exp, r=[XB], w=[P1B[hb]])

            def pv(h):
                hb = h % 2
                for jj, j in enumerate(js):
                    mm(O[:, h, 0:HD + 1], P1[hb][:, jj * 128:(jj + 1) * 128], VA[:, j, h, 0:HD + 1], jj == 0, jj == nj - 1,
                       r=[P1B[hb], KVB[j // 4]], w=[OB[h // 4]], sig=(jj == nj - 1))

            if fill is not None:
                for f in fill.get(-1, ()):
                    f()
            conv_some(12)
            qk(0)
            for h in range(NH):
                if h + 1 < NH:
                    qk(h + 1)
                conv_some(8)
                pv(h)
                if cpos[0] >= len(convq) and not ydone[0]:
                    ydone[0] = True
                    tt("dve", yb[u], Y, cbb, ALU.add, r=[YB, vecB], w=[ybB[u]])
                if fill is not None:
                    for f in fill.get(h, ()):
                        f()
            conv_some(len(convq))
            for hf in range(2):
                TR.op("dve", lambda e, hf=hf: e.reciprocal(out=rden[:, hf * 4:hf * 4 + 4], in_=O[:, hf * 4:hf * 4 + 4, HD]),
                      r=[OB[hf]], w=[rdenB])
                tt("dve", ocat[u][:, hf * 256:(hf + 1) * 256].rearrange("p (h d) -> p h d", h=4), O[:, hf * 4:hf * 4 + 4, 0:HD],
                   rden[:, hf * 4:hf * 4 + 4].unsqueeze(2).to_broadcast([128, 4, HD]), ALU.mult, r=[OB[hf], rdenB], w=[ocA[u]])
            if not ydone[0]:
                tt("dve", yb[u], Y, cbb, ALU.add, r=[YB, vecB], w=[ybB[u]])
            if fill is not None:
                for f in fill.get(8, ()):
                    f()
            if i % 4 == 3 and qi + 1 < len(seq_lens):
                tcur = i // 4
                nt = S // 512
                chs = [tcur - 1, tcur] if tcur == nt - 1 else [tcur - 1]
                for chn in chs:
                    if 0 <= chn < seq_lens[qi + 1] // 512:
                        load_kv(qi + 1, chn)

        def S2(n, part=None):
            qi, i, tok0 = subs[n]
            sl = n % NQ
            u = n % 2
            if part is None or part == 0:
                S2a(n, sl, u, tok0)
            if part is None or part == 1:
                S2b(n, sl, u)
            if part is None or part == 2:
                S2c(n, sl, u)

        def S2a(n, sl, u, tok0):
            TR.dma("sp", xs[sl], src[tok0:tok0 + 128, :], w=[xB[sl]], dsem=xds[sl])
            TR.dma("sp", obf[sl][:, 0:256], of_d[tok0:tok0 + 128, :], w=[obfB[sl]], dsem=ofds[sl])
            TR.op("dve", lambda e: e.bn_stats(out=bst, in_=yb[u]), r=[ybB[u]], w=[bstB])
            TR.op("dve", lambda e: e.bn_aggr(out=mv, in_=bst), r=[bstB], w=[mvB])
            rsqrt_pool(rsl, mv[:, 1:2], 1.0, LN_EPS, r=[mvB], w=[rslB])
            ts("dve", yn, yb[u], mv[:, 0:1], rsl[:, 0:1], ALU.subtract, ALU.mult, r=[ybB[u], mvB, rslB], w=[ynB])
            tt("pool", yn, yn, lgb, ALU.mult, r=[ynB, vecB], w=[ynB])
            tt("pool", yn, yn, lbb, ALU.add, r=[ynB, vecB], w=[ynB])

        def S2b(n, sl, u):
            act(ey, yn, AF.Exp, r=[ynB], w=[eyB], scale=-1.0)
            ts("pool", ey, ey, 1.0, 1.0, ALU.add, ALU.mult, r=[eyB], w=[eyB])
            TR.op("dve", lambda e: e.reciprocal(out=ey, in_=ey), r=[eyB], w=[eyB])
            tt("dve", ocat[u][:, 512:768], yn, ey, ALU.mult, r=[ynB, eyB], w=[ocC[u]])

        def S2c(n, sl, u):
            act(sq, ocat[u], AF.Square, r=[ocA[u], ocC[u]], w=[sqB])
            TR.op("dve", lambda e: e.tensor_reduce(out=ssg, in_=sq.rearrange("p (g d) -> p g d", g=12), axis=AX.X, op=ALU.add),
                  r=[sqB], w=[ssgB])
            rsqrt_pool(rsg, ssg, 1.0 / HD, RMS_EPS, r=[ssgB], w=[rsgB])
            tt("dve", obf[sl][:, 256:D].rearrange("p (g d) -> p g d", g=12), ocat[u].rearrange("p (g d) -> p g d", g=12),
               rsg.unsqueeze(2).to_broadcast([128, 12, HD]), ALU.mult, r=[ocA[u], ocC[u], rsgB], w=[obfB[sl]])

        def S3(n, part=None):
            qi, i, tok0 = subs[n]
            sl = n % NQ
            u = n % 2
            if part is None or part == 0:
                for kc in range(NKC):
                    tr(bankb(6)[:, kc * 128:(kc + 1) * 128], obf[sl][:, kc * 128:(kc + 1) * 128], r=[obfB[sl]], w=[TB], sig=(kc == NKC - 1))
                evac("act", oT[u][:, 0:4, :].rearrange("p a b -> p (a b)"), bankb(6)[:, 0:512], r=[TB], w=[oTB[u][0]])
                evac("act", oT[u][:, 4:8, :].rearrange("p a b -> p (a b)"), bankb(6)[:, 512:1024], r=[TB], w=[oTB[u][1]])
            for half in range(2):
                if part is not None and part != half + 1:
                    continue
                bk = 7 - half
                bB = PB7 if half == 0 else TB
                for kc in range(NKC):
                    mm(bank(bk), oT[u][:, kc, :], Wo[:, kc, half * 512:(half + 1) * 512], kc == 0, kc == NKC - 1,
                       r=[oTB[u][kc // 4], WoB[kc]], w=[bB], sig=(kc == NKC - 1))
                tt("dve", xs[sl][:, half * 512:(half + 1) * 512], xs[sl][:, half * 512:(half + 1) * 512], bank(bk), ALU.add,
                   r=[xB[sl], bB], w=[xB[sl]])
            if part is None or part == 2:
                TR.dma("sp", xb_d[tok0:tok0 + 128, :], xs[sl], r=[xB[sl]], dsem=sds_[sl])

        load_kv(0, 0)
        load_E()
        for ch in range(1, seq_lens[0] // 512):
            load_kv(0, ch)
        done_extra = set()

        def extra_loads(qi):
            if qi + 1 < len(seq_lens) and qi not in done_extra:
                done_extra.add(qi)
                for chn in range(seq_lens[qi] // 512, seq_lens[qi + 1] // 512):
                    load_kv(qi + 1, chn)

        FE(0)
        if NSUB > 1:
            FE(1)
        S1(0)
        for k in range(NSUB + 1):
            if k + 2 < NSUB:
                FE(k + 2)
            if k < NSUB and subs[k][1] == 0:
                extra_loads(subs[k][0])
            n2 = k
            n3 = k - 1
            v2 = 0 <= n2 < NSUB
            v3 = 0 <= n3 < NSUB
            if k + 1 < NSUB:
                hooks = {}
                if v2:
                    hooks.setdefault(-1, []).append(lambda n=n2: S2(n, 0))
                    hooks.setdefault(1, []).append(lambda n=n2: S2(n, 1))
                    hooks.setdefault(3, []).append(lambda n=n2: S2(n, 2))
                if v3:
                    hooks.setdefault(2, []).append(lambda n=n3: S3(n, 0))
                    hooks.setdefault(4, []).append(lambda n=n3: S3(n, 1))
                    hooks.setdefault(8, []).append(lambda n=n3: S3(n, 2))
                S1(k + 1, hooks)
            else:
                if v3:
                    S3(n3)
                if v2:
                    S2(n2)
        TR.barrier()

    def phase_C(l):
        PA.reset()
        TR.dsem_next = 0
        last = (l == L - 1)
        dst = y if last else xa_d
        W1 = PA([128, NKC, DFF], BF16, "W1")
        W2 = PA([128, 32, D], BF16, "W2")
        g2t = PA([128, NKC], F32, "g2t")
        g2B = Buf()
        TR.dma("sp", g2t, norm2_gT[l], w=[g2B], dsem=TR.new_dsem())
        w1cb = {}
        W2G = [Buf() for _ in range(8)]
        W2B = [W2G[kc // 4] for kc in range(32)]

        def load_rest_weights():
            w1cb.update(load_fold_cols(W1, w_ff_in[l], [(b * 512, (b + 1) * 512) for b in range(1, 8)], g2t, g2B))
            for g4 in range(8):
                TR.dma("pool", W2[:, g4 * 4:(g4 + 1) * 4, :], w_ff_out[l, g4 * 512:(g4 + 1) * 512, :].rearrange("(k p) c -> p k c", p=128),
                       w=[W2G[g4]], dsem=TR.new_swsem())
        if last:
            gfb = PA([128, D], F32, "gfb")
            gfB = Buf()
            bcast_load(gfb, final_norm_g[0:1, :], gfB, TR.new_dsem())
        xs = [PA([128, D], F32, "x") for _ in range(2)]
        xB = [Buf(), Buf()]
        xds = [TR.new_dsem(), TR.new_dsem()]
        hs = [PA([128, D], BF16, "h") for _ in range(4)]
        hB = [Buf() for _ in range(4)]
        ss = PA([128, 4], F32, "ss")
        rs = PA([128, 4], F32, "rs")
        ssB = [Buf() for _ in range(4)]
        rsB = [Buf() for _ in range(4)]
        hT = PA([128, NKC, 512], BF16, "hT")
        hTB = [Buf() for _ in range(4)]
        fT = PA([128, 32, 512], BF16, "fT")
        fTB = [Buf() for _ in range(32)]
        rt = [PA([128, 512], F32, "rt") for _ in range(2)]
        rtB = [Buf(), Buf()]
        xr = [PA([128, D], F32, "xr") for _ in range(2)]
        xrB = [Buf(), Buf()]
        xrds = [TR.new_dsem(), TR.new_dsem()]
        ods = [TR.new_dsem(), TR.new_dsem()]
        s2 = PA([128, 2], F32, "s2")
        r2 = PA([128, 2], F32, "r2")
        s2B = [Buf(), Buf()]
        r2B = [Buf(), Buf()]
        jk = PA([128, D], BF16, "jk")
        jkB = Buf()
        bctr = [0]
        xctr = [0]

        def FE_el(k):
            qi, t, tok0 = tiles[k]
            for s in range(4):
                xi = xctr[0] % 2
                xctr[0] += 1
                TR.dma("sp", xs[xi], xb_d[tok0 + s * 128: tok0 + (s + 1) * 128, :], w=[xB[xi]], dsem=xds[xi])
                act(hs[s], xs[xi], AF.Square, r=[xB[xi]], w=[hB[s], ssB[s]], accum=ss[:, s:s + 1])
                rsqrt_pool(rs[:, s:s + 1], ss[:, s:s + 1], 1.0 / D, RMS_EPS, r=[ssB[s]], w=[rsB[s]])
                ts("dve", hs[s], xs[xi], rs[:, s:s + 1], None, ALU.mult, None, r=[xB[xi], rsB[s]], w=[hB[s]])

        def FE_pe(k):
            for pr in range(4):
                bk = 6 + (pr % 2)
                for kk in range(2):
                    kc = pr * 2 + kk
                    for s in range(4):
                        tr(bankb(bk)[:, kk * 512 + s * 128: kk * 512 + (s + 1) * 128], hs[s][:, kc * 128:(kc + 1) * 128],
                           r=[hB[s]], w=[bankB[bk]], sig=(kk == 1 and s == 3))
                evac(evq(), hT[:, pr * 2:pr * 2 + 2, :].rearrange("p a b -> p (a b)"), bankb(bk), r=[bankB[bk]], w=[hTB[pr]])

        def BE1(k, hook):
            for fc in range(32):
                if fc == 10 and hook is not None:
                    hook()
                bk = bctr[0] % 6
                bctr[0] += 1
                for kc in range(NKC):
                    mm(bank(bk), W1[:, kc, fc * 128:(fc + 1) * 128], hT[:, kc, :], kc == 0, kc == NKC - 1,
                       r=[w1cb[((fc // 4) * 512, (fc // 4) * 512 + 512)], hTB[kc // 2]], w=[bankB[bk]], sig=(kc == NKC - 1))
                u = fc % 2
                act(rt[u], bank(bk), AF.Relu, r=[bankB[bk]], w=[rtB[u]])
                tt("dve", fT[:, fc, :], rt[u], rt[u], ALU.mult, r=[rtB[u]], w=[fTB[fc]])

        def BE2(k):
            qi, t, tok0 = tiles[k]
            for s in range(4):
                u = s % 2
                TR.dma("sp", xr[u], xb_d[tok0 + s * 128: tok0 + (s + 1) * 128, :], w=[xrB[u]], dsem=xrds[u])
                for half in range(2):
                    bk = bctr[0] % 6
                    bctr[0] += 1
                    for kc in range(32):
                        mm(bank(bk), fT[:, kc, s * 128:(s + 1) * 128], W2[:, kc, half * 512:(half + 1) * 512], kc == 0, kc == 31,
                           r=[fTB[kc], W2B[kc]], w=[bankB[bk]], sig=(kc == 31))
                    tt("dve", xr[u][:, half * 512:(half + 1) * 512], xr[u][:, half * 512:(half + 1) * 512], bank(bk), ALU.add,
                       r=[xrB[u], bankB[bk]], w=[xrB[u]])
                if last:
                    act(jk, xr[u], AF.Square, r=[xrB[u]], w=[jkB, s2B[u]], accum=s2[:, u:u + 1])
                    rsqrt_pool(r2[:, u:u + 1], s2[:, u:u + 1], 1.0 / D, RMS_EPS, r=[s2B[u]], w=[r2B[u]])
                    stt(xr[u], xr[u], r2[:, u:u + 1], gfb, ALU.mult, ALU.mult, r=[xrB[u], r2B[u], gfB], w=[xrB[u]])
                TR.dma("sp", dst[tok0 + s * 128: tok0 + (s + 1) * 128, :], xr[u], r=[xrB[u]], dsem=ods[u])

        w1cb.update(load_fold_cols(W1, w_ff_in[l], [(0, 512)], g2t, g2B))
        FE_el(0)
        load_rest_weights()
        run_folds()
        FE_pe(0)
        for k in range(NT):
            BE1(k, (lambda kk=k: FE_el(kk + 1)) if k + 1 < NT else None)
            if k + 1 < NT:
                FE_pe(k + 1)
            BE2(k)
        TR.barrier()

    import os as _os
    _ph = _os.environ.get("MK_PHASES", "AFBC")
    for l in range(L):
        if "A" in _ph:
            phase_A(l)
        if "F" in _ph:
            phase_F(l)
        if "B" in _ph:
            phase_B(l)
        if "C" in _ph:
            phase_C(l)
    TR.replay()
    return nc, TR, PA


_CACHE = {}


def _const_inputs(seq_lens):
    key = tuple(sorted(set(seq_lens)))
    if key not in _CACHE:
        d = {"ident": np.eye(128, dtype=np.float32).astype(ml_dtypes.bfloat16),
             "jrev": np.eye(128, dtype=np.float32)[::-1].copy().astype(ml_dtypes.bfloat16),
             "alt": np.stack([(-1.0) ** np.arange(128)] * 2, axis=1).astype(ml_dtypes.bfloat16)}
        for S in key:
            c, s, bc, bs = _dft_tables(S)
            d["cos%d" % S] = c
            d["sin%d" % S] = s
            d["bdc%d" % S] = bc
            d["bds%d" % S] = bs
        _CACHE[key] = d
    return _CACHE[key]


def make_weight_inputs(depth, norm1_g, w_in, rpb, conv_w, conv_b, conv_ln_g, conv_ln_b, mix_norm_g, w_out,
                       norm2_g, w_ff_in, w_ff_out, final_norm_g):
    f = lambda a: np.ascontiguousarray(np.asarray(a, dtype=np.float32))
    gT = lambda a: np.ascontiguousarray(f(a).reshape(depth, NKC, 128).transpose(0, 2, 1))
    DR, DC, VAL = _bias_index_tables()
    rpb = f(rpb)
    g = rpb[:, :, DR, DC]
    g = np.where(VAL[None, None], g, np.float32(NEG))
    g = np.ascontiguousarray(g.transpose(0, 3, 1, 2, 4)).reshape(depth, 128, NH * NMAT * 128)
    return {
        "norm1_gT": gT(norm1_g), "w_in": f(w_in), "biasmat": g.astype(ml_dtypes.bfloat16),
        "conv_wT": np.ascontiguousarray(f(conv_w).transpose(0, 2, 1)), "conv_b": f(conv_b),
        "conv_ln_g": f(conv_ln_g), "conv_ln_b": f(conv_ln_b), "mix_gT": gT(mix_norm_g), "w_out": f(w_out),
        "norm2_gT": gT(norm2_g), "w_ff_in": f(w_ff_in), "w_ff_out": f(w_ff_out),
        "final_norm_g": f(final_norm_g).reshape(1, D),
    }


_PROG = {}


def get_program(seq_lens, depth):
    key = (tuple(seq_lens), depth)
    if key not in _PROG:
        _PROG[key] = build_program(list(seq_lens), depth)[0]
    return _PROG[key]


def kernel(x_prompt, x_sample, norm1_g, w_in, rpb, conv_w, conv_b, conv_ln_g, conv_ln_b, mix_norm_g, w_out,
           norm2_g, w_ff_in, w_ff_out, final_norm_g):
    x_prompt = np.asarray(x_prompt, dtype=np.float32)
    x_sample = np.asarray(x_sample, dtype=np.float32)
    n = 8
    depth = int(np.asarray(w_in).shape[0])
    BP, SP_, _ = x_prompt.shape
    BS, SS, _ = x_sample.shape
    pp = BP // n
    sp = BS // n
    seq_lens = [SP_] * pp + [SS] * sp
    nc = get_program(seq_lens, depth)
    wts = make_weight_inputs(depth, norm1_g, w_in, rpb, conv_w, conv_b, conv_ln_g, conv_ln_b, mix_norm_g, w_out,
                             norm2_g, w_ff_in, w_ff_out, final_norm_g)
    consts = _const_inputs(seq_lens)
    in_maps = []
    for c in range(n):
        xin = np.concatenate([x_prompt[c * pp:(c + 1) * pp].reshape(pp * SP_, D),
                              x_sample[c * sp:(c + 1) * sp].reshape(sp * SS, D)], axis=0)
        m = {"xin": np.ascontiguousarray(xin)}
        m.update(wts)
        m.update(consts)
        in_maps.append(m)
    res = run_bass_kernel_spmd(nc, in_maps, core_ids=list(range(n)))
    yp = np.empty_like(x_prompt)
    ys = np.empty_like(x_sample)
    for c in range(n):
        yc = np.asarray(res.results[c]["y"], dtype=np.float32)
        yp[c * pp:(c + 1) * pp] = yc[:pp * SP_].reshape(pp, SP_, D)
        ys[c * sp:(c + 1) * sp] = yc[pp * SP_:].reshape(sp, SS, D)
    return (yp, ys)
```
